# Optimizing a Trainium2 kernel written in Bass

```python
import math
import jax, jax.numpy as jnp
from jax import lax
import numpy as np

D_MODEL = 1024
BATCH = 8
SEQ = 2048
DEPTH = 2

N_MEM = 256
GRID_W = 64
HY_W = 512
NA_HEADS = 8
NA_HEAD_DIM = 64
NA_W = NA_HEADS * NA_HEAD_DIM
NA_WIN_ROWS = 8
NA_WIN_COLS = 16
SC_W = 512
XA_HEADS = 4
XA_HEAD_DIM = D_MODEL // XA_HEADS
D_FF = 2816
HY_ORDER = 2
HY_EMB = 33
HY_HIDDEN = 64
HY_FAST_DECAY = 0.3
HY_SLOW_DECAY = 1.5
HY_TARGET = 1e-2
N_BRANCH = 3
EPS = 1e-6
PROJ_W = 3 * HY_W + 3 * NA_W + 3 * SC_W + N_BRANCH * D_MODEL

kernel_name = "hybrid_hyena_natten_shortconv_encoder"


def rms_norm(x, g):
    xf = x.astype(jnp.float32)
    y = xf * lax.rsqrt(jnp.mean(xf * xf, axis=-1, keepdims=True) + EPS)
    return (y * g.astype(jnp.float32)).astype(x.dtype)


def dwconv3(x, w):
    xp = jnp.pad(x, ((0, 0), (1, 1), (0, 0)))
    return xp[:, :-2] * w[0] + xp[:, 1:-1] * w[1] + xp[:, 2:] * w[2]


def hyena_filters(L, w1, b1, w2, b2, w3, freq):
    f32 = jnp.float32
    t = jnp.linspace(0.0, 1.0, L, dtype=f32)[:, None]
    bands = (HY_EMB - 1) // 2
    w = 2.0 * math.pi * jnp.arange(L, dtype=f32)[:, None] / L
    f = jnp.linspace(1e-4, bands - 1, bands, dtype=f32)[None, :]
    z = jnp.concatenate([t, jnp.cos(f * w), -jnp.sin(f * w)], axis=-1)
    h = jnp.sin(freq[0].astype(f32) * (z @ w1.astype(f32) + b1.astype(f32)))
    h = jnp.sin(freq[1].astype(f32) * (h @ w2.astype(f32) + b2.astype(f32)))
    h = (h @ w3.astype(f32)).reshape(L, 2, HY_ORDER, HY_W)
    deltas = jnp.abs(jnp.linspace(math.log(HY_TARGET) / HY_SLOW_DECAY,
                                  math.log(HY_TARGET) / HY_FAST_DECAY, HY_W, dtype=f32))
    h = h * jnp.exp(-t * deltas)[:, None, None, :]
    h_fwd, h_bwd = h[:, 0], h[:, 1]
    k2 = jnp.concatenate([h_fwd, jnp.zeros((1, HY_ORDER, HY_W), f32), h_bwd[1:][::-1]], axis=0)
    return jnp.fft.rfft(k2, axis=0)


def long_conv(z, kf, bias):
    L = z.shape[1]
    zf32 = z.astype(jnp.float32)
    zf = jnp.fft.rfft(zf32, n=2 * L, axis=1)
    y = jnp.fft.irfft(zf * kf[None], n=2 * L, axis=1)[:, :L]
    return (y + zf32 * bias.astype(jnp.float32)).astype(z.dtype)


def hyena_mixer(u, short_w, kf, bias):
    u = dwconv3(u, short_w)
    v, x1, x2 = jnp.split(u, 3, axis=-1)
    z = x1 * long_conv(v, kf[:, 0], bias[0])
    return x2 * long_conv(z, kf[:, 1], bias[1])


def neighbourhood_attention(q, k, v, rpb):
    B, L, _ = q.shape
    rows = L // GRID_W
    kr = min(NA_WIN_ROWS, rows)

    def grid(t):
        return t.reshape(B, rows, GRID_W, NA_HEADS, NA_HEAD_DIM).transpose(0, 3, 1, 2, 4)

    qg = grid(q) * (NA_HEAD_DIM ** -0.5)
    kg, vg = grid(k), grid(v)
    r = jnp.arange(rows)
    row_idx = jnp.clip(r - kr // 2, 0, rows - kr)[:, None] + jnp.arange(kr)[None, :]
    k_rows = kg[:, :, row_idx]
    v_rows = vg[:, :, row_idx]
    c = jnp.arange(GRID_W)
    col_start = jnp.clip(c - NA_WIN_COLS // 2, 0, GRID_W - NA_WIN_COLS)
    col_mask = (c[None, :] >= col_start[:, None]) & (c[None, :] < col_start[:, None] + NA_WIN_COLS)
    dr = row_idx - r[:, None] + (NA_WIN_ROWS - 1)
    dc = jnp.clip(c[None, :] - c[:, None] + NA_WIN_COLS - 1, 0, 2 * NA_WIN_COLS - 2)
    bias = rpb[:, dr][..., dc].transpose(0, 1, 3, 2, 4)
    s = jnp.einsum('bhrqd,bhrkcd->bhrqkc', qg, k_rows).astype(jnp.float32)
    s = s + bias[None].astype(jnp.float32)
    s = jnp.where(col_mask[:, None, :], s, -1e30)
    p = jax.nn.softmax(s, axis=(-2, -1)).astype(v.dtype)
    o = jnp.einsum('bhrqkc,bhrkcd->bhrqd', p, v_rows)
    return o.transpose(0, 2, 3, 1, 4).reshape(B, L, NA_W)


def short_conv_mixer(b_gate, c_gate, x_in, w):
    return b_gate * dwconv3(c_gate * x_in, w)


def memory_cross_attention(h, mem_n, wq, wkv, wo):
    B, L, _ = h.shape
    M = mem_n.shape[1]
    q = (h @ wq).reshape(B, L, XA_HEADS, XA_HEAD_DIM)
    km, vm = jnp.split(mem_n @ wkv, 2, axis=-1)
    km = km.reshape(B, M, XA_HEADS, XA_HEAD_DIM)
    vm = vm.reshape(B, M, XA_HEADS, XA_HEAD_DIM)
    s = jnp.einsum('bshd,bmhd->bhsm', q, km).astype(jnp.float32) * (XA_HEAD_DIM ** -0.5)
    p = jax.nn.softmax(s, axis=-1).astype(h.dtype)
    o = jnp.einsum('bhsm,bmhd->bshd', p, vm).reshape(B, L, D_MODEL)
    return o @ wo


def conv_glu_ffn(h, w_up, w_conv, w_down):
    u = dwconv3(h @ w_up, w_conv)
    g, val = jnp.split(u, 2, axis=-1)
    return (jax.nn.gelu(g, approximate=True) * val) @ w_down


def setup_inputs(seed: int = 0) -> dict:
    key = jax.random.key(seed)
    ks = jax.random.split(key, 24)

    def nrm(k, shape, scale):
        return jax.random.normal(k, shape, jnp.float32) * scale

    return {
        "x": nrm(ks[0], (BATCH, SEQ, D_MODEL), 1.0),
        "mem": nrm(ks[1], (BATCH, N_MEM, D_MODEL), 1.0),
        "norm_gains": 1.0 + nrm(ks[2], (DEPTH, 6, D_MODEL), 0.05),
        "mem_norm": 1.0 + nrm(ks[3], (DEPTH, D_MODEL), 0.05),
        "w_in": nrm(ks[4], (DEPTH, D_MODEL, PROJ_W), D_MODEL ** -0.5),
        "gate_bias": nrm(ks[5], (DEPTH, N_BRANCH, D_MODEL), 0.02),
        "hy_short_w": nrm(ks[6], (DEPTH, 3, 3 * HY_W), 3 ** -0.5),
        "hy_w1": nrm(ks[7], (DEPTH, HY_EMB, HY_HIDDEN), HY_EMB ** -0.5),
        "hy_b1": nrm(ks[8], (DEPTH, HY_HIDDEN), 0.02),
        "hy_w2": nrm(ks[9], (DEPTH, HY_HIDDEN, HY_HIDDEN), HY_HIDDEN ** -0.5),
        "hy_b2": nrm(ks[10], (DEPTH, HY_HIDDEN), 0.02),
        "hy_w3": nrm(ks[11], (DEPTH, HY_HIDDEN, 2 * HY_ORDER * HY_W), 0.05 * HY_HIDDEN ** -0.5),
        "hy_freq": 1.0 + nrm(ks[12], (DEPTH, 2, HY_HIDDEN), 0.05),
        "hy_bias": nrm(ks[13], (DEPTH, HY_ORDER, HY_W), 0.1),
        "na_rpb": nrm(ks[14], (DEPTH, NA_HEADS, 2 * NA_WIN_ROWS - 1, 2 * NA_WIN_COLS - 1), 0.02),
        "sc_conv_w": nrm(ks[15], (DEPTH, 3, SC_W), 3 ** -0.5),
        "w_branch": nrm(ks[16], (DEPTH, N_BRANCH, HY_W, D_MODEL), HY_W ** -0.5),
        "w_out": nrm(ks[17], (DEPTH, D_MODEL, D_MODEL), D_MODEL ** -0.5),
        "xa_wq": nrm(ks[18], (DEPTH, D_MODEL, D_MODEL), D_MODEL ** -0.5),
        "xa_wkv": nrm(ks[19], (DEPTH, D_MODEL, 2 * D_MODEL), D_MODEL ** -0.5),
        "xa_wo": nrm(ks[20], (DEPTH, D_MODEL, D_MODEL), D_MODEL ** -0.5),
        "ffn_up": nrm(ks[21], (DEPTH, D_MODEL, 2 * D_FF), D_MODEL ** -0.5),
        "ffn_conv": nrm(ks[22], (DEPTH, 3, 2 * D_FF), 3 ** -0.5),
        "ffn_down": nrm(ks[23], (DEPTH, D_FF, D_MODEL), D_FF ** -0.5),
    }


def reference(x, mem, norm_gains, mem_norm, w_in, gate_bias, hy_short_w, hy_w1, hy_b1, hy_w2, hy_b2,
              hy_w3, hy_freq, hy_bias, na_rpb, sc_conv_w, w_branch, w_out, xa_wq, xa_wkv, xa_wo,
              ffn_up, ffn_conv, ffn_down):
    B, L, _ = x.shape
    splits = [3 * HY_W, 3 * HY_W + 3 * NA_W, 3 * HY_W + 3 * NA_W + 3 * SC_W]
    for l in range(DEPTH):
        g = norm_gains[l]
        h = rms_norm(x, g[0])
        proj = h @ w_in[l]
        hy_u, na_qkv, sc_u, gate_pre = jnp.split(proj, splits, axis=-1)
        kf = hyena_filters(L, hy_w1[l], hy_b1[l], hy_w2[l], hy_b2[l], hy_w3[l], hy_freq[l])
        y_a = hyena_mixer(hy_u, hy_short_w[l], kf, hy_bias[l])
        q, k, v = jnp.split(na_qkv, 3, axis=-1)
        y_b = neighbourhood_attention(q, k, v, na_rpb[l])
        b_gate, c_gate, x_in = jnp.split(sc_u, 3, axis=-1)
        y_c = short_conv_mixer(b_gate, c_gate, x_in, sc_conv_w[l])
        gates = jax.nn.sigmoid(gate_pre.reshape(B, L, N_BRANCH, D_MODEL) + gate_bias[l])
        merged = (gates[:, :, 0] * (y_a @ w_branch[l, 0])
                  + gates[:, :, 1] * (y_b @ w_branch[l, 1])
                  + gates[:, :, 2] * (y_c @ w_branch[l, 2]))
        x = x + rms_norm(merged @ w_out[l], g[1])
        h = rms_norm(x, g[2])
        mem_n = rms_norm(mem, mem_norm[l])
        x = x + rms_norm(memory_cross_attention(h, mem_n, xa_wq[l], xa_wkv[l], xa_wo[l]), g[3])
        h = rms_norm(x, g[4])
        x = x + rms_norm(conv_glu_ffn(h, ffn_up[l], ffn_conv[l], ffn_down[l]), g[5])
    return x
```

```python
import numpy as np, ml_dtypes
import concourse.bass as bass
import concourse.mybir as mybir
from concourse.bass_utils import run_bass_kernel_spmd
from contextlib import ExitStack, contextmanager

F32 = mybir.dt.float32; BF16 = mybir.dt.bfloat16
ALU = mybir.AluOpType; AF = mybir.ActivationFunctionType
BF = ml_dtypes.bfloat16

L = 2048; D = 1024; NM = 256; DFF = 2816; PW = 7680; NL = 2
EPS = 1e-6
NV = 272
cG = lambda i: 8 * i
cMN = 48
cGB = 56
cHSW = lambda k, cc: 80 + 12 * k + cc
cHB = lambda o, cc: 116 + 4 * o + cc
cSCW = lambda k, cc: 124 + 4 * k + cc
cFCW = lambda k, cc: 136 + 44 * k + cc
cB1, cB2, cF0, cF1 = 268, 269, 270, 271
NA_GROUP_CHUNKS = [list(range(0, 6)), list(range(2, 10)), list(range(6, 14)), list(range(10, 16))]
NA_TILE_BASE = [0, 6, 6, 14]
MASKV = -30000.0
NS = 23
SB_LO = 16640; SB_HI = 221184


class Buf:
    __slots__ = ("w", "r")

    def __init__(self):
        self.w = None; self.r = {}


class KB:
    def __init__(self, nc, stack, n_dma_sems=40, same_engine_sync=True):
        self.nc = nc; self.ses = same_engine_sync
        self.eng = {"pe": nc.tensor, "dve": nc.vector, "act": nc.scalar, "pool": nc.gpsimd, "sp": nc.sync}
        self.semh = {}; self.cnt = {}
        for e in ("pe", "dve", "act", "pool"):
            self.semh[e] = stack.enter_context(nc.semaphore("s_" + e)); self.cnt[e] = 0
        self.nd = n_dma_sems
        for i in range(n_dma_sems):
            self.semh[("d", i)] = stack.enter_context(nc.semaphore(f"d{i}")); self.cnt[("d", i)] = 0
        self.dnext = 0
        self.known = {e: {} for e in self.eng}
        self.uid = 0
        self.dbufs = {}
        self.psums = []; self.psn = 0; self.accn = 0; self.rot = 6
        self.lo = SB_LO; self.hi = SB_HI; self.side = 0; self.ghosts = []; self.peak = 0

    def tile(self, stack, shape, dt):
        self.uid += 1
        nb = 1
        for d in shape[1:]: nb *= d
        nb *= (4 if dt == F32 else 2)
        nb = (nb + 63) // 64 * 64
        side = self.side
        if side == 0:
            off = self.lo; self.lo += nb
        else:
            self.hi -= nb; off = self.hi
        assert self.lo <= self.hi, f"SBUF overflow lo={self.lo} hi={self.hi}"
        self.peak = max(self.peak, (self.lo - SB_LO) + (SB_HI - self.hi))
        tl = Tl(self.nc.alloc_sbuf_tensor_at(f"t{self.uid}", list(shape), dt, offset=off))
        s0, e0 = off, off + nb
        newg = []
        for (gs, ge, ev) in self.ghosts:
            if ge <= s0 or gs >= e0:
                newg.append((gs, ge, ev)); continue
            for k, v in ev.items():
                if tl.b.r.get(k, 0) < v: tl.b.r[k] = v
            if gs < s0: newg.append((gs, s0, ev))
            if ge > e0: newg.append((e0, ge, ev))
        self.ghosts = newg
        stack.callback(self._free, tl, off, nb, side)
        return tl

    def _free(self, tl, off, nb, side):
        ev = dict(tl.b.r)
        if tl.b.w is not None and ev.get(tl.b.w[0], 0) < tl.b.w[1]: ev[tl.b.w[0]] = tl.b.w[1]
        self.ghosts.append((off, off + nb, ev))
        if side == 0:
            assert self.lo == off + nb, "non-LIFO free (lo)"; self.lo = off
        else:
            assert self.hi == off, "non-LIFO free (hi)"; self.hi = off + nb

    def dbuf(self, *key):
        b = self.dbufs.get(key)
        if b is None:
            b = self.dbufs[key] = Buf()
        return b

    def psum(self):
        t = self.psums[self.psn % self.rot]; self.psn += 1
        return t

    def acc(self):
        t = self.psums[6 + self.accn % 2]; self.accn += 1
        return t

    def _wait(self, e, k, v):
        kn = self.known[e]
        if kn.get(k, 0) >= v: return
        self.eng[e].wait_ge(self.semh[k], v); kn[k] = v

    def _waits(self, e, reads, writes):
        need = {}
        for b in reads:
            if b.w is not None and need.get(b.w[0], 0) < b.w[1]: need[b.w[0]] = b.w[1]
        for b in writes:
            if b.w is not None and need.get(b.w[0], 0) < b.w[1]: need[b.w[0]] = b.w[1]
            for k, v in b.r.items():
                if need.get(k, 0) < v: need[k] = v
        for k, v in need.items():
            if k == e and (not self.ses or e == "pe"): continue
            self._wait(e, k, v)

    def _record(self, ev, reads, writes):
        k, v = ev
        for b in reads:
            if b.r.get(k, 0) < v: b.r[k] = v
        for b in writes:
            b.w = ev; b.r = {}

    def op(self, e, reads, writes, fn):
        self._waits(e, reads, writes)
        ins = fn(self.eng[e])
        self.cnt[e] += 1
        ins.then_inc(self.semh[e], 1)
        self._record((e, self.cnt[e]), reads, writes)

    def dma(self, q, out, in_, reads, writes):
        i = self.dnext; self.dnext = (i + 1) % self.nd
        key = ("d", i)
        if self.cnt[key] > 0: self._wait(q, key, self.cnt[key])
        self._waits(q, reads, writes)
        self.cnt[key] += 16
        self.eng[q].dma_start(out=out, in_=in_).then_inc(self.semh[key], 16)
        self._record((key, self.cnt[key]), reads, writes)

    def barrier(self):
        for e in self.eng:
            for k, c in self.cnt.items():
                if c > 0 and not (k == e and e == "pe"): self._wait(e, k, c)

    @contextmanager
    def phase(self, flip=False):
        if flip: self.side = 1 - self.side
        with ExitStack() as ph:
            yield ph


class Tl:
    __slots__ = ("t", "b", "__weakref__")

    def __init__(self, t):
        self.t = t; self.b = Buf()


class Model:
    def __init__(self, nc, kb, st, dbg=False):
        self.nc = nc; self.kb = kb; self.dbg = dbg
        dt = lambda n, s, d, k=("ExternalOutput" if dbg else "Internal"): nc.dram_tensor(n, list(s), d, kind=k).ap()
        ein = lambda n, s, d=F32: dt(n, s, d, "ExternalInput")
        self.I = {
            "xT": ein("xT", [D, L]), "memT": ein("memT", [D, NM]), "pv": ein("pv", [NL, 128, NV]),
            "w_in": ein("w_in", [NL, D, PW]), "hy_w1": ein("hy_w1", [NL, 33, 64]), "hy_w2": ein("hy_w2", [NL, 64, 64]),
            "hy_w3": ein("hy_w3", [NL, 64, 2048]), "natab": ein("natab", [NL, 128, 8 * NS * 64]),
            "narm": ein("narm", [128, 20 * 512], BF16), "nasel": ein("nasel", [128, 128], BF16),
            "w_branch": ein("w_branch", [NL, 3, 512, D]), "w_out": ein("w_out", [NL, D, D]),
            "xa_wq": ein("xa_wq", [NL, D, D]), "xa_wkv": ein("xa_wkv", [NL, D, 2 * D]), "xa_wo": ein("xa_wo", [NL, D, D]),
            "ffn_up": ein("ffn_up", [NL, D, 2 * DFF]), "ffn_down": ein("ffn_down", [NL, DFF, D]),
            "CTb": ein("CTb", [8, 128, 8, 128], BF16), "STb": ein("STb", [8, 128, 8, 128], BF16),
            "Cm": ein("Cm", [2, 128, 8, 512], BF16), "Sm": ein("Sm", [2, 128, 8, 512], BF16),
            "tw": ein("tw", [128, 16]),
            "zT": ein("zT", [33, L]), "decay": ein("decay", [L, 512]), "decayb": ein("decayb", [L, 512]),
            "identf": ein("identf", [128, 128]), "identb": ein("identb", [128, 128], BF16),
        }
        self.out = dt("outT", [D, L], F32, "ExternalOutput")
        self.S = {
            "hyuT": dt("hyuT", [1536, L], F32), "qT": dt("qT", [512, L], BF16), "kT": dt("kT", [512, L], BF16),
            "vaug": dt("vaug", [16, 128, 1024], BF16), "scuT": dt("scuT", [1536, L], F32),
            "gatesT": dt("gatesT", [3072, L], BF16), "kfr": dt("kfr", [L, 1024], F32), "kfi": dt("kfi", [L, 1024], F32),
            "yaT": dt("yaT", [512, L], BF16), "ybT": dt("ybT", [512, L], BF16), "ycT": dt("ycT", [512, L], BF16),
            "mergedT": dt("mergedT", [D, L], BF16), "hsd": dt("hsd", [2, L, 1024], BF16), "hT": dt("hT", [D, L], BF16),
        }
        mk = lambda shape, d: kb.tile(st, shape, d)
        self.onesb = mk([128, 128], BF16); self.identb = mk([128, 128], BF16); self.identf = mk([128, 128], F32)
        self.onesf = mk([1, 64], F32); self.epst = mk([128, 1], F32)
        self.pv = [mk([128, NV], F32) for _ in range(NL)]
        self.tw = mk([128, 16], F32)
        kb.dma("sp", self.tw.t[:], self.I["tw"][:, :], [], [self.tw.b])
        self.nidentb = mk([128, 128], BF16)
        kb.op("dve", [], [self.onesb.b], lambda v: v.memset(self.onesb.t[:], 1.0))
        kb.op("dve", [], [self.onesf.b], lambda v: v.memset(self.onesf.t[:], 1.0))
        kb.op("dve", [], [self.epst.b], lambda v: v.memset(self.epst.t[:], EPS))
        kb.dma("sp", self.identb.t[:], self.I["identb"][:, :], [], [self.identb.b])
        kb.dma("sp", self.identf.t[:], self.I["identf"][:, :], [], [self.identf.b])
        kb.op("dve", [self.identb.b], [self.nidentb.b], lambda v: v.tensor_scalar(out=self.nidentb.t[:], in0=self.identb.t[:], scalar1=-1.0, scalar2=None, op0=ALU.mult))
        for l in range(NL):
            kb.dma("sp", self.pv[l].t[:], self.I["pv"][l], [], [self.pv[l].b])
        for i in range(8):
            kb.psums.append(Tl(st.enter_context(nc.psum_tensor(f"psum{i}", [128, 512], F32))))

    def tiles(self, ph, n, shape, d):
        return [self.kb.tile(ph, shape, d) for _ in range(n)]

    def load_fm(self, ph, src, nchunks, T, d, q="sp", deps=None):
        ts = self.tiles(ph, nchunks, [128, T], d)
        for c, t in enumerate(ts):
            self.kb.dma(q, t.t[:], src[c * 128:(c + 1) * 128, :], deps(c) if deps else [], [t.b])
        return ts

    def xdeps(self, c):
        return [self.kb.dbuf("x", c)]

    def rms_stats(self, ph, xs, T, sq_tiles):
        kb = self.kb
        rstd = kb.tile(ph, [128, T], F32)
        n = len(xs)
        for t0 in range(0, T, 512):
            w = min(512, T - t0)
            ps = kb.psum()
            for c, x in enumerate(xs):
                sq = sq_tiles[c % len(sq_tiles)]
                kb.op("act", [x.b], [sq.b], lambda a, x=x, sq=sq: a.activation(out=sq.t[:, 0:w], in_=x.t[:, t0:t0 + w], func=AF.Square))
                kb.op("pe", [sq.b, self.onesb.b], [ps.b], lambda p, sq=sq, c=c: p.matmul(ps.t[:, 0:w], lhsT=self.onesb.t[:, :], rhs=sq.t[:, 0:w], start=(c == 0), stop=(c == n - 1)))
            kb.op("act", [ps.b, self.epst.b], [rstd.b], lambda a: a.activation(out=rstd.t[:, t0:t0 + w], in_=ps.t[:, 0:w], func=AF.Ln, bias=self.epst.t[:, 0:1], scale=1.0 / (128 * n)))
        kb.op("act", [rstd.b], [rstd.b], lambda a: a.activation(out=rstd.t[:, :], in_=rstd.t[:, :], func=AF.Exp, scale=-0.5))
        return rstd

    def norm_apply(self, tmp, xs, hs, T, pvt, gcol):
        kb = self.kb
        sq = self.tiles(tmp, 3, [128, 512], BF16)
        rstd = self.rms_stats(tmp, xs, T, sq)
        for c, (x, h) in enumerate(zip(xs, hs)):
            kb.op("dve", [x.b, rstd.b, pvt.b], [h.b], lambda v, x=x, h=h, c=c: v.scalar_tensor_tensor(
                out=h.t[:, :], in0=x.t[:, :], scalar=pvt.t[:, gcol + c:gcol + c + 1], in1=rstd.t[:, :], op0=ALU.mult, op1=ALU.mult))

    def prefetch_w(self, ph, w, K, c0, ncols, cgw, nbuf):
        kb = self.kb; KC = K // 128
        wb = self.tiles(ph, nbuf, [128, KC, cgw], BF16)
        n = 0
        for g0 in list(range(0, ncols, cgw))[:nbuf]:
            gw = min(cgw, ncols - g0)
            kb.dma("pool", wb[n].t[:, :, 0:gw], w[0:K, c0 + g0:c0 + g0 + gw].rearrange("(kc p) n -> p kc n", p=128), [], [wb[n].b])
            n += 1
        return (wb, n)

    def linear_fm(self, ph, w, K, c0, ncols, srcs, T, epi, cgw=512, nbuf=3, wpre=None):
        kb = self.kb; KC = K // 128
        if wpre is not None:
            wb, npre = wpre; nbuf = len(wb)
        else:
            wb = self.tiles(ph, nbuf, [128, KC, cgw], BF16); npre = 0
        gi = 0
        for g0 in range(0, ncols, cgw):
            gw = min(cgw, ncols - g0)
            wt = wb[gi % nbuf]; gi += 1
            if gi > npre:
                kb.dma("pool", wt.t[:, :, 0:gw], w[0:K, c0 + g0:c0 + g0 + gw].rearrange("(kc p) n -> p kc n", p=128), [], [wt.b])
            for o0 in range(0, gw, 128):
                oc = (g0 + o0) // 128
                for t0 in range(0, T, 512):
                    tw = min(512, T - t0)
                    ps = kb.psum()

                    def mm(p, wt=wt, o0=o0, t0=t0, tw=tw, ps=ps):
                        for kc in range(KC):
                            ins = p.matmul(ps.t[:, 0:tw], lhsT=wt.t[:, kc, o0:o0 + 128], rhs=srcs[kc].t[:, t0:t0 + tw], start=(kc == 0), stop=(kc == KC - 1))
                        return ins
                    kb.op("pe", [wt.b] + [s.b for s in srcs], [ps.b], mm)
                    epi(oc, t0, tw, ps)

    def linear_tm(self, ph, w, K, c0, ncols, srcs, T, epi, nbuf=2):
        kb = self.kb; KC = K // 128
        wb = self.tiles(ph, nbuf, [128, KC, 512], BF16)
        for gi, g0 in enumerate(range(0, ncols, 512)):
            gw = min(512, ncols - g0)
            wt = wb[gi % nbuf]
            kb.dma("pool", wt.t[:, :, 0:gw], w[0:K, c0 + g0:c0 + g0 + gw].rearrange("(kc p) n -> p kc n", p=128), [], [wt.b])
            for tc in range(T // 128):
                ps = kb.psum()

                def mm(p, wt=wt, tc=tc, gw=gw, ps=ps):
                    for kc in range(KC):
                        ins = p.matmul(ps.t[:, 0:gw], lhsT=srcs[kc].t[:, tc * 128:(tc + 1) * 128], rhs=wt.t[:, kc, 0:gw], start=(kc == 0), stop=(kc == KC - 1))
                    return ins
                kb.op("pe", [wt.b] + [s.b for s in srcs], [ps.b], mm)
                epi(tc, gi, gw, ps)

    def conv3(self, ub, ob, pvt, col_fn, T=L, center_done=False):
        kb = self.kb
        if not center_done:
            kb.op("dve", [ub.b, pvt.b], [ob.b], lambda v: v.tensor_scalar(out=ob.t[:, 0:T], in0=ub.t[:, 1:1 + T], scalar1=pvt.t[:, col_fn(1):col_fn(1) + 1], scalar2=None, op0=ALU.mult))
        for k in (0, 2):
            kb.op("dve", [ub.b, pvt.b, ob.b], [ob.b], lambda v, k=k: v.scalar_tensor_tensor(
                out=ob.t[:, 0:T], in0=ub.t[:, k:k + T], scalar=pvt.t[:, col_fn(k):col_fn(k) + 1], in1=ob.t[:, 0:T], op0=ALU.mult, op1=ALU.add))

    def padded(self, ph, n, T=L):
        kb = self.kb
        ts = self.tiles(ph, n, [128, T + 2], F32)
        for t in ts:
            kb.op("dve", [], [t.b], lambda v, t=t: v.memset(t.t[:, :], 0.0))
        return ts

    def proj_norm_resid(self, srcs, w, K, pvt, gcol, cgw=512, final=False, nxt=None, nxb=4, wpre=None):
        kb = self.kb
        with kb.phase() as ph:
            o = self.tiles(ph, 8, [128, L], F32)
            sq = self.tiles(ph, 4, [128, 512], BF16)
            stat = kb.psums[4:8]
            xb = self.tiles(ph, nxb, [128, L], F32)
            if nxb == 8:
                for c in range(8):
                    kb.dma("sp", xb[c].t[:], self.xcur[c * 128:(c + 1) * 128, :], [kb.dbuf("x", c)], [xb[c].b])
            old_rot = kb.rot; kb.rot = 4
            pend = []; nsq = [0]

            def flush(keep):
                while len(pend) > keep:
                    s_, tg, first, last = pend.pop(0)
                    kb.op("pe", [s_.b, self.onesb.b], [stat[tg].b], lambda p: p.matmul(stat[tg].t[:, :], lhsT=self.onesb.t[:, :], rhs=s_.t[:, :], start=first, stop=last))

            def epi(oc, t0, tw, ps):
                tg = t0 // 512
                kb.op("act", [ps.b], [o[oc].b], lambda a: a.activation(out=o[oc].t[:, t0:t0 + tw], in_=ps.t[:, 0:tw], func=AF.Copy))
                s_ = sq[nsq[0] % 4]; nsq[0] += 1
                kb.op("act", [ps.b], [s_.b], lambda a: a.activation(out=s_.t[:, :], in_=ps.t[:, 0:tw], func=AF.Square))
                pend.append((s_, tg, oc == 0, oc == 7))
                flush(2)
            with kb.phase() as ph2:
                self.linear_fm(ph2, w, K, 0, D, srcs, L, epi, cgw=cgw, nbuf=2, wpre=wpre)
            flush(0)
            rstd = kb.tile(ph, [128, L], F32)
            for tg in range(4):
                kb.op("act", [stat[tg].b, self.epst.b], [rstd.b], lambda a: a.activation(out=rstd.t[:, tg * 512:(tg + 1) * 512], in_=stat[tg].t[:, :], func=AF.Ln, bias=self.epst.t[:, 0:1], scale=1.0 / D))
            kb.op("act", [rstd.b], [rstd.b], lambda a: a.activation(out=rstd.t[:, :], in_=rstd.t[:, :], func=AF.Exp, scale=-0.5))
            dst = self.out if final else self.xs
            for c in range(8):
                x = xb[c % nxb]
                if nxb != 8:
                    kb.dma("sp", x.t[:], self.xcur[c * 128:(c + 1) * 128, :], [kb.dbuf("x", c)], [x.b])
                kb.op("dve", [o[c].b, rstd.b], [o[c].b], lambda v: v.tensor_tensor(out=o[c].t[:, :], in0=o[c].t[:, :], in1=rstd.t[:, :], op=ALU.mult))
                kb.op("dve", [o[c].b, x.b, pvt.b], [o[c].b], lambda v: v.scalar_tensor_tensor(
                    out=o[c].t[:, :], in0=o[c].t[:, :], scalar=pvt.t[:, gcol + c:gcol + c + 1], in1=x.t[:, :], op0=ALU.mult, op1=ALU.add))
                kb.dma("sp" if nxb == 8 else "pool", dst[c * 128:(c + 1) * 128, :], o[c].t[:], [o[c].b], [kb.dbuf("xo" if final else "x", c)])
                if nxt is not None:
                    for tg in range(4):
                        s_ = sq[nsq[0] % 4]; nsq[0] += 1
                        kb.op("act", [o[c].b], [s_.b], lambda a: a.activation(out=s_.t[:, :], in_=o[c].t[:, tg * 512:(tg + 1) * 512], func=AF.Square))
                        pend.append((s_, tg, c == 0, c == 7))
                    flush(0)
            self.xcur = self.xs
            if nxt is not None:
                flush(0)
                pvn, gn = nxt
                for tg in range(4):
                    kb.op("act", [stat[tg].b, self.epst.b], [rstd.b], lambda a: a.activation(out=rstd.t[:, tg * 512:(tg + 1) * 512], in_=stat[tg].t[:, :], func=AF.Ln, bias=self.epst.t[:, 0:1], scale=1.0 / D))
                kb.op("act", [rstd.b], [rstd.b], lambda a: a.activation(out=rstd.t[:, :], in_=rstd.t[:, :], func=AF.Exp, scale=-0.5))
                hst = self.tiles(ph, 2, [128, L], BF16)
                for c in range(8):
                    h = hst[c % 2]
                    kb.op("dve", [o[c].b, rstd.b, pvn.b], [h.b], lambda v: v.scalar_tensor_tensor(
                        out=h.t[:, :], in0=o[c].t[:, :], scalar=pvn.t[:, gn + c:gn + c + 1], in1=rstd.t[:, :], op0=ALU.mult, op1=ALU.mult))
                    kb.dma("sp", self.S["hT"][c * 128:(c + 1) * 128, :], h.t[:], [h.b], [kb.dbuf("h", c)])
            kb.rot = old_rot

    def proj_norm_resid_tg(self, srcs, w, pvt, gcol, nxt, wpre):
        kb = self.kb
        wb, npre = wpre
        pvn, gn = nxt
        with kb.phase() as ph:
            o = [self.tiles(ph, 4, [128, 512], F32) for _ in range(8)]
            sq = self.tiles(ph, 8, [128, 512], BF16)
            xb = self.tiles(ph, 8, [128, L], F32)
            for c in range(8):
                kb.dma("sp", xb[c].t[:], self.xcur[c * 128:(c + 1) * 128, :], [kb.dbuf("x", c)], [xb[c].b])
            pe_pend = []; tick = [0]

            def pe_flush(min_age):
                while pe_pend and tick[0] - pe_pend[0][0] >= min_age:
                    pe_pend.pop(0)[1]()
            rs1 = self.tiles(ph, 4, [128, 512], F32); rs2 = rs1
            hst = self.tiles(ph, 4, [128, 512], BF16)
            stat = kb.psums[4:8]
            old_rot = kb.rot; kb.rot = 4
            nsq = [0]; nh = [0]

            def rstd_from(st_, dstt):
                kb.op("act", [st_.b, self.epst.b], [dstt.b], lambda a: a.activation(out=dstt.t[:, :], in_=st_.t[:, :], func=AF.Ln, bias=self.epst.t[:, 0:1], scale=1.0 / D))
                kb.op("act", [dstt.b], [dstt.b], lambda a: a.activation(out=dstt.t[:, :], in_=dstt.t[:, :], func=AF.Exp, scale=-0.5))

            def post(tg):
                tsl = slice(tg * 512, (tg + 1) * 512)
                rstd_from(stat[tg], rs1[tg])
                yield
                for c in range(8):
                    oc_ = o[c][tg]
                    kb.op("dve", [oc_.b, rs1[tg].b], [oc_.b], lambda v: v.tensor_tensor(out=oc_.t[:, :], in0=oc_.t[:, :], in1=rs1[tg].t[:, :], op=ALU.mult))
                    kb.op("dve", [oc_.b, xb[c].b, pvt.b], [oc_.b], lambda v: v.scalar_tensor_tensor(
                        out=oc_.t[:, :], in0=oc_.t[:, :], scalar=pvt.t[:, gcol + c:gcol + c + 1], in1=xb[c].t[:, tsl], op0=ALU.mult, op1=ALU.add))
                    kb.dma("sp", self.xs[c * 128:(c + 1) * 128, tsl], oc_.t[:, :], [oc_.b], [kb.dbuf("x", c)])
                    s_ = sq[nsq[0] % 8]; nsq[0] += 1
                    kb.op("act", [oc_.b], [s_.b], lambda a: a.activation(out=s_.t[:, :], in_=oc_.t[:, :], func=AF.Square))
                    pe_pend.append((tick[0], lambda s_=s_, c=c: kb.op("pe", [s_.b, self.onesb.b], [stat[tg].b], lambda p: p.matmul(stat[tg].t[:, :], lhsT=self.onesb.t[:, :], rhs=s_.t[:, :], start=(c == 0), stop=(c == 7)))))
                    while len(pe_pend) > 4: pe_pend.pop(0)[1]()
                    yield
                pe_flush(0)
                rstd_from(stat[tg], rs2[tg])
                yield
                for c in range(8):
                    oc_ = o[c][tg]; h = hst[nh[0] % 4]; nh[0] += 1
                    kb.op("dve", [oc_.b, rs2[tg].b, pvn.b], [h.b], lambda v: v.scalar_tensor_tensor(
                        out=h.t[:, :], in0=oc_.t[:, :], scalar=pvn.t[:, gn + c:gn + c + 1], in1=rs2[tg].t[:, :], op0=ALU.mult, op1=ALU.mult))
                    kb.dma("sp", self.S["hT"][c * 128:(c + 1) * 128, tsl], h.t[:, :], [h.b], [kb.dbuf("h", c)])
                    yield

            gens = []

            def step_all(k=1):
                for _ in range(k):
                    for g_ in list(gens):
                        try:
                            next(g_)
                        except StopIteration:
                            gens.remove(g_)
            for tg in range(4):
                pend = []
                for oc in range(8):
                    wt = wb[oc // 4]; o0 = (oc % 4) * 128
                    ps = kb.psum()

                    def mm(p):
                        for kc in range(8):
                            ins = p.matmul(ps.t[:, :], lhsT=wt.t[:, kc, o0:o0 + 128], rhs=srcs[kc].t[:, tg * 512:(tg + 1) * 512], start=(kc == 0), stop=(kc == 7))
                        return ins
                    kb.op("pe", [wt.b] + [s_.b for s_ in srcs], [ps.b], mm)
                    ot = o[oc][tg]
                    kb.op("act", [ps.b], [ot.b], lambda a: a.activation(out=ot.t[:, :], in_=ps.t[:, :], func=AF.Copy))
                    s_ = sq[nsq[0] % 8]; nsq[0] += 1
                    kb.op("act", [ps.b], [s_.b], lambda a: a.activation(out=s_.t[:, :], in_=ps.t[:, :], func=AF.Square))
                    pend.append((s_, oc))
                    tick[0] += 1
                    pe_flush(2)
                    if len(pend) > 1:
                        s2, oc2 = pend.pop(0)
                        kb.op("pe", [s2.b, self.onesb.b], [stat[tg].b], lambda p: p.matmul(stat[tg].t[:, :], lhsT=self.onesb.t[:, :], rhs=s2.t[:, :], start=(oc2 == 0), stop=(oc2 == 7)))
                    step_all(2)
                s2, oc2 = pend.pop(0)
                kb.op("pe", [s2.b, self.onesb.b], [stat[tg].b], lambda p: p.matmul(stat[tg].t[:, :], lhsT=self.onesb.t[:, :], rhs=s2.t[:, :], start=(oc2 == 0), stop=(oc2 == 7)))
                gens.append(post(tg))
            while gens:
                step_all(); tick[0] += 1; pe_flush(1)
            pe_flush(0)
            kb.rot = old_rot
        self.xcur = self.xs

    def load_h(self, ph):
        return self.load_fm(ph, self.S["hT"], 8, L, BF16, deps=lambda c: [self.kb.dbuf("h", c)])

    def phase_proj(self, l):
        kb = self.kb; S = self.S; pvt = self.pv[l]; w = self.I["w_in"][l]
        with kb.phase(flip=True) as ph:
            if l == 0:
                hs = self.tiles(ph, 8, [128, L], BF16)
                with kb.phase() as phx:
                    xs = self.load_fm(phx, self.xcur, 8, L, F32, deps=self.xdeps)
                    self.norm_apply(phx, xs, hs, L, pvt, cG(0))
            else:
                hs = self.load_h(ph)
            ub = self.padded(ph, 2); ob = self.tiles(ph, 2, [128, L], F32)
            stf = self.tiles(ph, 4, [128, 512], F32); stb = self.tiles(ph, 4, [128, 512], BF16)
            cnt = {"f": 0, "b": 0, "e": 0}

            gst = ExitStack()
            self.bg_gen = self.filters_gen(l, gst)

            def epi(oc, t0, tw, ps):
                tg = t0 // 512
                cnt["e"] += 1
                if cnt["e"] > 48 and cnt["e"] % 4 == 0: self.bg()
                if oc < 12:
                    u = ub[oc % 2]; o = ob[oc % 2]
                    kb.op("act", [ps.b], [u.b], lambda a: a.activation(out=u.t[:, 1 + t0:1 + t0 + tw], in_=ps.t[:, 0:tw], func=AF.Copy))
                    kb.op("act", [ps.b, pvt.b], [o.b], lambda a: a.activation(out=o.t[:, t0:t0 + tw], in_=ps.t[:, 0:tw], func=AF.Identity, bias=0.0, scale=pvt.t[:, cHSW(1, oc):cHSW(1, oc) + 1]))
                    if tg == 3:
                        self.conv3(u, o, pvt, lambda k: cHSW(k, oc), center_done=True)
                        kb.dma("sp", S["hyuT"][oc * 128:(oc + 1) * 128, :], o.t[:], [o.b], [kb.dbuf("hyu", oc)])
                elif oc < 20:
                    s = stb[cnt["b"] % 4]; cnt["b"] += 1
                    sc = 0.125 if oc < 16 else 1.0
                    kb.op("act", [ps.b], [s.b], lambda a: a.activation(out=s.t[:, 0:tw], in_=ps.t[:, 0:tw], func=AF.Identity, bias=0.0, scale=sc))
                    dst = S["qT"] if oc < 16 else S["kT"]; r = (oc - 12) % 4
                    kb.dma("act", dst[r * 128:(r + 1) * 128, t0:t0 + tw], s.t[:, 0:tw], [s.b], [kb.dbuf("qk", oc, tg)])
                elif oc < 24:
                    raise AssertionError
                elif oc < 36:
                    s = stf[cnt["f"] % 4]; cnt["f"] += 1
                    kb.op("act", [ps.b], [s.b], lambda a: a.activation(out=s.t[:, 0:tw], in_=ps.t[:, 0:tw], func=AF.Copy))
                    r = oc - 24
                    kb.dma("act", S["scuT"][r * 128:(r + 1) * 128, t0:t0 + tw], s.t[:, 0:tw], [s.b], [kb.dbuf("scu", r, tg)])
                else:
                    s = stb[cnt["b"] % 4]; cnt["b"] += 1
                    r = oc - 36
                    kb.op("act", [ps.b, pvt.b], [s.b], lambda a: a.activation(out=s.t[:, 0:tw], in_=ps.t[:, 0:tw], func=AF.Sigmoid, bias=pvt.t[:, cGB + r:cGB + r + 1], scale=1.0))
                    kb.dma("act", S["gatesT"][r * 128:(r + 1) * 128, t0:t0 + tw], s.t[:, 0:tw], [s.b], [kb.dbuf("gates", r, tg)])
            with kb.phase() as ph2:
                self.linear_fm(ph2, w, D, 0, 2560, hs, L, epi)
            with kb.phase() as ph2:
                self.linear_fm(ph2, w, D, 3072, PW - 3072, hs, L, lambda oc, t0, tw, ps: epi(oc + 24, t0, tw, ps))
            with kb.phase() as ph2:
                va = self.tiles(ph2, 3, [128, 8, 128], BF16)
                for t in va:
                    kb.op("dve", [], [t.b], lambda v, t=t: v.memset(t.t[:, :, :], 1.0))

                def epiv(tc, gi, gw, ps):
                    t = va[tc % 3]
                    kb.op("act", [ps.b], [t.b], lambda a: a.activation(out=t.t[:, :, 0:64], in_=ps.t[:, :].rearrange("p (h d) -> p h d", h=8), func=AF.Copy))
                    kb.dma("act", S["vaug"][tc].rearrange("p (h d) -> p h d", h=8), t.t[:, :, :], [t.b], [kb.dbuf("vaug", tc)])
                self.linear_tm(ph2, w, D, 2560, 512, hs, L, epiv)
            while self.bg_gen is not None: self.bg()
            kb.side ^= 1
            gst.close()
            kb.side ^= 1

    def sin_act(self, a, m, ps, w, fcol, bcol, pvt, dst, t0):
        kb = self.kb
        kb.op("dve", [ps.b, pvt.b], [a.b], lambda v: v.tensor_scalar(out=a.t[:, 0:w], in0=ps.t[0:64, 0:w], scalar1=pvt.t[0:64, bcol:bcol + 1], scalar2=pvt.t[0:64, fcol:fcol + 1], op0=ALU.add, op1=ALU.mult))
        for it in range(2):
            kb.op("dve", [a.b], [m.b], lambda v: v.tensor_scalar(out=m.t[:, 0:w], in0=a.t[:, 0:w], scalar1=float(np.pi), scalar2=float(-2 * np.pi), op0=ALU.is_gt, op1=ALU.mult))
            kb.op("dve", [a.b, m.b], [a.b], lambda v: v.tensor_tensor(out=a.t[:, 0:w], in0=a.t[:, 0:w], in1=m.t[:, 0:w], op=ALU.add))
            kb.op("dve", [a.b], [m.b], lambda v: v.tensor_scalar(out=m.t[:, 0:w], in0=a.t[:, 0:w], scalar1=float(-np.pi), scalar2=float(2 * np.pi), op0=ALU.is_lt, op1=ALU.mult))
            kb.op("dve", [a.b, m.b], [a.b], lambda v: v.tensor_tensor(out=a.t[:, 0:w], in0=a.t[:, 0:w], in1=m.t[:, 0:w], op=ALU.add))
        kb.op("act", [a.b], [dst.b], lambda c: c.activation(out=dst.t[0:64, t0:t0 + w], in_=a.t[:, 0:w], func=AF.Sin))

    def filters_gen(self, l, p1):
        kb = self.kb; S = self.S; I = self.I; pvt = self.pv[l]
        kb.side ^= 1
        zT = kb.tile(p1, [33, L], F32); w1 = kb.tile(p1, [33, 64], F32); w2 = kb.tile(p1, [64, 64], F32)
        w3 = kb.tile(p1, [64, 2048], BF16)
        h1 = kb.tile(p1, [64, L], F32); h2 = kb.tile(p1, [64, L], F32); h2b = kb.tile(p1, [64, L], BF16)
        sa = kb.tile(p1, [64, 512], F32); sm = kb.tile(p1, [64, 512], F32)
        dec = self.tiles(p1, 2, [128, 512], F32); decb = self.tiles(p1, 2, [128, 512], F32)
        tf = self.tiles(p1, 2, [128, 512], F32); tb = self.tiles(p1, 2, [128, 512], F32)
        hso = self.tiles(p1, 2, [128, 1024], BF16); hdo = self.tiles(p1, 2, [128, 1024], BF16)
        kb.side ^= 1
        kb.dma("sp", zT.t[:], I["zT"][:, :], [], [zT.b]); kb.dma("sp", w1.t[:], I["hy_w1"][l], [], [w1.b])
        kb.dma("sp", w2.t[:], I["hy_w2"][l], [], [w2.b]); kb.dma("pool", w3.t[:], I["hy_w3"][l], [], [w3.b])
        yield
        for src, wt, dst, fc, bc in ((zT, w1, h1, cF0, cB1), (h1, w2, h2, cF1, cB2)):
            kk = 33 if src is zT else 64
            for t0 in range(0, L, 512):
                ps = kb.psum()
                kb.op("pe", [src.b, wt.b], [ps.b], lambda p: p.matmul(ps.t[0:64, :], lhsT=wt.t[0:kk, :], rhs=src.t[0:kk, t0:t0 + 512], start=True, stop=True))
                self.sin_act(sa, sm, ps, 512, fc, bc, pvt, dst, t0)
                yield
        kb.op("act", [h2.b], [h2b.b], lambda a: a.activation(out=h2b.t[:, :], in_=h2.t[:, :], func=AF.Copy))
        yield
        for jc in range(16):
            d = dec[jc % 2]; db = decb[jc % 2]; hs_ = hso[jc % 2]; hd_ = hdo[jc % 2]
            kb.dma("sp", d.t[:], I["decay"][jc * 128:(jc + 1) * 128, :], [], [d.b])
            kb.dma("sp", db.t[:], I["decayb"][jc * 128:(jc + 1) * 128, :], [], [db.b])
            for o in range(2):
                psf = kb.psum(); psb = kb.psum()
                par = jc // 8; mc = jc % 8
                taps = h2b.t[0:64, mc * 256 + par:mc * 256 + 256:2]
                kb.op("pe", [h2b.b, w3.b], [psf.b], lambda p: p.matmul(psf.t[:, :], lhsT=taps, rhs=w3.t[0:64, o * 512:(o + 1) * 512], start=True, stop=True))
                kb.op("pe", [h2b.b, w3.b], [psb.b], lambda p: p.matmul(psb.t[:, :], lhsT=taps, rhs=w3.t[0:64, 1024 + o * 512:1024 + (o + 1) * 512], start=True, stop=True))
                f = tf[o]; b = tb[o]
                kb.op("dve", [psf.b, d.b], [f.b], lambda v: v.tensor_tensor(out=f.t[:, :], in0=psf.t[:, :], in1=d.t[:, :], op=ALU.mult))
                kb.op("dve", [psb.b, db.b], [b.b], lambda v: v.tensor_tensor(out=b.t[:, :], in0=psb.t[:, :], in1=db.t[:, :], op=ALU.mult))
                kb.op("dve", [f.b, b.b], [hs_.b], lambda v: v.tensor_tensor(out=hs_.t[:, o * 512:(o + 1) * 512], in0=f.t[:, :], in1=b.t[:, :], op=ALU.add))
                kb.op("dve", [f.b, b.b], [hd_.b], lambda v: v.tensor_tensor(out=hd_.t[:, o * 512:(o + 1) * 512], in0=b.t[:, :], in1=f.t[:, :], op=ALU.subtract))
                yield
            kb.dma("sp", S["hsd"][0, jc * 128:(jc + 1) * 128, :], hs_.t[:], [hs_.b], [kb.dbuf("hsd", 0, jc)])
            kb.dma("sp", S["hsd"][1, jc * 128:(jc + 1) * 128, :], hd_.t[:], [hd_.b], [kb.dbuf("hsd", 1, jc)])

    def bg(self):
        if self.bg_gen is not None:
            try:
                next(self.bg_gen)
            except StopIteration:
                self.bg_gen = None

    def phase_filters(self, l):
        kb = self.kb; S = self.S; I = self.I; pvt = self.pv[l]
        with kb.phase(flip=True) as ph:
            hsum = self.tiles(ph, 16, [128, 1024], BF16); hdif = self.tiles(ph, 16, [128, 1024], BF16)
            for jc in range(16):
                kb.dma("sp", hsum[jc].t[:], S["hsd"][0, jc * 128:(jc + 1) * 128, :], [kb.dbuf("hsd", 0, jc)], [hsum[jc].b])
                kb.dma("sp", hdif[jc].t[:], S["hsd"][1, jc * 128:(jc + 1) * 128, :], [kb.dbuf("hsd", 1, jc)], [hdif[jc].b])
            ct = self.tiles(ph, 2, [128, 8, 128], BF16); stt = self.tiles(ph, 2, [128, 8, 128], BF16)
            t1 = self.tiles(ph, 2, [128, 512], F32); t3 = self.tiles(ph, 2, [128, 512], F32)
            n1 = self.tiles(ph, 2, [128, 512], F32); t2 = self.tiles(ph, 2, [128, 512], F32)
            og_ = self.tiles(ph, 8, [128, 512], F32); n = 0; it = 0
            tw = self.tw
            for kc in range(8):
                c = ct[kc % 2]; s_ = stt[kc % 2]
                kb.dma("sp", c.t[:], I["CTb"][kc], [], [c.b]); kb.dma("sp", s_.t[:], I["STb"][kc], [], [s_.b])
                cth = tw.t[:, kc:kc + 1]; sth = tw.t[:, 8 + kc:9 + kc]
                for og in range(2):
                    def tr(mat, src, par):
                        ps = kb.psum()

                        def mm(p):
                            for mc in range(8):
                                ins = p.matmul(ps.t[:, :], lhsT=mat.t[:, mc, :], rhs=src[par * 8 + mc].t[:, og * 512:(og + 1) * 512], start=(mc == 0), stop=(mc == 7))
                            return ins
                        kb.op("pe", [mat.b] + [x.b for x in src[par * 8:par * 8 + 8]], [ps.b], mm)
                        return ps
                    a1 = t1[it % 2]; a3 = t3[it % 2]; N1 = n1[it % 2]; T2 = t2[it % 2]; it += 1
                    Ec = tr(c, hsum, 0); Oc = tr(c, hsum, 1); Os = tr(s_, hsum, 1)
                    kb.op("act", [Oc.b, tw.b], [a1.b], lambda a: a.activation(out=a1.t[:, :], in_=Oc.t[:, :], func=AF.Identity, bias=0.0, scale=cth))
                    kb.op("dve", [Os.b, a1.b, tw.b], [N1.b], lambda v: v.scalar_tensor_tensor(out=N1.t[:, :], in0=Os.t[:, :], scalar=sth, in1=a1.t[:, :], op0=ALU.mult, op1=ALU.subtract))
                    outs = []
                    for half, op_ in ((0, ALU.subtract), (1, ALU.add)):
                        g = og_[n % 8]; n += 1
                        kb.op("dve", [Ec.b, N1.b], [g.b], lambda v: v.tensor_tensor(out=g.t[:, :], in0=Ec.t[:, :], in1=N1.t[:, :], op=op_))
                        kb.dma("act", S["kfr"][(2 * kc + half) * 128:(2 * kc + half + 1) * 128, og * 512:(og + 1) * 512], g.t[:], [g.b], [kb.dbuf("kfr", 2 * kc + half, og)])
                    Es = tr(s_, hdif, 0); Oc2 = tr(c, hdif, 1); Os2 = tr(s_, hdif, 1)
                    kb.op("act", [Oc2.b, tw.b], [a3.b], lambda a: a.activation(out=a3.t[:, :], in_=Oc2.t[:, :], func=AF.Identity, bias=0.0, scale=sth))
                    kb.op("dve", [Os2.b, a3.b, tw.b], [T2.b], lambda v: v.scalar_tensor_tensor(out=T2.t[:, :], in0=Os2.t[:, :], scalar=cth, in1=a3.t[:, :], op0=ALU.mult, op1=ALU.add))
                    for half, op_ in ((0, ALU.add), (1, ALU.subtract)):
                        g = og_[n % 8]; n += 1
                        kb.op("dve", [Es.b, T2.b], [g.b], lambda v: v.tensor_tensor(out=g.t[:, :], in0=T2.t[:, :], in1=Es.t[:, :], op=op_))
                        kb.dma("act", S["kfi"][(2 * kc + half) * 128:(2 * kc + half + 1) * 128, og * 512:(og + 1) * 512], g.t[:], [g.b], [kb.dbuf("kfi", 2 * kc + half, og)])

    def phase_hyena(self, l):
        kb = self.kb; S = self.S; I = self.I; pvt = self.pv[l]; tw = self.tw
        with kb.phase(flip=True) as ph:
            zv = self.load_fm(ph, S["hyuT"][0:512, :], 4, L, F32, deps=lambda c: [kb.dbuf("hyu", c)])
            cm1 = kb.tile(ph, [128, 8, 512], BF16); sm1 = kb.tile(ph, [128, 8, 512], BF16)
            UV = [self.tiles(ph, 8, [128, 512], BF16) for _ in range(4)]
            xm = self.tiles(ph, 2, [128, 1024], F32); e1 = self.tiles(ph, 2, [128, 512], F32); ost = self.tiles(ph, 2, [128, 1024], BF16)
            ztok = self.tiles(ph, 16, [128, 512], BF16)
            ct = self.tiles(ph, 2, [128, 8, 128], BF16); stt = self.tiles(ph, 2, [128, 8, 128], BF16)
            kf = [self.tiles(ph, 2, [128, 512], F32) for _ in range(4)]
            F = lambda k: self.tiles(ph, k, [128, 512], F32)
            a1, a3, N1, T2 = F(4)
            ABd = [F(4), F(4)]
            Md = [F(4), F(4)]; TTd = [F(2), F(2)]
            YWb = [self.tiles(ph, 4, [128, 512], BF16), self.tiles(ph, 4, [128, 512], BF16)]
            tt = lambda o, a, b, op: (lambda v: v.tensor_tensor(out=o, in0=a, in1=b, op=op))
            for order in range(2):
                for par in range(2):
                    for mc in range(8):
                        ps = kb.psum()

                        def tr(p):
                            for cc in range(4):
                                ins = p.transpose(ps.t[:, cc * 128:(cc + 1) * 128], zv[cc].t[:, mc * 256 + par:mc * 256 + 256:2], self.identf.t[:, :])
                            return ins
                        kb.op("pe", [z.b for z in zv] + [self.identf.b], [ps.b], tr)
                        zt = ztok[par * 8 + mc]
                        kb.op("act", [ps.b], [zt.b], lambda a: a.activation(out=zt.t[:, :], in_=ps.t[:, :], func=AF.Copy))
                def stageA(kc):
                    c = ct[kc % 2]; s_ = stt[kc % 2]
                    kb.dma("sp", c.t[:], I["CTb"][kc], [], [c.b]); kb.dma("sp", s_.t[:], I["STb"][kc], [], [s_.b])
                    K = [kf[i][kc % 2] for i in range(4)]
                    for i, (nm, half) in enumerate((("kfr", 0), ("kfr", 1), ("kfi", 0), ("kfi", 1))):
                        kb.dma("sp", K[i].t[:], S[nm][(2 * kc + half) * 128:(2 * kc + half + 1) * 128, order * 512:(order + 1) * 512], [kb.dbuf(nm, 2 * kc + half, order)], [K[i].b])
                    cth = tw.t[:, kc:kc + 1]; sth = tw.t[:, 8 + kc:9 + kc]

                    def trf(mat, par):
                        ps = kb.psum()

                        def mm(p):
                            for mc in range(8):
                                ins = p.matmul(ps.t[:, :], lhsT=mat.t[:, mc, :], rhs=ztok[par * 8 + mc].t[:, :], start=(mc == 0), stop=(mc == 7))
                            return ins
                        kb.op("pe", [mat.b] + [x.b for x in ztok[par * 8:par * 8 + 8]], [ps.b], mm)
                        return ps
                    Ec = trf(c, 0); Es = trf(s_, 0); Oc = trf(c, 1); Os = trf(s_, 1)
                    kb.op("act", [Oc.b, tw.b], [a1.b], lambda a: a.activation(out=a1.t[:, :], in_=Oc.t[:, :], func=AF.Identity, bias=0.0, scale=cth))
                    kb.op("act", [Oc.b, tw.b], [a3.b], lambda a: a.activation(out=a3.t[:, :], in_=Oc.t[:, :], func=AF.Identity, bias=0.0, scale=sth))
                    kb.op("dve", [Os.b, a1.b, tw.b], [N1.b], lambda v: v.scalar_tensor_tensor(out=N1.t[:, :], in0=Os.t[:, :], scalar=sth, in1=a1.t[:, :], op0=ALU.mult, op1=ALU.subtract))
                    kb.op("dve", [Os.b, a3.b, tw.b], [T2.b], lambda v: v.scalar_tensor_tensor(out=T2.t[:, :], in0=Os.t[:, :], scalar=cth, in1=a3.t[:, :], op0=ALU.mult, op1=ALU.add))
                    Alo, Ahi, Blo, Bhi = ABd[kc % 2]
                    kb.op("dve", [Ec.b, N1.b], [Alo.b], tt(Alo.t[:, :], Ec.t[:, :], N1.t[:, :], ALU.subtract))
                    kb.op("dve", [Ec.b, N1.b], [Ahi.b], tt(Ahi.t[:, :], Ec.t[:, :], N1.t[:, :], ALU.add))
                    kb.op("dve", [Es.b, T2.b], [Blo.b], tt(Blo.t[:, :], T2.t[:, :], Es.t[:, :], ALU.add))
                    kb.op("dve", [Es.b, T2.b], [Bhi.b], tt(Bhi.t[:, :], T2.t[:, :], Es.t[:, :], ALU.subtract))
                    for half, (A_, B_, Kr, Ki) in enumerate(((Alo, Blo, K[0], K[2]), (Ahi, Bhi, K[1], K[3]))):
                        mA = Md[kc % 2][2 * half]; mB = Md[kc % 2][2 * half + 1]
                        kb.op("dve", [A_.b, Ki.b], [mA.b], tt(mA.t[:, :], A_.t[:, :], Ki.t[:, :], ALU.mult))
                        kb.op("dve", [B_.b, Ki.b], [mB.b], tt(mB.t[:, :], B_.t[:, :], Ki.t[:, :], ALU.mult))
                        kb.op("dve", [A_.b, Kr.b], [A_.b], tt(A_.t[:, :], A_.t[:, :], Kr.t[:, :], ALU.mult))
                        kb.op("dve", [B_.b, Kr.b], [B_.b], tt(B_.t[:, :], B_.t[:, :], Kr.t[:, :], ALU.mult))
                        Yb = YWb[kc % 2][half]; Wb = YWb[kc % 2][2 + half]
                        kb.op("pool", [A_.b, mB.b], [Yb.b], tt(Yb.t[:, :], A_.t[:, :], mB.t[:, :], ALU.add))
                        kb.op("pool", [B_.b, mA.b], [Wb.b], tt(Wb.t[:, :], B_.t[:, :], mA.t[:, :], ALU.subtract))
                def stageB(kc):
                    cth = tw.t[:, kc:kc + 1]; sth = tw.t[:, 8 + kc:9 + kc]
                    Ue, Ve, Uo, Vo = (UV[i][kc] for i in range(4))
                    x1_, x2_ = TTd[kc % 2]
                    Y0, Y1, W0, W1 = YWb[kc % 2]
                    I_ = self.identb; N_ = self.nidentb

                    def comb(x0, x1, s1):
                        ps = kb.psum()

                        def mm(p):
                            p.matmul(ps.t[:, :], lhsT=I_.t[:, :], rhs=x0.t[:, :], start=True, stop=False)
                            return p.matmul(ps.t[:, :], lhsT=s1.t[:, :], rhs=x1.t[:, :], start=False, stop=True)
                        kb.op("pe", [x0.b, x1.b, I_.b, s1.b], [ps.b], mm)
                        return ps
                    pUe = comb(Y0, Y1, I_); pVe = comb(W0, W1, N_); pP = comb(Y0, Y1, N_); pQ = comb(W0, W1, I_)
                    kb.op("act", [pUe.b], [Ue.b], lambda a: a.activation(out=Ue.t[:, :], in_=pUe.t[:, :], func=AF.Copy))
                    kb.op("act", [pVe.b], [Ve.b], lambda a: a.activation(out=Ve.t[:, :], in_=pVe.t[:, :], func=AF.Copy))
                    kb.op("act", [pP.b, tw.b], [x1_.b], lambda a: a.activation(out=x1_.t[:, :], in_=pP.t[:, :], func=AF.Identity, bias=0.0, scale=cth))
                    kb.op("act", [pP.b, tw.b], [x2_.b], lambda a: a.activation(out=x2_.t[:, :], in_=pP.t[:, :], func=AF.Identity, bias=0.0, scale=sth))
                    kb.op("dve", [pQ.b, x1_.b, tw.b], [Uo.b], lambda v: v.scalar_tensor_tensor(out=Uo.t[:, :], in0=pQ.t[:, :], scalar=sth, in1=x1_.t[:, :], op0=ALU.mult, op1=ALU.add))
                    kb.op("dve", [pQ.b, x2_.b, tw.b], [Vo.b], lambda v: v.scalar_tensor_tensor(out=Vo.t[:, :], in0=pQ.t[:, :], scalar=cth, in1=x2_.t[:, :], op0=ALU.mult, op1=ALU.subtract))
                kb.rot = 8
                stageA(0)
                for kc in range(8):
                    if kc + 1 < 8: stageA(kc + 1)
                    stageB(kc)
                kb.rot = 6
                xoff = 512 * (order + 1); it = 0
                for mg in range(2):
                    kb.dma("sp", cm1.t[:], I["Cm"][mg], [], [cm1.b]); kb.dma("sp", sm1.t[:], I["Sm"][mg], [], [sm1.b])
                    for cc in range(4):
                        x_ = xm[it % 2]; o_ = ost[it % 2]; it += 1
                        kb.dma("sp", x_.t[:], S["hyuT"][xoff + cc * 128:xoff + (cc + 1) * 128, mg * 1024:(mg + 1) * 1024], [kb.dbuf("hyu", xoff // 128 + cc)], [x_.b])
                        z = zv[cc]
                        for par in range(2):
                            U = UV[2 * par]; V = UV[2 * par + 1]
                            ps = kb.psum()

                            def mm(p):
                                for kc in range(8):
                                    p.matmul(ps.t[:, :], lhsT=U[kc].t[:, cc * 128:(cc + 1) * 128], rhs=cm1.t[:, kc, :], start=(kc == 0), stop=False)
                                for kc in range(8):
                                    ins = p.matmul(ps.t[:, :], lhsT=V[kc].t[:, cc * 128:(cc + 1) * 128], rhs=sm1.t[:, kc, :], start=False, stop=(kc == 7))
                                return ins
                            kb.op("pe", [cm1.b, sm1.b] + [x.b for x in U] + [x.b for x in V], [ps.b], mm)
                            t_ = e1[par]
                            zsl = z.t[:, mg * 1024 + par:(mg + 1) * 1024:2]; xsl = x_.t[:, par:1024:2]
                            kb.op("dve", [z.b, ps.b, pvt.b], [t_.b], lambda v: v.scalar_tensor_tensor(
                                out=t_.t[:, :], in0=zsl, scalar=pvt.t[:, cHB(order, cc):cHB(order, cc) + 1], in1=ps.t[:, :], op0=ALU.mult, op1=ALU.add))
                            if order == 0:
                                kb.op("dve", [t_.b, x_.b], [z.b], lambda v: v.tensor_tensor(out=zsl, in0=t_.t[:, :], in1=xsl, op=ALU.mult))
                            else:
                                kb.op("dve", [t_.b, x_.b], [o_.b], lambda v: v.tensor_tensor(out=o_.t[:, par:1024:2], in0=t_.t[:, :], in1=xsl, op=ALU.mult))
                        if order == 1:
                            kb.dma("pool", S["yaT"][cc * 128:(cc + 1) * 128, mg * 1024:(mg + 1) * 1024], o_.t[:], [o_.b], [kb.dbuf("ya", cc, 2 * mg), kb.dbuf("ya", cc, 2 * mg + 1)])

    def phase_na(self, l):
        kb = self.kb; S = self.S; I = self.I
        with kb.phase(flip=True) as ph:
            qT = self.tiles(ph, 4, [128, L], BF16)
            kTm = self.tiles(ph, 8, [128, L], BF16)
            tabs = self.tiles(ph, 8, [128, NS * 64], BF16); rm = kb.tile(ph, [128, 20 * 512], BF16); sel = kb.tile(ph, [128, 128], BF16)
            va = self.tiles(ph, 16, [128, 1024], BF16)
            for h in range(8):
                kb.op("dve", [], [kTm[h].b], lambda v: v.memset(kTm[h].t[:, :], 0.0))
                kb.dma("pool", tabs[h].t[:], I["natab"][l][:, h * NS * 64:(h + 1) * NS * 64], [], [tabs[h].b])
            kb.dma("sp", sel.t[:], I["nasel"][:, :], [], [sel.b]); kb.dma("sp", rm.t[:], I["narm"][:, :], [], [rm.b])
            for h in range(8):
                hc = h // 2; bp = (h % 2) * 64
                if h % 2 == 0:
                    kb.dma("sp", qT[hc].t[:], S["qT"][hc * 128:(hc + 1) * 128, :], [kb.dbuf("qk", 12 + hc, tg) for tg in range(4)], [qT[hc].b])
                kb.dma("sp", kTm[h].t[bp:bp + 64, :], S["kT"][hc * 128 + bp:hc * 128 + bp + 64, :], [kb.dbuf("qk", 16 + hc, tg) for tg in range(4)], [kTm[h].b])
                if h == 0:
                    for tc in range(16):
                        kb.dma("sp", va[tc].t[:], S["vaug"][tc], [kb.dbuf("vaug", tc)], [va[tc].b])
            yb = self.tiles(ph, 4, [128, L], BF16)
            pt = self.tiles(ph, 4, [128, 512], BF16)
            rd = self.tiles(ph, 2, [64, 512], F32)
            items = [(h, g, ji, j) for h in range(8) for g in range(4) for ji, j in enumerate(NA_GROUP_CHUNKS[g])]
            P = {}
            gst = ExitStack(); self.bg_gen = self.sc_gen(l, gst)

            def emit_s(i):
                h, g, ji, j = items[i]
                hc = h // 2
                p_ = pt[i % 4]
                ps = kb.psum()
                s0 = 11 - 2 * j + 8 * g; ti = NA_TILE_BASE[g] + ji
                assert 0 <= s0 and s0 + 8 <= NS

                def mm(p):
                    p.matmul(ps.t[:, :], lhsT=kTm[h].t[:, j * 128:(j + 1) * 128], rhs=qT[hc].t[:, g * 512:(g + 1) * 512], start=True, stop=False)
                    p.matmul(ps.t[:, :], lhsT=self.identb.t[:, :], rhs=tabs[h].t[:, s0 * 64:(s0 + 8) * 64], start=False, stop=False)
                    return p.matmul(ps.t[:, :], lhsT=sel.t[:, :], rhs=rm.t[:, ti * 512:(ti + 1) * 512], start=False, stop=True)
                kb.op("pe", [kTm[h].b, qT[hc].b, tabs[h].b, rm.b, sel.b, self.identb.b], [ps.b], mm)
                kb.op("act", [ps.b], [p_.b], lambda a: a.activation(out=p_.t[:, :], in_=ps.t[:, :], func=AF.Exp))
                P[i] = p_
            emit_s(0); emit_s(1)
            m = 0; po = None
            for i, (h, g, ji, j) in enumerate(items):
                hc = h // 2; bp = (h % 2) * 64
                if i + 2 < len(items): emit_s(i + 2)
                if i % 7 == 3: self.bg()
                nch = len(NA_GROUP_CHUNKS[g])
                if ji == 0: po = kb.acc()
                p_ = P.pop(i)
                kb.op("pe", [va[j].b, p_.b], [po.b], lambda p: p.matmul(po.t[:, :], lhsT=va[j].t[:, h * 128:(h + 1) * 128], rhs=p_.t[:, :], start=(ji == 0), stop=(ji == nch - 1)))
                if ji == nch - 1:
                    r_ = rd[m % 2]; m += 1
                    kb.op("act", [po.b], [r_.b], lambda a: a.activation(out=r_.t[:, :], in_=po.t[64:128, :], func=AF.Ln))
                    kb.op("act", [r_.b], [r_.b], lambda a: a.activation(out=r_.t[:, :], in_=r_.t[:, :], func=AF.Exp, scale=-1.0))
                    kb.op("dve", [po.b, r_.b], [yb[hc].b], lambda v: v.tensor_tensor(out=yb[hc].t[bp:bp + 64, g * 512:(g + 1) * 512], in0=po.t[0:64, :], in1=r_.t[:, :], op=ALU.mult))
            while self.bg_gen is not None: self.bg()
            gst.close()
            for c in range(4):
                kb.dma("sp", S["ybT"][c * 128:(c + 1) * 128, :], yb[c].t[:], [yb[c].b], [kb.dbuf("yb", c)])

    def sc_gen(self, l, st):
        kb = self.kb; S = self.S; pvt = self.pv[l]
        kb.side ^= 1
        ub = self.tiles(st, 1, [128, L + 2], F32)[0]; cv = kb.tile(st, [128, L], F32); ot = self.tiles(st, 2, [128, L], BF16)
        bb = kb.tile(st, [128, L], F32); cb = kb.tile(st, [128, L], F32); xb = kb.tile(st, [128, L], F32)
        kb.side ^= 1
        kb.op("dve", [], [ub.b], lambda v: v.memset(ub.t[:, :], 0.0))
        yield
        for cc in range(4):
            for t, off in ((bb, 0), (cb, 512), (xb, 1024)):
                kb.dma("sp", t.t[:], S["scuT"][off + cc * 128:off + (cc + 1) * 128, :], [kb.dbuf("scu", off // 128 + cc, tg) for tg in range(4)], [t.b])
            yield
            kb.op("dve", [cb.b, xb.b], [ub.b], lambda v: v.tensor_tensor(out=ub.t[:, 1:L + 1], in0=cb.t[:, :], in1=xb.t[:, :], op=ALU.mult))
            yield
            kb.op("dve", [ub.b, pvt.b], [cv.b], lambda v: v.tensor_scalar(out=cv.t[:, :], in0=ub.t[:, 1:1 + L], scalar1=pvt.t[:, cSCW(1, cc):cSCW(1, cc) + 1], scalar2=None, op0=ALU.mult))
            yield
            for k in (0, 2):
                kb.op("dve", [ub.b, pvt.b, cv.b], [cv.b], lambda v: v.scalar_tensor_tensor(
                    out=cv.t[:, :], in0=ub.t[:, k:k + L], scalar=pvt.t[:, cSCW(k, cc):cSCW(k, cc) + 1], in1=cv.t[:, :], op0=ALU.mult, op1=ALU.add))
                yield
            o_ = ot[cc % 2]
            kb.op("dve", [cv.b, bb.b], [o_.b], lambda v: v.tensor_tensor(out=o_.t[:, :], in0=cv.t[:, :], in1=bb.t[:, :], op=ALU.mult))
            kb.dma("sp", S["ycT"][cc * 128:(cc + 1) * 128, :], o_.t[:], [o_.b], [kb.dbuf("yc", cc)])
            yield

    def phase_sc(self, l):
        pass

    def phase_merge(self, l):
        kb = self.kb; S = self.S; I = self.I; pvt = self.pv[l]
        with kb.phase(flip=True) as ph0:
          ms = self.tiles(ph0, 8, [128, L], BF16)
          wpre = self.prefetch_w(ph0, I["w_out"][l], D, 0, D, 512, 2)
          with kb.phase() as ph:
            ys = [self.load_fm(ph, S["yaT"], 4, L, BF16, deps=lambda c: [kb.dbuf("ya", c, tg) for tg in range(4)]),
                  self.load_fm(ph, S["ybT"], 4, L, BF16, deps=lambda c: [kb.dbuf("yb", c)]),
                  self.load_fm(ph, S["ycT"], 4, L, BF16, deps=lambda c: [kb.dbuf("yc", c)])]
            wb = self.tiles(ph, 3, [128, 4, D], BF16)
            for b in range(3):
                kb.dma("pool", wb[b].t[:], I["w_branch"][l, b].rearrange("(kc p) n -> p kc n", p=128), [], [wb[b].b])
            gt = self.tiles(ph, 6, [128, 512], BF16); acc = self.tiles(ph, 2, [128, 512], F32); tmp = self.tiles(ph, 4, [128, 512], F32)
            ost = self.tiles(ph, 2, [128, 512], BF16)
            n = 0; it = 0
            for oc in range(8):
                for tg in range(4):
                    a_ = acc[it % 2]; o_ = ost[it % 2]; it += 1
                    for b in range(3):
                        t_ = tmp[(2 * it + b) % 4]
                        g_ = gt[n % 6]; n += 1
                        kb.dma("sp", g_.t[:], S["gatesT"][(b * 8 + oc) * 128:(b * 8 + oc + 1) * 128, tg * 512:(tg + 1) * 512], [kb.dbuf("gates", b * 8 + oc, tg)], [g_.b])
                        ps = kb.psum()

                        def mm(p, b=b, ps=ps):
                            for cc in range(4):
                                ins = p.matmul(ps.t[:, :], lhsT=wb[b].t[:, cc, oc * 128:(oc + 1) * 128], rhs=ys[b][cc].t[:, tg * 512:(tg + 1) * 512], start=(cc == 0), stop=(cc == 3))
                            return ins
                        kb.op("pe", [wb[b].b] + [y.b for y in ys[b]], [ps.b], mm)
                        if b == 0:
                            kb.op("dve", [ps.b, g_.b], [a_.b], lambda v, a_=a_, g_=g_, ps=ps: v.tensor_tensor(out=a_.t[:, :], in0=ps.t[:, :], in1=g_.t[:, :], op=ALU.mult))
                        else:
                            kb.op("dve", [ps.b, g_.b], [t_.b], lambda v, t_=t_, g_=g_, ps=ps: v.tensor_tensor(out=t_.t[:, :], in0=ps.t[:, :], in1=g_.t[:, :], op=ALU.mult))
                            if b == 1:
                                kb.op("pool", [a_.b, t_.b], [a_.b], lambda v, a_=a_, t_=t_: v.tensor_tensor(out=a_.t[:, :], in0=a_.t[:, :], in1=t_.t[:, :], op=ALU.add))
                            else:
                                kb.op("dve", [a_.b, t_.b], [ms[oc].b], lambda v, a_=a_, t_=t_: v.tensor_tensor(out=ms[oc].t[:, tg * 512:(tg + 1) * 512], in0=a_.t[:, :], in1=t_.t[:, :], op=ALU.add))
          self.proj_norm_resid_tg(ms, I["w_out"][l], pvt, cG(1), (pvt, cG(2)), wpre)

    def phase_xattn(self, l):
        kb = self.kb; I = self.I; pvt = self.pv[l]
        with kb.phase(flip=True) as ph:
            oT = self.tiles(ph, 8, [128, L], BF16)
            wpre = (self.tiles(ph, 2, [128, 8, 512], BF16), 2)
            with kb.phase() as pa:
                hs = self.load_h(pa); mn = self.tiles(pa, 8, [128, NM], BF16)
                qT = self.tiles(pa, 8, [128, L], BF16); kmT = self.tiles(pa, 8, [128, NM], BF16); vm = self.tiles(pa, 2, [128, D], BF16)
                pt = self.tiles(pa, 6, [128, 512], BF16); rdn = self.tiles(pa, 3, [128, 512], F32)
                with kb.phase() as phx:
                    ms = self.load_fm(phx, I["memT"], 8, NM, F32)
                    self.norm_apply(phx, ms, mn, NM, pvt, cMN)
                with kb.phase() as p2:
                    self.linear_fm(p2, I["xa_wq"][l], D, 0, D, hs, L, lambda oc, t0, tw, ps: kb.op("act", [ps.b], [qT[oc].b], lambda a: a.activation(out=qT[oc].t[:, t0:t0 + tw], in_=ps.t[:, 0:tw], func=AF.Identity, bias=0.0, scale=1.0 / 16)))
                with kb.phase() as p2:
                    self.linear_fm(p2, I["xa_wkv"][l], D, 0, D, mn, NM, lambda oc, t0, tw, ps: kb.op("act", [ps.b], [kmT[oc].b], lambda a: a.activation(out=kmT[oc].t[:, t0:t0 + tw], in_=ps.t[:, 0:tw], func=AF.Copy)))
                with kb.phase() as p2:
                    self.linear_tm(p2, I["xa_wkv"][l], D, D, D, mn, NM, lambda tc, gi, gw, ps: kb.op("act", [ps.b], [vm[tc].b], lambda a: a.activation(out=vm[tc].t[:, gi * 512:gi * 512 + gw], in_=ps.t[:, 0:gw], func=AF.Copy)))
                items = [(hh, tg) for hh in range(4) for tg in range(4)]
                ST = {}
                for n_ in range(2):
                    kb.dma("pool", wpre[0][n_].t[:], I["xa_wo"][l][:, n_ * 512:(n_ + 1) * 512].rearrange("(kc p) n -> p kc n", p=128), [], [wpre[0][n_].b])

                def stage1(i):
                    hh, tg = items[i]
                    P = []
                    for mc in range(2):
                        ps = kb.psum(); p_ = pt[(2 * i + mc) % 6]

                        def mm(p):
                            for fc in range(2):
                                ins = p.matmul(ps.t[:, :], lhsT=kmT[2 * hh + fc].t[:, mc * 128:(mc + 1) * 128], rhs=qT[2 * hh + fc].t[:, tg * 512:(tg + 1) * 512], start=(fc == 0), stop=(fc == 1))
                            return ins
                        kb.op("pe", [kmT[2 * hh].b, kmT[2 * hh + 1].b, qT[2 * hh].b, qT[2 * hh + 1].b], [ps.b], mm)
                        kb.op("act", [ps.b], [p_.b], lambda a: a.activation(out=p_.t[:, :], in_=ps.t[:, :], func=AF.Exp))
                        P.append(p_)
                    pd = kb.acc()

                    def mmd(p):
                        for mc in range(2):
                            ins = p.matmul(pd.t[:, :], lhsT=self.onesb.t[:, :], rhs=P[mc].t[:, :], start=(mc == 0), stop=(mc == 1))
                        return ins
                    kb.op("pe", [P[0].b, P[1].b, self.onesb.b], [pd.b], mmd)
                    r_ = rdn[i % 3]
                    kb.op("act", [pd.b], [r_.b], lambda a: a.activation(out=r_.t[:, :], in_=pd.t[:, :], func=AF.Ln))
                    kb.op("act", [r_.b], [r_.b], lambda a: a.activation(out=r_.t[:, :], in_=r_.t[:, :], func=AF.Exp, scale=-1.0))
                    ST[i] = (P, r_)

                def stage2(i):
                    hh, tg = items[i]
                    P, r_ = ST.pop(i)
                    for dc in range(2):
                        po = kb.psum()

                        def mmo(p):
                            for mc in range(2):
                                ins = p.matmul(po.t[:, :], lhsT=vm[mc].t[:, hh * 256 + dc * 128:hh * 256 + (dc + 1) * 128], rhs=P[mc].t[:, :], start=(mc == 0), stop=(mc == 1))
                            return ins
                        kb.op("pe", [P[0].b, P[1].b, vm[0].b, vm[1].b], [po.b], mmo)
                        ot = oT[2 * hh + dc]
                        kb.op("dve", [po.b, r_.b], [ot.b], lambda v: v.tensor_tensor(out=ot.t[:, tg * 512:(tg + 1) * 512], in0=po.t[:, :], in1=r_.t[:, :], op=ALU.mult))
                stage1(0)
                for i in range(len(items)):
                    if i + 1 < len(items): stage1(i + 1)
                    stage2(i)
            self.proj_norm_resid_tg(oT, I["xa_wo"][l], pvt, cG(3), (pvt, cG(4)), wpre)

    def phase_ffn(self, l, final):
        kb = self.kb; I = self.I; pvt = self.pv[l]; w = I["ffn_up"][l]
        with kb.phase(flip=True) as ph:
            tT = self.tiles(ph, 22, [128, L], BF16)
            with kb.phase() as pa:
                hs = self.load_h(pa)
                ug = self.padded(pa, 2); uv = self.padded(pa, 2)
                cg = self.tiles(pa, 2, [128, L], F32); cv = self.tiles(pa, 2, [128, L], F32)
                wb = self.tiles(pa, 4, [128, 8, 128], BF16)
                for i in range(22):
                    for part, (ub, ob, col, ci) in enumerate(((ug[i % 2], cg[i % 2], i * 128, i), (uv[i % 2], cv[i % 2], DFF + i * 128, 22 + i))):
                        wt = wb[(2 * i + part) % 4]
                        kb.dma("pool", wt.t[:], w[:, col:col + 128].rearrange("(kc p) n -> p kc n", p=128), [], [wt.b])
                        for tg in range(4):
                            ps = kb.psum()

                            def mm(p):
                                for kc in range(8):
                                    ins = p.matmul(ps.t[:, :], lhsT=wt.t[:, kc, :], rhs=hs[kc].t[:, tg * 512:(tg + 1) * 512], start=(kc == 0), stop=(kc == 7))
                                return ins
                            kb.op("pe", [wt.b] + [h.b for h in hs], [ps.b], mm)
                            kb.op("act", [ps.b], [ub.b], lambda a: a.activation(out=ub.t[:, 1 + tg * 512:1 + (tg + 1) * 512], in_=ps.t[:, :], func=AF.Copy))
                            kb.op("act", [ps.b, pvt.b], [ob.b], lambda a: a.activation(out=ob.t[:, tg * 512:(tg + 1) * 512], in_=ps.t[:, :], func=AF.Identity, bias=0.0, scale=pvt.t[:, cFCW(1, ci):cFCW(1, ci) + 1]))
                    g_ = cg[i % 2]; v_ = cv[i % 2]
                    self.conv3(ug[i % 2], g_, pvt, lambda k: cFCW(k, i), center_done=True)
                    kb.op("act", [g_.b], [g_.b], lambda a: a.activation(out=g_.t[:, :], in_=g_.t[:, :], func=AF.Gelu_apprx_tanh))
                    self.conv3(uv[i % 2], v_, pvt, lambda k: cFCW(k, 22 + i), center_done=True)
                    kb.op("dve", [g_.b, v_.b], [tT[i].b], lambda v: v.tensor_tensor(out=tT[i].t[:, :], in0=g_.t[:, :], in1=v_.t[:, :], op=ALU.mult))
            self.proj_norm_resid(tT, I["ffn_down"][l], DFF, pvt, cG(5), cgw=256, final=final, nxt=(None if final else (self.pv[l + 1], cG(0))), nxb=2)

    def run(self, stop_after=None):
        kb = self.kb
        self.xs = self.nc.dram_tensor("xscr", [D, L], F32, kind=("ExternalOutput" if self.dbg else "Internal")).ap()
        self.xcur = self.I["xT"]
        n = 0
        for l in range(NL):
            for f in (self.phase_proj, self.phase_filters, self.phase_hyena, self.phase_na, self.phase_sc, self.phase_merge, self.phase_xattn):
                if stop_after is not None and n >= stop_after: break
                f(l); n += 1
            if stop_after is not None and n >= stop_after: break
            self.phase_ffn(l, final=(l == NL - 1)); n += 1
        kb.barrier()


_CONST = {}


def _constants():
    if _CONST: return _CONST
    H = L // 2
    th = np.pi * (np.arange(H, dtype=np.float64) + 0.5) / L
    ang = np.outer(2.0 * th, np.arange(H, dtype=np.float64))
    C = np.cos(ang); Sn = np.sin(ang)
    def fwd(M):
        return np.ascontiguousarray(M.reshape(8, 128, 8, 128).transpose(0, 3, 2, 1)).astype(BF)
    def inv(M):
        return np.ascontiguousarray(M.reshape(8, 128, 2, 512).transpose(2, 1, 0, 3)).astype(BF)
    _CONST["CTb"] = fwd(C); _CONST["STb"] = fwd(Sn); _CONST["Cm"] = inv(C); _CONST["Sm"] = inv(Sn)
    tw = np.zeros((128, 16), np.float32)
    tw[:, 0:8] = np.cos(th).reshape(8, 128).T; tw[:, 8:16] = np.sin(th).reshape(8, 128).T
    _CONST["tw"] = tw
    f32 = np.float32
    t = np.linspace(0.0, 1.0, L, dtype=f32)[:, None]
    bands = 16
    w = (2.0 * np.pi * np.arange(L, dtype=f32)[:, None] / L).astype(f32)
    f = np.linspace(1e-4, bands - 1, bands, dtype=f32)[None, :]
    z = np.concatenate([t, np.cos(f * w), -np.sin(f * w)], axis=-1).astype(f32)
    _CONST["zT"] = np.ascontiguousarray(z.T)
    deltas = np.abs(np.linspace(np.log(1e-2) / 1.5, np.log(1e-2) / 0.3, 512, dtype=f32))
    dec = np.exp(-t * deltas[None, :]).astype(f32)
    decb = dec.copy(); decb[0, :] = 0.0
    eo = lambda a: np.ascontiguousarray(np.concatenate([a[0::2], a[1::2]], axis=0) * f32(1.0 / L))
    _CONST["decay"] = eo(dec)
    _CONST["decayb"] = eo(decb)
    _CONST["identf"] = np.eye(128, dtype=f32); _CONST["identb"] = np.eye(128).astype(BF)
    p = np.arange(128)[:, None, None]; sl = np.arange(NS)[None, :, None]; q = np.arange(64)[None, None, :]
    krl = p // 64; c = p % 64
    dr = 14 - (sl - krl - 4)
    cs = np.clip(q - 8, 0, 48)
    val = (dr >= 0) & (dr <= 14) & (c >= cs) & (c < cs + 16)
    _CONST["na_idx"] = (np.broadcast_to(np.clip(dr, 0, 14), (128, NS, 64)).copy(), np.broadcast_to(np.clip(c - q + 15, 0, 30), (128, NS, 64)).copy(), np.broadcast_to(val, (128, NS, 64)).copy())
    rmk = np.zeros((2, 20, 512), f32)
    for g in (0, 1, 3):
        for ji, j in enumerate(NA_GROUP_CHUNKS[g]):
            ti = NA_TILE_BASE[g] + ji
            for k_ in range(2):
                kr = 2 * j + k_
                r = 8 * g + np.arange(512) // 64
                rs = np.clip(r - 4, 0, 24)
                rmk[k_, ti] = np.where((kr >= rs) & (kr < rs + 8), 0.0, MASKV)
    rmp = np.zeros((128, 20 * 512), f32); rmp[0:2] = rmk.reshape(2, 20 * 512)
    _CONST["narm"] = rmp.astype(BF)
    sel = np.zeros((128, 128), f32); sel[0, :64] = 1; sel[1, 64:] = 1
    _CONST["nasel"] = sel.astype(BF)
    return _CONST


def _fm(v):
    return np.ascontiguousarray(np.asarray(v, np.float32).reshape(-1, 128).T)


def _prep(inputs):
    C = _constants()
    I = {k: np.asarray(v) for k, v in inputs.items()}
    pv = np.zeros((NL, 128, NV), np.float32)
    for l in range(NL):
        for i in range(6): pv[l, :, cG(i):cG(i) + 8] = _fm(I["norm_gains"][l, i])
        pv[l, :, cMN:cMN + 8] = _fm(I["mem_norm"][l])
        for b in range(3): pv[l, :, cGB + 8 * b:cGB + 8 * b + 8] = _fm(I["gate_bias"][l, b])
        for k in range(3):
            pv[l, :, cHSW(k, 0):cHSW(k, 0) + 12] = _fm(I["hy_short_w"][l, k])
            pv[l, :, cSCW(k, 0):cSCW(k, 0) + 4] = _fm(I["sc_conv_w"][l, k])
            pv[l, :, cFCW(k, 0):cFCW(k, 0) + 44] = _fm(I["ffn_conv"][l, k])
        for o in range(2): pv[l, :, cHB(o, 0):cHB(o, 0) + 4] = _fm(I["hy_bias"][l, o])
        pv[l, 0:64, cB1] = I["hy_b1"][l]; pv[l, 0:64, cB2] = I["hy_b2"][l]
        pv[l, 0:64, cF0] = I["hy_freq"][l, 0]; pv[l, 0:64, cF1] = I["hy_freq"][l, 1]
    idr, idc, val = C["na_idx"]
    rpb = I["na_rpb"].astype(np.float32)
    tab = np.where(val[None, None], rpb[:, :, idr, idc], np.float32(MASKV)).astype(np.float32)
    tab = np.ascontiguousarray(tab.transpose(0, 2, 1, 3, 4)).reshape(NL, 128, 8 * NS * 64)
    shared = {"pv": pv, "natab": tab}
    for k in ("w_in", "hy_w1", "hy_w2", "hy_w3", "w_branch", "w_out", "xa_wq", "xa_wkv", "xa_wo", "ffn_up", "ffn_down"):
        shared[k] = np.ascontiguousarray(I[k], dtype=np.float32)
    for k in ("CTb", "STb", "Cm", "Sm", "tw", "zT", "decay", "decayb", "identf", "identb", "narm", "nasel"):
        shared[k] = C[k]
    maps = []
    for b in range(8):
        m = dict(shared)
        m["xT"] = np.ascontiguousarray(I["x"][b].T.astype(np.float32))
        m["memT"] = np.ascontiguousarray(I["mem"][b].T.astype(np.float32))
        maps.append(m)
    return maps


def build(dbg=False, stop_after=None):
    nc = bass.Bass("TRN2", target_bir_lowering=False)
    st = ExitStack()
    kb = KB(nc, st)
    m = Model(nc, kb, st, dbg=dbg)
    m.run(stop_after)
    st.close()
    return nc


def kernel(**inputs):
    maps = _prep(inputs)
    nc = build()
    res = run_bass_kernel_spmd(nc, maps, core_ids=list(range(8)))
    out = np.stack([np.asarray(r["outT"], np.float32).T for r in res.results], axis=0)
    return np.ascontiguousarray(out)
```

```python
import numpy as np, ml_dtypes
import concourse.bass as bass
import concourse.mybir as mybir
from concourse.bass_utils import run_bass_kernel_spmd
from contextlib import ExitStack, contextmanager

F32 = mybir.dt.float32; BF16 = mybir.dt.bfloat16
ALU = mybir.AluOpType; AF = mybir.ActivationFunctionType
BF = ml_dtypes.bfloat16

L = 2048; D = 1024; NM = 256; DFF = 2816; PW = 7680; NL = 2
EPS = 1e-6
NV = 272
cG = lambda i: 8 * i
cMN = 48
cGB = 56
cHSW = lambda k, cc: 80 + 12 * k + cc
cHB = lambda o, cc: 116 + 4 * o + cc
cSCW = lambda k, cc: 124 + 4 * k + cc
cFCW = lambda k, cc: 136 + 44 * k + cc
cB1, cB2, cF0, cF1 = 268, 269, 270, 271
NA_GROUP_CHUNKS = [list(range(0, 6)), list(range(2, 10)), list(range(6, 14)), list(range(10, 16))]
NA_TILE_BASE = [0, 6, 6, 14]
MASKV = -30000.0
NS = 23
SB_LO = 16640; SB_HI = 221184


class Buf:
    __slots__ = ("w", "r")

    def __init__(self):
        self.w = None; self.r = {}


class KB:
    def __init__(self, nc, stack, n_dma_sems=40, same_engine_sync=True):
        self.nc = nc; self.ses = same_engine_sync
        self.eng = {"pe": nc.tensor, "dve": nc.vector, "act": nc.scalar, "pool": nc.gpsimd, "sp": nc.sync}
        self.semh = {}; self.cnt = {}
        for e in ("pe", "dve", "act", "pool"):
            self.semh[e] = stack.enter_context(nc.semaphore("s_" + e)); self.cnt[e] = 0
        self.nd = n_dma_sems
        for i in range(n_dma_sems):
            self.semh[("d", i)] = stack.enter_context(nc.semaphore(f"d{i}")); self.cnt[("d", i)] = 0
        self.dnext = 0
        self.known = {e: {} for e in self.eng}
        self.uid = 0
        self.dbufs = {}
        self.psums = []; self.psn = 0; self.accn = 0; self.rot = 6
        self.lo = SB_LO; self.hi = SB_HI; self.side = 0; self.ghosts = []; self.peak = 0

    def tile(self, stack, shape, dt):
        self.uid += 1
        nb = 1
        for d in shape[1:]: nb *= d
        nb *= (4 if dt == F32 else 2)
        nb = (nb + 63) // 64 * 64
        side = self.side
        if side == 0:
            off = self.lo; self.lo += nb
        else:
            self.hi -= nb; off = self.hi
        assert self.lo <= self.hi, f"SBUF overflow lo={self.lo} hi={self.hi}"
        self.peak = max(self.peak, (self.lo - SB_LO) + (SB_HI - self.hi))
        tl = Tl(self.nc.alloc_sbuf_tensor_at(f"t{self.uid}", list(shape), dt, offset=off))
        s0, e0 = off, off + nb
        newg = []
        for (gs, ge, ev) in self.ghosts:
            if ge <= s0 or gs >= e0:
                newg.append((gs, ge, ev)); continue
            for k, v in ev.items():
                if tl.b.r.get(k, 0) < v: tl.b.r[k] = v
            if gs < s0: newg.append((gs, s0, ev))
            if ge > e0: newg.append((e0, ge, ev))
        self.ghosts = newg
        stack.callback(self._free, tl, off, nb, side)
        return tl

    def _free(self, tl, off, nb, side):
        ev = dict(tl.b.r)
        if tl.b.w is not None and ev.get(tl.b.w[0], 0) < tl.b.w[1]: ev[tl.b.w[0]] = tl.b.w[1]
        self.ghosts.append((off, off + nb, ev))
        if side == 0:
            assert self.lo == off + nb, "non-LIFO free (lo)"; self.lo = off
        else:
            assert self.hi == off, "non-LIFO free (hi)"; self.hi = off + nb

    def dbuf(self, *key):
        b = self.dbufs.get(key)
        if b is None:
            b = self.dbufs[key] = Buf()
        return b

    def psum(self):
        t = self.psums[self.psn % self.rot]; self.psn += 1
        return t

    def acc(self):
        t = self.psums[6 + self.accn % 2]; self.accn += 1
        return t

    def _wait(self, e, k, v):
        kn = self.known[e]
        if kn.get(k, 0) >= v: return
        self.eng[e].wait_ge(self.semh[k], v); kn[k] = v

    def _waits(self, e, reads, writes):
        need = {}
        for b in reads:
            if b.w is not None and need.get(b.w[0], 0) < b.w[1]: need[b.w[0]] = b.w[1]
        for b in writes:
            if b.w is not None and need.get(b.w[0], 0) < b.w[1]: need[b.w[0]] = b.w[1]
            for k, v in b.r.items():
                if need.get(k, 0) < v: need[k] = v
        for k, v in need.items():
            if k == e and (not self.ses or e == "pe"): continue
            self._wait(e, k, v)

    def _record(self, ev, reads, writes):
        k, v = ev
        for b in reads:
            if b.r.get(k, 0) < v: b.r[k] = v
        for b in writes:
            b.w = ev; b.r = {}

    def op(self, e, reads, writes, fn):
        self._waits(e, reads, writes)
        ins = fn(self.eng[e])
        self.cnt[e] += 1
        ins.then_inc(self.semh[e], 1)
        self._record((e, self.cnt[e]), reads, writes)

    def dma(self, q, out, in_, reads, writes):
        i = self.dnext; self.dnext = (i + 1) % self.nd
        key = ("d", i)
        if self.cnt[key] > 0: self._wait(q, key, self.cnt[key])
        self._waits(q, reads, writes)
        self.cnt[key] += 16
        self.eng[q].dma_start(out=out, in_=in_).then_inc(self.semh[key], 16)
        self._record((key, self.cnt[key]), reads, writes)

    def barrier(self):
        for e in self.eng:
            for k, c in self.cnt.items():
                if c > 0 and not (k == e and e == "pe"): self._wait(e, k, c)

    @contextmanager
    def phase(self, flip=False):
        if flip: self.side = 1 - self.side
        with ExitStack() as ph:
            yield ph


class Tl:
    __slots__ = ("t", "b", "__weakref__")

    def __init__(self, t):
        self.t = t; self.b = Buf()


class Model:
    def __init__(self, nc, kb, st, dbg=False):
        self.nc = nc; self.kb = kb; self.dbg = dbg
        dt = lambda n, s, d, k=("ExternalOutput" if dbg else "Internal"): nc.dram_tensor(n, list(s), d, kind=k).ap()
        ein = lambda n, s, d=F32: dt(n, s, d, "ExternalInput")
        self.I = {
            "xT": ein("xT", [D, L]), "memT": ein("memT", [D, NM]), "pv": ein("pv", [NL, 128, NV]),
            "w_in": ein("w_in", [NL, D, PW]), "hy_w1": ein("hy_w1", [NL, 33, 64]), "hy_w2": ein("hy_w2", [NL, 64, 64]),
            "hy_w3": ein("hy_w3", [NL, 64, 2048]), "natab": ein("natab", [NL, 128, 8 * NS * 64]),
            "narm": ein("narm", [128, 20 * 512], BF16), "nasel": ein("nasel", [128, 128], BF16),
            "w_branch": ein("w_branch", [NL, 3, 512, D]), "w_out": ein("w_out", [NL, D, D]),
            "xa_wq": ein("xa_wq", [NL, D, D]), "xa_wkv": ein("xa_wkv", [NL, D, 2 * D]), "xa_wo": ein("xa_wo", [NL, D, D]),
            "ffn_up": ein("ffn_up", [NL, D, 2 * DFF]), "ffn_down": ein("ffn_down", [NL, DFF, D]),
            "CTb": ein("CTb", [8, 128, 8, 128], BF16), "STb": ein("STb", [8, 128, 8, 128], BF16),
            "Cm": ein("Cm", [2, 128, 8, 512], BF16), "Sm": ein("Sm", [2, 128, 8, 512], BF16),
            "tw": ein("tw", [128, 16]),
            "zT": ein("zT", [33, L]), "decay": ein("decay", [L, 512]), "decayb": ein("decayb", [L, 512]),
            "identf": ein("identf", [128, 128]), "identb": ein("identb", [128, 128], BF16),
        }
        self.out = dt("outT", [D, L], F32, "ExternalOutput")
        self.S = {
            "hyuT": dt("hyuT", [1536, L], F32), "qT": dt("qT", [512, L], BF16), "kT": dt("kT", [512, L], BF16),
            "vaug": dt("vaug", [16, 128, 1024], BF16), "scuT": dt("scuT", [1536, L], F32),
            "gatesT": dt("gatesT", [3072, L], BF16), "kfr": dt("kfr", [L, 1024], F32), "kfi": dt("kfi", [L, 1024], F32),
            "yaT": dt("yaT", [512, L], BF16), "ybT": dt("ybT", [512, L], BF16), "ycT": dt("ycT", [512, L], BF16),
            "mergedT": dt("mergedT", [D, L], BF16), "hsd": dt("hsd", [2, L, 1024], BF16), "hT": dt("hT", [D, L], BF16),
        }
        mk = lambda shape, d: kb.tile(st, shape, d)
        self.onesb = mk([128, 128], BF16); self.identb = mk([128, 128], BF16); self.identf = mk([128, 128], F32)
        self.onesf = mk([1, 64], F32); self.epst = mk([128, 1], F32)
        self.pv = [mk([128, NV], F32) for _ in range(NL)]
        self.tw = mk([128, 16], F32)
        kb.dma("sp", self.tw.t[:], self.I["tw"][:, :], [], [self.tw.b])
        self.nidentb = mk([128, 128], BF16)
        kb.op("dve", [], [self.onesb.b], lambda v: v.memset(self.onesb.t[:], 1.0))
        kb.op("dve", [], [self.onesf.b], lambda v: v.memset(self.onesf.t[:], 1.0))
        kb.op("dve", [], [self.epst.b], lambda v: v.memset(self.epst.t[:], EPS))
        kb.dma("sp", self.identb.t[:], self.I["identb"][:, :], [], [self.identb.b])
        kb.dma("sp", self.identf.t[:], self.I["identf"][:, :], [], [self.identf.b])
        kb.op("dve", [self.identb.b], [self.nidentb.b], lambda v: v.tensor_scalar(out=self.nidentb.t[:], in0=self.identb.t[:], scalar1=-1.0, scalar2=None, op0=ALU.mult))
        for l in range(NL):
            kb.dma("sp", self.pv[l].t[:], self.I["pv"][l], [], [self.pv[l].b])
        for i in range(8):
            kb.psums.append(Tl(st.enter_context(nc.psum_tensor(f"psum{i}", [128, 512], F32))))

    def tiles(self, ph, n, shape, d):
        return [self.kb.tile(ph, shape, d) for _ in range(n)]

    def load_fm(self, ph, src, nchunks, T, d, q="sp", deps=None):
        ts = self.tiles(ph, nchunks, [128, T], d)
        for c, t in enumerate(ts):
            self.kb.dma(q, t.t[:], src[c * 128:(c + 1) * 128, :], deps(c) if deps else [], [t.b])
        return ts

    def xdeps(self, c):
        return [self.kb.dbuf("x", c)]

    def rms_stats(self, ph, xs, T, sq_tiles):
        kb = self.kb
        rstd = kb.tile(ph, [128, T], F32)
        n = len(xs)
        for t0 in range(0, T, 512):
            w = min(512, T - t0)
            ps = kb.psum()
            for c, x in enumerate(xs):
                sq = sq_tiles[c % len(sq_tiles)]
                kb.op("act", [x.b], [sq.b], lambda a, x=x, sq=sq: a.activation(out=sq.t[:, 0:w], in_=x.t[:, t0:t0 + w], func=AF.Square))
                kb.op("pe", [sq.b, self.onesb.b], [ps.b], lambda p, sq=sq, c=c: p.matmul(ps.t[:, 0:w], lhsT=self.onesb.t[:, :], rhs=sq.t[:, 0:w], start=(c == 0), stop=(c == n - 1)))
            kb.op("act", [ps.b, self.epst.b], [rstd.b], lambda a: a.activation(out=rstd.t[:, t0:t0 + w], in_=ps.t[:, 0:w], func=AF.Ln, bias=self.epst.t[:, 0:1], scale=1.0 / (128 * n)))
        kb.op("act", [rstd.b], [rstd.b], lambda a: a.activation(out=rstd.t[:, :], in_=rstd.t[:, :], func=AF.Exp, scale=-0.5))
        return rstd

    def norm_apply(self, tmp, xs, hs, T, pvt, gcol):
        kb = self.kb
        sq = self.tiles(tmp, 3, [128, 512], BF16)
        rstd = self.rms_stats(tmp, xs, T, sq)
        for c, (x, h) in enumerate(zip(xs, hs)):
            kb.op("dve", [x.b, rstd.b, pvt.b], [h.b], lambda v, x=x, h=h, c=c: v.scalar_tensor_tensor(
                out=h.t[:, :], in0=x.t[:, :], scalar=pvt.t[:, gcol + c:gcol + c + 1], in1=rstd.t[:, :], op0=ALU.mult, op1=ALU.mult))

    def prefetch_w(self, ph, w, K, c0, ncols, cgw, nbuf):
        kb = self.kb; KC = K // 128
        wb = self.tiles(ph, nbuf, [128, KC, cgw], BF16)
        n = 0
        for g0 in list(range(0, ncols, cgw))[:nbuf]:
            gw = min(cgw, ncols - g0)
            kb.dma("pool", wb[n].t[:, :, 0:gw], w[0:K, c0 + g0:c0 + g0 + gw].rearrange("(kc p) n -> p kc n", p=128), [], [wb[n].b])
            n += 1
        return (wb, n)

    def linear_fm(self, ph, w, K, c0, ncols, srcs, T, epi, cgw=512, nbuf=3, wpre=None):
        kb = self.kb; KC = K // 128
        if wpre is not None:
            wb, npre = wpre; nbuf = len(wb)
        else:
            wb = self.tiles(ph, nbuf, [128, KC, cgw], BF16); npre = 0
        gi = 0
        for g0 in range(0, ncols, cgw):
            gw = min(cgw, ncols - g0)
            wt = wb[gi % nbuf]; gi += 1
            if gi > npre:
                kb.dma("pool", wt.t[:, :, 0:gw], w[0:K, c0 + g0:c0 + g0 + gw].rearrange("(kc p) n -> p kc n", p=128), [], [wt.b])
            for o0 in range(0, gw, 128):
                oc = (g0 + o0) // 128
                for t0 in range(0, T, 512):
                    tw = min(512, T - t0)
                    ps = kb.psum()

                    def mm(p, wt=wt, o0=o0, t0=t0, tw=tw, ps=ps):
                        for kc in range(KC):
                            ins = p.matmul(ps.t[:, 0:tw], lhsT=wt.t[:, kc, o0:o0 + 128], rhs=srcs[kc].t[:, t0:t0 + tw], start=(kc == 0), stop=(kc == KC - 1))
                        return ins
                    kb.op("pe", [wt.b] + [s.b for s in srcs], [ps.b], mm)
                    epi(oc, t0, tw, ps)

    def linear_tm(self, ph, w, K, c0, ncols, srcs, T, epi, nbuf=2):
        kb = self.kb; KC = K // 128
        wb = self.tiles(ph, nbuf, [128, KC, 512], BF16)
        for gi, g0 in enumerate(range(0, ncols, 512)):
            gw = min(512, ncols - g0)
            wt = wb[gi % nbuf]
            kb.dma("pool", wt.t[:, :, 0:gw], w[0:K, c0 + g0:c0 + g0 + gw].rearrange("(kc p) n -> p kc n", p=128), [], [wt.b])
            for tc in range(T // 128):
                ps = kb.psum()

                def mm(p, wt=wt, tc=tc, gw=gw, ps=ps):
                    for kc in range(KC):
                        ins = p.matmul(ps.t[:, 0:gw], lhsT=srcs[kc].t[:, tc * 128:(tc + 1) * 128], rhs=wt.t[:, kc, 0:gw], start=(kc == 0), stop=(kc == KC - 1))
                    return ins
                kb.op("pe", [wt.b] + [s.b for s in srcs], [ps.b], mm)
                epi(tc, gi, gw, ps)

    def conv3(self, ub, ob, pvt, col_fn, T=L, center_done=False):
        kb = self.kb
        if not center_done:
            kb.op("dve", [ub.b, pvt.b], [ob.b], lambda v: v.tensor_scalar(out=ob.t[:, 0:T], in0=ub.t[:, 1:1 + T], scalar1=pvt.t[:, col_fn(1):col_fn(1) + 1], scalar2=None, op0=ALU.mult))
        for k in (0, 2):
            kb.op("dve", [ub.b, pvt.b, ob.b], [ob.b], lambda v, k=k: v.scalar_tensor_tensor(
                out=ob.t[:, 0:T], in0=ub.t[:, k:k + T], scalar=pvt.t[:, col_fn(k):col_fn(k) + 1], in1=ob.t[:, 0:T], op0=ALU.mult, op1=ALU.add))

    def padded(self, ph, n, T=L):
        kb = self.kb
        ts = self.tiles(ph, n, [128, T + 2], F32)
        for t in ts:
            kb.op("dve", [], [t.b], lambda v, t=t: v.memset(t.t[:, :], 0.0))
        return ts

    def proj_norm_resid(self, srcs, w, K, pvt, gcol, cgw=512, final=False, nxt=None, nxb=4, wpre=None):
        kb = self.kb
        with kb.phase() as ph:
            o = self.tiles(ph, 8, [128, L], F32)
            sq = self.tiles(ph, 4, [128, 512], BF16)
            stat = kb.psums[4:8]
            xb = self.tiles(ph, nxb, [128, L], F32)
            if nxb == 8:
                for c in range(8):
                    kb.dma("sp", xb[c].t[:], self.xcur[c * 128:(c + 1) * 128, :], [kb.dbuf("x", c)], [xb[c].b])
            old_rot = kb.rot; kb.rot = 4
            pend = []; nsq = [0]

            def flush(keep):
                while len(pend) > keep:
                    s_, tg, first, last = pend.pop(0)
                    kb.op("pe", [s_.b, self.onesb.b], [stat[tg].b], lambda p: p.matmul(stat[tg].t[:, :], lhsT=self.onesb.t[:, :], rhs=s_.t[:, :], start=first, stop=last))

            def epi(oc, t0, tw, ps):
                tg = t0 // 512
                kb.op("act", [ps.b], [o[oc].b], lambda a: a.activation(out=o[oc].t[:, t0:t0 + tw], in_=ps.t[:, 0:tw], func=AF.Copy))
                s_ = sq[nsq[0] % 4]; nsq[0] += 1
                kb.op("act", [ps.b], [s_.b], lambda a: a.activation(out=s_.t[:, :], in_=ps.t[:, 0:tw], func=AF.Square))
                pend.append((s_, tg, oc == 0, oc == 7))
                flush(2)
            with kb.phase() as ph2:
                self.linear_fm(ph2, w, K, 0, D, srcs, L, epi, cgw=cgw, nbuf=2, wpre=wpre)
            flush(0)
            rstd = kb.tile(ph, [128, L], F32)
            for tg in range(4):
                kb.op("act", [stat[tg].b, self.epst.b], [rstd.b], lambda a: a.activation(out=rstd.t[:, tg * 512:(tg + 1) * 512], in_=stat[tg].t[:, :], func=AF.Ln, bias=self.epst.t[:, 0:1], scale=1.0 / D))
            kb.op("act", [rstd.b], [rstd.b], lambda a: a.activation(out=rstd.t[:, :], in_=rstd.t[:, :], func=AF.Exp, scale=-0.5))
            dst = self.out if final else self.xs
            for c in range(8):
                x = xb[c % nxb]
                if nxb != 8:
                    kb.dma("sp", x.t[:], self.xcur[c * 128:(c + 1) * 128, :], [kb.dbuf("x", c)], [x.b])
                kb.op("dve", [o[c].b, rstd.b], [o[c].b], lambda v: v.tensor_tensor(out=o[c].t[:, :], in0=o[c].t[:, :], in1=rstd.t[:, :], op=ALU.mult))
                kb.op("dve", [o[c].b, x.b, pvt.b], [o[c].b], lambda v: v.scalar_tensor_tensor(
                    out=o[c].t[:, :], in0=o[c].t[:, :], scalar=pvt.t[:, gcol + c:gcol + c + 1], in1=x.t[:, :], op0=ALU.mult, op1=ALU.add))
                kb.dma("sp" if nxb == 8 else "pool", dst[c * 128:(c + 1) * 128, :], o[c].t[:], [o[c].b], [kb.dbuf("xo" if final else "x", c)])
                if nxt is not None:
                    for tg in range(4):
                        s_ = sq[nsq[0] % 4]; nsq[0] += 1
                        kb.op("act", [o[c].b], [s_.b], lambda a: a.activation(out=s_.t[:, :], in_=o[c].t[:, tg * 512:(tg + 1) * 512], func=AF.Square))
                        pend.append((s_, tg, c == 0, c == 7))
                    flush(0)
            self.xcur = self.xs
            if nxt is not None:
                flush(0)
                pvn, gn = nxt
                for tg in range(4):
                    kb.op("act", [stat[tg].b, self.epst.b], [rstd.b], lambda a: a.activation(out=rstd.t[:, tg * 512:(tg + 1) * 512], in_=stat[tg].t[:, :], func=AF.Ln, bias=self.epst.t[:, 0:1], scale=1.0 / D))
                kb.op("act", [rstd.b], [rstd.b], lambda a: a.activation(out=rstd.t[:, :], in_=rstd.t[:, :], func=AF.Exp, scale=-0.5))
                hst = self.tiles(ph, 2, [128, L], BF16)
                for c in range(8):
                    h = hst[c % 2]
                    kb.op("dve", [o[c].b, rstd.b, pvn.b], [h.b], lambda v: v.scalar_tensor_tensor(
                        out=h.t[:, :], in0=o[c].t[:, :], scalar=pvn.t[:, gn + c:gn + c + 1], in1=rstd.t[:, :], op0=ALU.mult, op1=ALU.mult))
                    kb.dma("sp", self.S["hT"][c * 128:(c + 1) * 128, :], h.t[:], [h.b], [kb.dbuf("h", c)])
            kb.rot = old_rot

    def proj_norm_resid_tg(self, srcs, w, pvt, gcol, nxt, wpre):
        kb = self.kb
        wb, npre = wpre
        pvn, gn = nxt
        with kb.phase() as ph:
            o = [self.tiles(ph, 4, [128, 512], F32) for _ in range(8)]
            sq = self.tiles(ph, 8, [128, 512], BF16)
            xb = self.tiles(ph, 8, [128, L], F32)
            for c in range(8):
                kb.dma("sp", xb[c].t[:], self.xcur[c * 128:(c + 1) * 128, :], [kb.dbuf("x", c)], [xb[c].b])
            pe_pend = []; tick = [0]

            def pe_flush(min_age):
                while pe_pend and tick[0] - pe_pend[0][0] >= min_age:
                    pe_pend.pop(0)[1]()
            rs1 = self.tiles(ph, 4, [128, 512], F32); rs2 = rs1
            hst = self.tiles(ph, 4, [128, 512], BF16)
            stat = kb.psums[4:8]
            old_rot = kb.rot; kb.rot = 4
            nsq = [0]; nh = [0]

            def rstd_from(st_, dstt):
                kb.op("act", [st_.b, self.epst.b], [dstt.b], lambda a: a.activation(out=dstt.t[:, :], in_=st_.t[:, :], func=AF.Ln, bias=self.epst.t[:, 0:1], scale=1.0 / D))
                kb.op("act", [dstt.b], [dstt.b], lambda a: a.activation(out=dstt.t[:, :], in_=dstt.t[:, :], func=AF.Exp, scale=-0.5))

            def post(tg):
                tsl = slice(tg * 512, (tg + 1) * 512)
                rstd_from(stat[tg], rs1[tg])
                yield
                for c in range(8):
                    oc_ = o[c][tg]
                    kb.op("dve", [oc_.b, rs1[tg].b], [oc_.b], lambda v: v.tensor_tensor(out=oc_.t[:, :], in0=oc_.t[:, :], in1=rs1[tg].t[:, :], op=ALU.mult))
                    kb.op("dve", [oc_.b, xb[c].b, pvt.b], [oc_.b], lambda v: v.scalar_tensor_tensor(
                        out=oc_.t[:, :], in0=oc_.t[:, :], scalar=pvt.t[:, gcol + c:gcol + c + 1], in1=xb[c].t[:, tsl], op0=ALU.mult, op1=ALU.add))
                    kb.dma("sp", self.xs[c * 128:(c + 1) * 128, tsl], oc_.t[:, :], [oc_.b], [kb.dbuf("x", c)])
                    s_ = sq[nsq[0] % 8]; nsq[0] += 1
                    kb.op("act", [oc_.b], [s_.b], lambda a: a.activation(out=s_.t[:, :], in_=oc_.t[:, :], func=AF.Square))
                    pe_pend.append((tick[0], lambda s_=s_, c=c: kb.op("pe", [s_.b, self.onesb.b], [stat[tg].b], lambda p: p.matmul(stat[tg].t[:, :], lhsT=self.onesb.t[:, :], rhs=s_.t[:, :], start=(c == 0), stop=(c == 7)))))
                    while len(pe_pend) > 4: pe_pend.pop(0)[1]()
                    yield
                pe_flush(0)
                rstd_from(stat[tg], rs2[tg])
                yield
                for c in range(8):
                    oc_ = o[c][tg]; h = hst[nh[0] % 4]; nh[0] += 1
                    kb.op("dve", [oc_.b, rs2[tg].b, pvn.b], [h.b], lambda v: v.scalar_tensor_tensor(
                        out=h.t[:, :], in0=oc_.t[:, :], scalar=pvn.t[:, gn + c:gn + c + 1], in1=rs2[tg].t[:, :], op0=ALU.mult, op1=ALU.mult))
                    kb.dma("sp", self.S["hT"][c * 128:(c + 1) * 128, tsl], h.t[:, :], [h.b], [kb.dbuf("h", c)])
                    yield

            gens = []

            def step_all(k=1):
                for _ in range(k):
                    for g_ in list(gens):
                        try:
                            next(g_)
                        except StopIteration:
                            gens.remove(g_)
            for tg in range(4):
                pend = []
                for oc in range(8):
                    wt = wb[oc // 4]; o0 = (oc % 4) * 128
                    ps = kb.psum()

                    def mm(p):
                        for kc in range(8):
                            ins = p.matmul(ps.t[:, :], lhsT=wt.t[:, kc, o0:o0 + 128], rhs=srcs[kc].t[:, tg * 512:(tg + 1) * 512], start=(kc == 0), stop=(kc == 7))
                        return ins
                    kb.op("pe", [wt.b] + [s_.b for s_ in srcs], [ps.b], mm)
                    ot = o[oc][tg]
                    kb.op("act", [ps.b], [ot.b], lambda a: a.activation(out=ot.t[:, :], in_=ps.t[:, :], func=AF.Copy))
                    s_ = sq[nsq[0] % 8]; nsq[0] += 1
                    kb.op("act", [ps.b], [s_.b], lambda a: a.activation(out=s_.t[:, :], in_=ps.t[:, :], func=AF.Square))
                    pend.append((s_, oc))
                    tick[0] += 1
                    pe_flush(2)
                    if len(pend) > 1:
                        s2, oc2 = pend.pop(0)
                        kb.op("pe", [s2.b, self.onesb.b], [stat[tg].b], lambda p: p.matmul(stat[tg].t[:, :], lhsT=self.onesb.t[:, :], rhs=s2.t[:, :], start=(oc2 == 0), stop=(oc2 == 7)))
                    step_all(1 + (oc % 2))
                s2, oc2 = pend.pop(0)
                kb.op("pe", [s2.b, self.onesb.b], [stat[tg].b], lambda p: p.matmul(stat[tg].t[:, :], lhsT=self.onesb.t[:, :], rhs=s2.t[:, :], start=(oc2 == 0), stop=(oc2 == 7)))
                gens.append(post(tg))
            while gens:
                step_all(); tick[0] += 1; pe_flush(1)
            pe_flush(0)
            kb.rot = old_rot
        self.xcur = self.xs

    def load_h(self, ph):
        return self.load_fm(ph, self.S["hT"], 8, L, BF16, deps=lambda c: [self.kb.dbuf("h", c)])

    def phase_proj(self, l):
        kb = self.kb; S = self.S; pvt = self.pv[l]; w = self.I["w_in"][l]
        with kb.phase(flip=True) as ph:
            if l == 0:
                hs = self.tiles(ph, 8, [128, L], BF16)
                with kb.phase() as phx:
                    xs = self.load_fm(phx, self.xcur, 8, L, F32, deps=self.xdeps)
                    self.norm_apply(phx, xs, hs, L, pvt, cG(0))
            else:
                hs = self.load_h(ph)
            ub = self.padded(ph, 2); ob = self.tiles(ph, 2, [128, L], F32)
            stf = self.tiles(ph, 4, [128, 512], F32); stb = self.tiles(ph, 4, [128, 512], BF16)
            cnt = {"f": 0, "b": 0, "e": 0}

            gst = ExitStack()
            self.bg_gen = self.filters_gen(l, gst)

            def epi(oc, t0, tw, ps):
                tg = t0 // 512
                cnt["e"] += 1
                if cnt["e"] > 48 and cnt["e"] % 4 == 0: self.bg()
                if oc < 12:
                    u = ub[oc % 2]; o = ob[oc % 2]
                    kb.op("act", [ps.b], [u.b], lambda a: a.activation(out=u.t[:, 1 + t0:1 + t0 + tw], in_=ps.t[:, 0:tw], func=AF.Copy))
                    kb.op("act", [ps.b, pvt.b], [o.b], lambda a: a.activation(out=o.t[:, t0:t0 + tw], in_=ps.t[:, 0:tw], func=AF.Identity, bias=0.0, scale=pvt.t[:, cHSW(1, oc):cHSW(1, oc) + 1]))
                    if tg == 3:
                        self.conv3(u, o, pvt, lambda k: cHSW(k, oc), center_done=True)
                        kb.dma("sp", S["hyuT"][oc * 128:(oc + 1) * 128, :], o.t[:], [o.b], [kb.dbuf("hyu", oc)])
                elif oc < 20:
                    s = stb[cnt["b"] % 4]; cnt["b"] += 1
                    sc = 0.125 if oc < 16 else 1.0
                    kb.op("act", [ps.b], [s.b], lambda a: a.activation(out=s.t[:, 0:tw], in_=ps.t[:, 0:tw], func=AF.Identity, bias=0.0, scale=sc))
                    dst = S["qT"] if oc < 16 else S["kT"]; r = (oc - 12) % 4
                    kb.dma("act", dst[r * 128:(r + 1) * 128, t0:t0 + tw], s.t[:, 0:tw], [s.b], [kb.dbuf("qk", oc, tg)])
                elif oc < 24:
                    raise AssertionError
                elif oc < 36:
                    s = stf[cnt["f"] % 4]; cnt["f"] += 1
                    kb.op("act", [ps.b], [s.b], lambda a: a.activation(out=s.t[:, 0:tw], in_=ps.t[:, 0:tw], func=AF.Copy))
                    r = oc - 24
                    kb.dma("act", S["scuT"][r * 128:(r + 1) * 128, t0:t0 + tw], s.t[:, 0:tw], [s.b], [kb.dbuf("scu", r, tg)])
                else:
                    s = stb[cnt["b"] % 4]; cnt["b"] += 1
                    r = oc - 36
                    kb.op("act", [ps.b, pvt.b], [s.b], lambda a: a.activation(out=s.t[:, 0:tw], in_=ps.t[:, 0:tw], func=AF.Sigmoid, bias=pvt.t[:, cGB + r:cGB + r + 1], scale=1.0))
                    kb.dma("act", S["gatesT"][r * 128:(r + 1) * 128, t0:t0 + tw], s.t[:, 0:tw], [s.b], [kb.dbuf("gates", r, tg)])
            with kb.phase() as ph2:
                self.linear_fm(ph2, w, D, 0, 2560, hs, L, epi)
            with kb.phase() as ph2:
                self.linear_fm(ph2, w, D, 3072, PW - 3072, hs, L, lambda oc, t0, tw, ps: epi(oc + 24, t0, tw, ps))
            with kb.phase() as ph2:
                va = self.tiles(ph2, 3, [128, 8, 128], BF16)
                for t in va:
                    kb.op("dve", [], [t.b], lambda v, t=t: v.memset(t.t[:, :, :], 1.0))

                def epiv(tc, gi, gw, ps):
                    t = va[tc % 3]
                    kb.op("act", [ps.b], [t.b], lambda a: a.activation(out=t.t[:, :, 0:64], in_=ps.t[:, :].rearrange("p (h d) -> p h d", h=8), func=AF.Copy))
                    kb.dma("act", S["vaug"][tc].rearrange("p (h d) -> p h d", h=8), t.t[:, :, :], [t.b], [kb.dbuf("vaug", tc)])
                self.linear_tm(ph2, w, D, 2560, 512, hs, L, epiv)
            while self.bg_gen is not None: self.bg()
            kb.side ^= 1
            gst.close()
            kb.side ^= 1

    def sin_act(self, a, m, ps, w, fcol, bcol, pvt, dst, t0):
        kb = self.kb
        kb.op("dve", [ps.b, pvt.b], [a.b], lambda v: v.tensor_scalar(out=a.t[:, 0:w], in0=ps.t[0:64, 0:w], scalar1=pvt.t[0:64, bcol:bcol + 1], scalar2=pvt.t[0:64, fcol:fcol + 1], op0=ALU.add, op1=ALU.mult))
        for it in range(2):
            kb.op("dve", [a.b], [m.b], lambda v: v.tensor_scalar(out=m.t[:, 0:w], in0=a.t[:, 0:w], scalar1=float(np.pi), scalar2=float(-2 * np.pi), op0=ALU.is_gt, op1=ALU.mult))
            kb.op("dve", [a.b, m.b], [a.b], lambda v: v.tensor_tensor(out=a.t[:, 0:w], in0=a.t[:, 0:w], in1=m.t[:, 0:w], op=ALU.add))
            kb.op("dve", [a.b], [m.b], lambda v: v.tensor_scalar(out=m.t[:, 0:w], in0=a.t[:, 0:w], scalar1=float(-np.pi), scalar2=float(2 * np.pi), op0=ALU.is_lt, op1=ALU.mult))
            kb.op("dve", [a.b, m.b], [a.b], lambda v: v.tensor_tensor(out=a.t[:, 0:w], in0=a.t[:, 0:w], in1=m.t[:, 0:w], op=ALU.add))
        kb.op("act", [a.b], [dst.b], lambda c: c.activation(out=dst.t[0:64, t0:t0 + w], in_=a.t[:, 0:w], func=AF.Sin))

    def filters_gen(self, l, p1):
        kb = self.kb; S = self.S; I = self.I; pvt = self.pv[l]
        kb.side ^= 1
        zT = kb.tile(p1, [33, L], F32); w1 = kb.tile(p1, [33, 64], F32); w2 = kb.tile(p1, [64, 64], F32)
        w3 = kb.tile(p1, [64, 2048], BF16)
        h1 = kb.tile(p1, [64, L], F32); h2 = kb.tile(p1, [64, L], F32); h2b = kb.tile(p1, [64, L], BF16)
        sa = kb.tile(p1, [64, 512], F32); sm = kb.tile(p1, [64, 512], F32)
        dec = self.tiles(p1, 2, [128, 512], F32); decb = self.tiles(p1, 2, [128, 512], F32)
        tf = self.tiles(p1, 2, [128, 512], F32); tb = self.tiles(p1, 2, [128, 512], F32)
        hso = self.tiles(p1, 2, [128, 1024], BF16); hdo = self.tiles(p1, 2, [128, 1024], BF16)
        kb.side ^= 1
        kb.dma("sp", zT.t[:], I["zT"][:, :], [], [zT.b]); kb.dma("sp", w1.t[:], I["hy_w1"][l], [], [w1.b])
        kb.dma("sp", w2.t[:], I["hy_w2"][l], [], [w2.b]); kb.dma("pool", w3.t[:], I["hy_w3"][l], [], [w3.b])
        yield
        for src, wt, dst, fc, bc in ((zT, w1, h1, cF0, cB1), (h1, w2, h2, cF1, cB2)):
            kk = 33 if src is zT else 64
            for t0 in range(0, L, 512):
                ps = kb.psum()
                kb.op("pe", [src.b, wt.b], [ps.b], lambda p: p.matmul(ps.t[0:64, :], lhsT=wt.t[0:kk, :], rhs=src.t[0:kk, t0:t0 + 512], start=True, stop=True))
                self.sin_act(sa, sm, ps, 512, fc, bc, pvt, dst, t0)
                yield
        kb.op("act", [h2.b], [h2b.b], lambda a: a.activation(out=h2b.t[:, :], in_=h2.t[:, :], func=AF.Copy))
        yield
        for jc in range(16):
            d = dec[jc % 2]; db = decb[jc % 2]; hs_ = hso[jc % 2]; hd_ = hdo[jc % 2]
            kb.dma("sp", d.t[:], I["decay"][jc * 128:(jc + 1) * 128, :], [], [d.b])
            kb.dma("sp", db.t[:], I["decayb"][jc * 128:(jc + 1) * 128, :], [], [db.b])
            for o in range(2):
                psf = kb.psum(); psb = kb.psum()
                par = jc // 8; mc = jc % 8
                taps = h2b.t[0:64, mc * 256 + par:mc * 256 + 256:2]
                kb.op("pe", [h2b.b, w3.b], [psf.b], lambda p: p.matmul(psf.t[:, :], lhsT=taps, rhs=w3.t[0:64, o * 512:(o + 1) * 512], start=True, stop=True))
                kb.op("pe", [h2b.b, w3.b], [psb.b], lambda p: p.matmul(psb.t[:, :], lhsT=taps, rhs=w3.t[0:64, 1024 + o * 512:1024 + (o + 1) * 512], start=True, stop=True))
                f = tf[o]; b = tb[o]
                kb.op("dve", [psf.b, d.b], [f.b], lambda v: v.tensor_tensor(out=f.t[:, :], in0=psf.t[:, :], in1=d.t[:, :], op=ALU.mult))
                kb.op("dve", [psb.b, db.b], [b.b], lambda v: v.tensor_tensor(out=b.t[:, :], in0=psb.t[:, :], in1=db.t[:, :], op=ALU.mult))
                kb.op("dve", [f.b, b.b], [hs_.b], lambda v: v.tensor_tensor(out=hs_.t[:, o * 512:(o + 1) * 512], in0=f.t[:, :], in1=b.t[:, :], op=ALU.add))
                kb.op("dve", [f.b, b.b], [hd_.b], lambda v: v.tensor_tensor(out=hd_.t[:, o * 512:(o + 1) * 512], in0=b.t[:, :], in1=f.t[:, :], op=ALU.subtract))
                yield
            kb.dma("sp", S["hsd"][0, jc * 128:(jc + 1) * 128, :], hs_.t[:], [hs_.b], [kb.dbuf("hsd", 0, jc)])
            kb.dma("sp", S["hsd"][1, jc * 128:(jc + 1) * 128, :], hd_.t[:], [hd_.b], [kb.dbuf("hsd", 1, jc)])

    def bg(self):
        if self.bg_gen is not None:
            try:
                next(self.bg_gen)
            except StopIteration:
                self.bg_gen = None

    def phase_filters(self, l):
        kb = self.kb; S = self.S; I = self.I; pvt = self.pv[l]
        with kb.phase(flip=True) as ph:
            hsum = self.tiles(ph, 16, [128, 1024], BF16); hdif = self.tiles(ph, 16, [128, 1024], BF16)
            for jc in range(16):
                kb.dma("sp", hsum[jc].t[:], S["hsd"][0, jc * 128:(jc + 1) * 128, :], [kb.dbuf("hsd", 0, jc)], [hsum[jc].b])
                kb.dma("sp", hdif[jc].t[:], S["hsd"][1, jc * 128:(jc + 1) * 128, :], [kb.dbuf("hsd", 1, jc)], [hdif[jc].b])
            ct = self.tiles(ph, 2, [128, 8, 128], BF16); stt = self.tiles(ph, 2, [128, 8, 128], BF16)
            t1 = self.tiles(ph, 2, [128, 512], F32); t3 = self.tiles(ph, 2, [128, 512], F32)
            n1 = self.tiles(ph, 2, [128, 512], F32); t2 = self.tiles(ph, 2, [128, 512], F32)
            og_ = self.tiles(ph, 8, [128, 512], F32); n = 0; it = 0
            tw = self.tw
            for kc in range(8):
                c = ct[kc % 2]; s_ = stt[kc % 2]
                kb.dma("sp", c.t[:], I["CTb"][kc], [], [c.b]); kb.dma("sp", s_.t[:], I["STb"][kc], [], [s_.b])
                cth = tw.t[:, kc:kc + 1]; sth = tw.t[:, 8 + kc:9 + kc]
                for og in range(2):
                    def tr(mat, src, par):
                        ps = kb.psum()

                        def mm(p):
                            for mc in range(8):
                                ins = p.matmul(ps.t[:, :], lhsT=mat.t[:, mc, :], rhs=src[par * 8 + mc].t[:, og * 512:(og + 1) * 512], start=(mc == 0), stop=(mc == 7))
                            return ins
                        kb.op("pe", [mat.b] + [x.b for x in src[par * 8:par * 8 + 8]], [ps.b], mm)
                        return ps
                    a1 = t1[it % 2]; a3 = t3[it % 2]; N1 = n1[it % 2]; T2 = t2[it % 2]; it += 1
                    Ec = tr(c, hsum, 0); Oc = tr(c, hsum, 1); Os = tr(s_, hsum, 1)
                    kb.op("act", [Oc.b, tw.b], [a1.b], lambda a: a.activation(out=a1.t[:, :], in_=Oc.t[:, :], func=AF.Identity, bias=0.0, scale=cth))
                    kb.op("dve", [Os.b, a1.b, tw.b], [N1.b], lambda v: v.scalar_tensor_tensor(out=N1.t[:, :], in0=Os.t[:, :], scalar=sth, in1=a1.t[:, :], op0=ALU.mult, op1=ALU.subtract))
                    outs = []
                    for half, op_ in ((0, ALU.subtract), (1, ALU.add)):
                        g = og_[n % 8]; n += 1
                        kb.op("dve", [Ec.b, N1.b], [g.b], lambda v: v.tensor_tensor(out=g.t[:, :], in0=Ec.t[:, :], in1=N1.t[:, :], op=op_))
                        kb.dma("act", S["kfr"][(2 * kc + half) * 128:(2 * kc + half + 1) * 128, og * 512:(og + 1) * 512], g.t[:], [g.b], [kb.dbuf("kfr", 2 * kc + half, og)])
                    Es = tr(s_, hdif, 0); Oc2 = tr(c, hdif, 1); Os2 = tr(s_, hdif, 1)
                    kb.op("act", [Oc2.b, tw.b], [a3.b], lambda a: a.activation(out=a3.t[:, :], in_=Oc2.t[:, :], func=AF.Identity, bias=0.0, scale=sth))
                    kb.op("dve", [Os2.b, a3.b, tw.b], [T2.b], lambda v: v.scalar_tensor_tensor(out=T2.t[:, :], in0=Os2.t[:, :], scalar=cth, in1=a3.t[:, :], op0=ALU.mult, op1=ALU.add))
                    for half, op_ in ((0, ALU.add), (1, ALU.subtract)):
                        g = og_[n % 8]; n += 1
                        kb.op("dve", [Es.b, T2.b], [g.b], lambda v: v.tensor_tensor(out=g.t[:, :], in0=T2.t[:, :], in1=Es.t[:, :], op=op_))
                        kb.dma("act", S["kfi"][(2 * kc + half) * 128:(2 * kc + half + 1) * 128, og * 512:(og + 1) * 512], g.t[:], [g.b], [kb.dbuf("kfi", 2 * kc + half, og)])

    def phase_hyena(self, l):
        kb = self.kb; S = self.S; I = self.I; pvt = self.pv[l]; tw = self.tw
        with kb.phase(flip=True) as ph:
            zv = self.load_fm(ph, S["hyuT"][0:512, :], 4, L, F32, deps=lambda c: [kb.dbuf("hyu", c)])
            cm1 = kb.tile(ph, [128, 8, 512], BF16); sm1 = kb.tile(ph, [128, 8, 512], BF16)
            UV = [self.tiles(ph, 8, [128, 512], BF16) for _ in range(4)]
            xm = self.tiles(ph, 2, [128, 1024], F32); e1 = self.tiles(ph, 2, [128, 512], F32); ost = self.tiles(ph, 2, [128, 1024], BF16)
            ztok = self.tiles(ph, 16, [128, 512], BF16)
            ct = self.tiles(ph, 2, [128, 8, 128], BF16); stt = self.tiles(ph, 2, [128, 8, 128], BF16)
            kf = [self.tiles(ph, 2, [128, 512], F32) for _ in range(4)]
            F = lambda k: self.tiles(ph, k, [128, 512], F32)
            a1, a3, N1, T2 = F(4)
            ABd = [F(4), F(4)]
            Md = [F(4), F(4)]; TTd = [F(2), F(2)]
            YWb = [self.tiles(ph, 4, [128, 512], BF16), self.tiles(ph, 4, [128, 512], BF16)]
            tt = lambda o, a, b, op: (lambda v: v.tensor_tensor(out=o, in0=a, in1=b, op=op))
            for order in range(2):
                for par in range(2):
                    for mc in range(8):
                        ps = kb.psum()

                        def tr(p):
                            for cc in range(4):
                                ins = p.transpose(ps.t[:, cc * 128:(cc + 1) * 128], zv[cc].t[:, mc * 256 + par:mc * 256 + 256:2], self.identf.t[:, :])
                            return ins
                        kb.op("pe", [z.b for z in zv] + [self.identf.b], [ps.b], tr)
                        zt = ztok[par * 8 + mc]
                        kb.op("act", [ps.b], [zt.b], lambda a: a.activation(out=zt.t[:, :], in_=ps.t[:, :], func=AF.Copy))
                def stageA(kc):
                    c = ct[kc % 2]; s_ = stt[kc % 2]
                    kb.dma("sp", c.t[:], I["CTb"][kc], [], [c.b]); kb.dma("sp", s_.t[:], I["STb"][kc], [], [s_.b])
                    K = [kf[i][kc % 2] for i in range(4)]
                    for i, (nm, half) in enumerate((("kfr", 0), ("kfr", 1), ("kfi", 0), ("kfi", 1))):
                        kb.dma("sp", K[i].t[:], S[nm][(2 * kc + half) * 128:(2 * kc + half + 1) * 128, order * 512:(order + 1) * 512], [kb.dbuf(nm, 2 * kc + half, order)], [K[i].b])
                    cth = tw.t[:, kc:kc + 1]; sth = tw.t[:, 8 + kc:9 + kc]

                    def trf(mat, par):
                        ps = kb.psum()

                        def mm(p):
                            for mc in range(8):
                                ins = p.matmul(ps.t[:, :], lhsT=mat.t[:, mc, :], rhs=ztok[par * 8 + mc].t[:, :], start=(mc == 0), stop=(mc == 7))
                            return ins
                        kb.op("pe", [mat.b] + [x.b for x in ztok[par * 8:par * 8 + 8]], [ps.b], mm)
                        return ps
                    Ec = trf(c, 0); Es = trf(s_, 0); Oc = trf(c, 1); Os = trf(s_, 1)
                    kb.op("act", [Oc.b, tw.b], [a1.b], lambda a: a.activation(out=a1.t[:, :], in_=Oc.t[:, :], func=AF.Identity, bias=0.0, scale=cth))
                    kb.op("act", [Oc.b, tw.b], [a3.b], lambda a: a.activation(out=a3.t[:, :], in_=Oc.t[:, :], func=AF.Identity, bias=0.0, scale=sth))
                    kb.op("dve", [Os.b, a1.b, tw.b], [N1.b], lambda v: v.scalar_tensor_tensor(out=N1.t[:, :], in0=Os.t[:, :], scalar=sth, in1=a1.t[:, :], op0=ALU.mult, op1=ALU.subtract))
                    kb.op("dve", [Os.b, a3.b, tw.b], [T2.b], lambda v: v.scalar_tensor_tensor(out=T2.t[:, :], in0=Os.t[:, :], scalar=cth, in1=a3.t[:, :], op0=ALU.mult, op1=ALU.add))
                    Alo, Ahi, Blo, Bhi = ABd[kc % 2]
                    kb.op("dve", [Ec.b, N1.b], [Alo.b], tt(Alo.t[:, :], Ec.t[:, :], N1.t[:, :], ALU.subtract))
                    kb.op("dve", [Ec.b, N1.b], [Ahi.b], tt(Ahi.t[:, :], Ec.t[:, :], N1.t[:, :], ALU.add))
                    kb.op("dve", [Es.b, T2.b], [Blo.b], tt(Blo.t[:, :], T2.t[:, :], Es.t[:, :], ALU.add))
                    kb.op("dve", [Es.b, T2.b], [Bhi.b], tt(Bhi.t[:, :], T2.t[:, :], Es.t[:, :], ALU.subtract))
                    for half, (A_, B_, Kr, Ki) in enumerate(((Alo, Blo, K[0], K[2]), (Ahi, Bhi, K[1], K[3]))):
                        mA = Md[kc % 2][2 * half]; mB = Md[kc % 2][2 * half + 1]
                        kb.op("dve", [A_.b, Ki.b], [mA.b], tt(mA.t[:, :], A_.t[:, :], Ki.t[:, :], ALU.mult))
                        kb.op("dve", [B_.b, Ki.b], [mB.b], tt(mB.t[:, :], B_.t[:, :], Ki.t[:, :], ALU.mult))
                        kb.op("dve", [A_.b, Kr.b], [A_.b], tt(A_.t[:, :], A_.t[:, :], Kr.t[:, :], ALU.mult))
                        kb.op("dve", [B_.b, Kr.b], [B_.b], tt(B_.t[:, :], B_.t[:, :], Kr.t[:, :], ALU.mult))
                        Yb = YWb[kc % 2][half]; Wb = YWb[kc % 2][2 + half]
                        kb.op("pool", [A_.b, mB.b], [Yb.b], tt(Yb.t[:, :], A_.t[:, :], mB.t[:, :], ALU.add))
                        kb.op("pool", [B_.b, mA.b], [Wb.b], tt(Wb.t[:, :], B_.t[:, :], mA.t[:, :], ALU.subtract))
                def stageB(kc):
                    cth = tw.t[:, kc:kc + 1]; sth = tw.t[:, 8 + kc:9 + kc]
                    Ue, Ve, Uo, Vo = (UV[i][kc] for i in range(4))
                    x1_, x2_ = TTd[kc % 2]
                    Y0, Y1, W0, W1 = YWb[kc % 2]
                    I_ = self.identb; N_ = self.nidentb

                    def comb(x0, x1, s1):
                        ps = kb.psum()

                        def mm(p):
                            p.matmul(ps.t[:, :], lhsT=I_.t[:, :], rhs=x0.t[:, :], start=True, stop=False)
                            return p.matmul(ps.t[:, :], lhsT=s1.t[:, :], rhs=x1.t[:, :], start=False, stop=True)
                        kb.op("pe", [x0.b, x1.b, I_.b, s1.b], [ps.b], mm)
                        return ps
                    pUe = comb(Y0, Y1, I_); pVe = comb(W0, W1, N_); pP = comb(Y0, Y1, N_); pQ = comb(W0, W1, I_)
                    kb.op("act", [pUe.b], [Ue.b], lambda a: a.activation(out=Ue.t[:, :], in_=pUe.t[:, :], func=AF.Copy))
                    kb.op("act", [pVe.b], [Ve.b], lambda a: a.activation(out=Ve.t[:, :], in_=pVe.t[:, :], func=AF.Copy))
                    kb.op("act", [pP.b, tw.b], [x1_.b], lambda a: a.activation(out=x1_.t[:, :], in_=pP.t[:, :], func=AF.Identity, bias=0.0, scale=cth))
                    kb.op("act", [pP.b, tw.b], [x2_.b], lambda a: a.activation(out=x2_.t[:, :], in_=pP.t[:, :], func=AF.Identity, bias=0.0, scale=sth))
                    kb.op("dve", [pQ.b, x1_.b, tw.b], [Uo.b], lambda v: v.scalar_tensor_tensor(out=Uo.t[:, :], in0=pQ.t[:, :], scalar=sth, in1=x1_.t[:, :], op0=ALU.mult, op1=ALU.add))
                    kb.op("dve", [pQ.b, x2_.b, tw.b], [Vo.b], lambda v: v.scalar_tensor_tensor(out=Vo.t[:, :], in0=pQ.t[:, :], scalar=cth, in1=x2_.t[:, :], op0=ALU.mult, op1=ALU.subtract))
                kb.rot = 8
                stageA(0)
                for kc in range(8):
                    if kc + 1 < 8: stageA(kc + 1)
                    stageB(kc)
                kb.rot = 6
                xoff = 512 * (order + 1); it = 0
                for mg in range(2):
                    kb.dma("sp", cm1.t[:], I["Cm"][mg], [], [cm1.b]); kb.dma("sp", sm1.t[:], I["Sm"][mg], [], [sm1.b])
                    for cc in range(4):
                        x_ = xm[it % 2]; o_ = ost[it % 2]; it += 1
                        kb.dma("sp", x_.t[:], S["hyuT"][xoff + cc * 128:xoff + (cc + 1) * 128, mg * 1024:(mg + 1) * 1024], [kb.dbuf("hyu", xoff // 128 + cc)], [x_.b])
                        z = zv[cc]
                        for par in range(2):
                            U = UV[2 * par]; V = UV[2 * par + 1]
                            ps = kb.psum()

                            def mm(p):
                                for kc in range(8):
                                    p.matmul(ps.t[:, :], lhsT=U[kc].t[:, cc * 128:(cc + 1) * 128], rhs=cm1.t[:, kc, :], start=(kc == 0), stop=False)
                                for kc in range(8):
                                    ins = p.matmul(ps.t[:, :], lhsT=V[kc].t[:, cc * 128:(cc + 1) * 128], rhs=sm1.t[:, kc, :], start=False, stop=(kc == 7))
                                return ins
                            kb.op("pe", [cm1.b, sm1.b] + [x.b for x in U] + [x.b for x in V], [ps.b], mm)
                            t_ = e1[par]
                            zsl = z.t[:, mg * 1024 + par:(mg + 1) * 1024:2]; xsl = x_.t[:, par:1024:2]
                            kb.op("dve", [z.b, ps.b, pvt.b], [t_.b], lambda v: v.scalar_tensor_tensor(
                                out=t_.t[:, :], in0=zsl, scalar=pvt.t[:, cHB(order, cc):cHB(order, cc) + 1], in1=ps.t[:, :], op0=ALU.mult, op1=ALU.add))
                            if order == 0:
                                kb.op("dve", [t_.b, x_.b], [z.b], lambda v: v.tensor_tensor(out=zsl, in0=t_.t[:, :], in1=xsl, op=ALU.mult))
                            else:
                                kb.op("dve", [t_.b, x_.b], [o_.b], lambda v: v.tensor_tensor(out=o_.t[:, par:1024:2], in0=t_.t[:, :], in1=xsl, op=ALU.mult))
                        if order == 1:
                            kb.dma("pool", S["yaT"][cc * 128:(cc + 1) * 128, mg * 1024:(mg + 1) * 1024], o_.t[:], [o_.b], [kb.dbuf("ya", cc, 2 * mg), kb.dbuf("ya", cc, 2 * mg + 1)])

    def phase_na(self, l):
        kb = self.kb; S = self.S; I = self.I
        with kb.phase(flip=True) as ph:
            qT = self.tiles(ph, 4, [128, L], BF16)
            kTm = self.tiles(ph, 8, [128, L], BF16)
            tabs = self.tiles(ph, 8, [128, NS * 64], BF16); rm = kb.tile(ph, [128, 20 * 512], BF16); sel = kb.tile(ph, [128, 128], BF16)
            va = self.tiles(ph, 16, [128, 1024], BF16)
            for h in range(8):
                kb.op("dve", [], [kTm[h].b], lambda v: v.memset(kTm[h].t[:, :], 0.0))
                kb.dma("pool", tabs[h].t[:], I["natab"][l][:, h * NS * 64:(h + 1) * NS * 64], [], [tabs[h].b])
            kb.dma("sp", sel.t[:], I["nasel"][:, :], [], [sel.b]); kb.dma("sp", rm.t[:], I["narm"][:, :], [], [rm.b])
            for h in range(8):
                hc = h // 2; bp = (h % 2) * 64
                if h % 2 == 0:
                    kb.dma("sp", qT[hc].t[:], S["qT"][hc * 128:(hc + 1) * 128, :], [kb.dbuf("qk", 12 + hc, tg) for tg in range(4)], [qT[hc].b])
                kb.dma("sp", kTm[h].t[bp:bp + 64, :], S["kT"][hc * 128 + bp:hc * 128 + bp + 64, :], [kb.dbuf("qk", 16 + hc, tg) for tg in range(4)], [kTm[h].b])
                if h == 0:
                    for tc in range(16):
                        kb.dma("sp", va[tc].t[:], S["vaug"][tc], [kb.dbuf("vaug", tc)], [va[tc].b])
            yb = self.tiles(ph, 4, [128, L], BF16)
            pt = self.tiles(ph, 4, [128, 512], BF16)
            rd = self.tiles(ph, 2, [64, 512], F32)
            items = [(h, g, ji, j) for h in range(8) for g in range(4) for ji, j in enumerate(NA_GROUP_CHUNKS[g])]
            P = {}
            gst = ExitStack(); self.bg_gen = self.sc_gen(l, gst)

            def emit_s(i):
                h, g, ji, j = items[i]
                hc = h // 2
                p_ = pt[i % 4]
                ps = kb.psum()
                s0 = 11 - 2 * j + 8 * g; ti = NA_TILE_BASE[g] + ji
                assert 0 <= s0 and s0 + 8 <= NS

                def mm(p):
                    p.matmul(ps.t[:, :], lhsT=kTm[h].t[:, j * 128:(j + 1) * 128], rhs=qT[hc].t[:, g * 512:(g + 1) * 512], start=True, stop=False)
                    p.matmul(ps.t[:, :], lhsT=self.identb.t[:, :], rhs=tabs[h].t[:, s0 * 64:(s0 + 8) * 64], start=False, stop=False)
                    return p.matmul(ps.t[:, :], lhsT=sel.t[:, :], rhs=rm.t[:, ti * 512:(ti + 1) * 512], start=False, stop=True)
                kb.op("pe", [kTm[h].b, qT[hc].b, tabs[h].b, rm.b, sel.b, self.identb.b], [ps.b], mm)
                kb.op("act", [ps.b], [p_.b], lambda a: a.activation(out=p_.t[:, :], in_=ps.t[:, :], func=AF.Exp))
                P[i] = p_
            emit_s(0); emit_s(1)
            m = 0; po = None
            for i, (h, g, ji, j) in enumerate(items):
                hc = h // 2; bp = (h % 2) * 64
                if i + 2 < len(items): emit_s(i + 2)
                if i % 7 == 3: self.bg()
                nch = len(NA_GROUP_CHUNKS[g])
                if ji == 0: po = kb.acc()
                p_ = P.pop(i)
                kb.op("pe", [va[j].b, p_.b], [po.b], lambda p: p.matmul(po.t[:, :], lhsT=va[j].t[:, h * 128:(h + 1) * 128], rhs=p_.t[:, :], start=(ji == 0), stop=(ji == nch - 1)))
                if ji == nch - 1:
                    r_ = rd[m % 2]; m += 1
                    kb.op("act", [po.b], [r_.b], lambda a: a.activation(out=r_.t[:, :], in_=po.t[64:128, :], func=AF.Ln))
                    kb.op("act", [r_.b], [r_.b], lambda a: a.activation(out=r_.t[:, :], in_=r_.t[:, :], func=AF.Exp, scale=-1.0))
                    kb.op("dve", [po.b, r_.b], [yb[hc].b], lambda v: v.tensor_tensor(out=yb[hc].t[bp:bp + 64, g * 512:(g + 1) * 512], in0=po.t[0:64, :], in1=r_.t[:, :], op=ALU.mult))
            while self.bg_gen is not None: self.bg()
            gst.close()
            for c in range(4):
                kb.dma("sp", S["ybT"][c * 128:(c + 1) * 128, :], yb[c].t[:], [yb[c].b], [kb.dbuf("yb", c)])

    def sc_gen(self, l, st):
        kb = self.kb; S = self.S; pvt = self.pv[l]
        kb.side ^= 1
        ub = self.tiles(st, 1, [128, L + 2], F32)[0]; cv = kb.tile(st, [128, L], F32); ot = self.tiles(st, 2, [128, L], BF16)
        bb = kb.tile(st, [128, L], F32); cb = kb.tile(st, [128, L], F32); xb = kb.tile(st, [128, L], F32)
        kb.side ^= 1
        kb.op("dve", [], [ub.b], lambda v: v.memset(ub.t[:, :], 0.0))
        yield
        for cc in range(4):
            for t, off in ((bb, 0), (cb, 512), (xb, 1024)):
                kb.dma("sp", t.t[:], S["scuT"][off + cc * 128:off + (cc + 1) * 128, :], [kb.dbuf("scu", off // 128 + cc, tg) for tg in range(4)], [t.b])
            yield
            kb.op("dve", [cb.b, xb.b], [ub.b], lambda v: v.tensor_tensor(out=ub.t[:, 1:L + 1], in0=cb.t[:, :], in1=xb.t[:, :], op=ALU.mult))
            yield
            kb.op("dve", [ub.b, pvt.b], [cv.b], lambda v: v.tensor_scalar(out=cv.t[:, :], in0=ub.t[:, 1:1 + L], scalar1=pvt.t[:, cSCW(1, cc):cSCW(1, cc) + 1], scalar2=None, op0=ALU.mult))
            yield
            for k in (0, 2):
                kb.op("dve", [ub.b, pvt.b, cv.b], [cv.b], lambda v: v.scalar_tensor_tensor(
                    out=cv.t[:, :], in0=ub.t[:, k:k + L], scalar=pvt.t[:, cSCW(k, cc):cSCW(k, cc) + 1], in1=cv.t[:, :], op0=ALU.mult, op1=ALU.add))
                yield
            o_ = ot[cc % 2]
            kb.op("dve", [cv.b, bb.b], [o_.b], lambda v: v.tensor_tensor(out=o_.t[:, :], in0=cv.t[:, :], in1=bb.t[:, :], op=ALU.mult))
            kb.dma("sp", S["ycT"][cc * 128:(cc + 1) * 128, :], o_.t[:], [o_.b], [kb.dbuf("yc", cc)])
            yield

    def phase_sc(self, l):
        pass

    def phase_merge(self, l):
        kb = self.kb; S = self.S; I = self.I; pvt = self.pv[l]
        with kb.phase(flip=True) as ph0:
          ms = self.tiles(ph0, 8, [128, L], BF16)
          wpre = self.prefetch_w(ph0, I["w_out"][l], D, 0, D, 512, 2)
          with kb.phase() as ph:
            ys = [self.load_fm(ph, S["yaT"], 4, L, BF16, deps=lambda c: [kb.dbuf("ya", c, tg) for tg in range(4)]),
                  self.load_fm(ph, S["ybT"], 4, L, BF16, deps=lambda c: [kb.dbuf("yb", c)]),
                  self.load_fm(ph, S["ycT"], 4, L, BF16, deps=lambda c: [kb.dbuf("yc", c)])]
            wb = self.tiles(ph, 3, [128, 4, D], BF16)
            for b in range(3):
                kb.dma("pool", wb[b].t[:], I["w_branch"][l, b].rearrange("(kc p) n -> p kc n", p=128), [], [wb[b].b])
            gt = self.tiles(ph, 12, [128, 512], BF16); acc = self.tiles(ph, 3, [128, 512], F32); tmp = self.tiles(ph, 6, [128, 512], F32)
            ost = self.tiles(ph, 2, [128, 512], BF16)
            n = 0; it = 0
            for oc in range(8):
                for tg in range(4):
                    a_ = acc[it % 3]; o_ = ost[it % 2]; it += 1
                    for b in range(3):
                        t_ = tmp[(2 * it + b) % 6]
                        g_ = gt[n % 12]; n += 1
                        kb.dma("sp", g_.t[:], S["gatesT"][(b * 8 + oc) * 128:(b * 8 + oc + 1) * 128, tg * 512:(tg + 1) * 512], [kb.dbuf("gates", b * 8 + oc, tg)], [g_.b])
                        ps = kb.psum()

                        def mm(p, b=b, ps=ps):
                            for cc in range(4):
                                ins = p.matmul(ps.t[:, :], lhsT=wb[b].t[:, cc, oc * 128:(oc + 1) * 128], rhs=ys[b][cc].t[:, tg * 512:(tg + 1) * 512], start=(cc == 0), stop=(cc == 3))
                            return ins
                        kb.op("pe", [wb[b].b] + [y.b for y in ys[b]], [ps.b], mm)
                        if b == 0:
                            kb.op("dve", [ps.b, g_.b], [a_.b], lambda v, a_=a_, g_=g_, ps=ps: v.tensor_tensor(out=a_.t[:, :], in0=ps.t[:, :], in1=g_.t[:, :], op=ALU.mult))
                        else:
                            kb.op("dve", [ps.b, g_.b], [t_.b], lambda v, t_=t_, g_=g_, ps=ps: v.tensor_tensor(out=t_.t[:, :], in0=ps.t[:, :], in1=g_.t[:, :], op=ALU.mult))
                            if b == 1:
                                kb.op("pool", [a_.b, t_.b], [a_.b], lambda v, a_=a_, t_=t_: v.tensor_tensor(out=a_.t[:, :], in0=a_.t[:, :], in1=t_.t[:, :], op=ALU.add))
                            else:
                                kb.op("dve", [a_.b, t_.b], [ms[oc].b], lambda v, a_=a_, t_=t_: v.tensor_tensor(out=ms[oc].t[:, tg * 512:(tg + 1) * 512], in0=a_.t[:, :], in1=t_.t[:, :], op=ALU.add))
          self.proj_norm_resid_tg(ms, I["w_out"][l], pvt, cG(1), (pvt, cG(2)), wpre)

    def phase_xattn(self, l):
        kb = self.kb; I = self.I; pvt = self.pv[l]
        with kb.phase(flip=True) as ph:
            oT = self.tiles(ph, 8, [128, L], BF16)
            wpre = (self.tiles(ph, 2, [128, 8, 512], BF16), 2)
            with kb.phase() as pa:
                hs = self.load_h(pa); mn = self.tiles(pa, 8, [128, NM], BF16)
                qT = self.tiles(pa, 8, [128, L], BF16); kmT = self.tiles(pa, 8, [128, NM], BF16); vm = self.tiles(pa, 2, [128, D], BF16)
                pt = self.tiles(pa, 6, [128, 512], BF16); rdn = self.tiles(pa, 3, [128, 512], F32)
                with kb.phase() as phx:
                    ms = self.load_fm(phx, I["memT"], 8, NM, F32)
                    self.norm_apply(phx, ms, mn, NM, pvt, cMN)
                with kb.phase() as p2:
                    self.linear_fm(p2, I["xa_wq"][l], D, 0, D, hs, L, lambda oc, t0, tw, ps: kb.op("act", [ps.b], [qT[oc].b], lambda a: a.activation(out=qT[oc].t[:, t0:t0 + tw], in_=ps.t[:, 0:tw], func=AF.Identity, bias=0.0, scale=1.0 / 16)))
                with kb.phase() as p2:
                    self.linear_fm(p2, I["xa_wkv"][l], D, 0, D, mn, NM, lambda oc, t0, tw, ps: kb.op("act", [ps.b], [kmT[oc].b], lambda a: a.activation(out=kmT[oc].t[:, t0:t0 + tw], in_=ps.t[:, 0:tw], func=AF.Copy)))
                with kb.phase() as p2:
                    self.linear_tm(p2, I["xa_wkv"][l], D, D, D, mn, NM, lambda tc, gi, gw, ps: kb.op("act", [ps.b], [vm[tc].b], lambda a: a.activation(out=vm[tc].t[:, gi * 512:gi * 512 + gw], in_=ps.t[:, 0:gw], func=AF.Copy)))
                items = [(hh, tg) for hh in range(4) for tg in range(4)]
                ST = {}
                for n_ in range(2):
                    kb.dma("pool", wpre[0][n_].t[:], I["xa_wo"][l][:, n_ * 512:(n_ + 1) * 512].rearrange("(kc p) n -> p kc n", p=128), [], [wpre[0][n_].b])

                def stage1(i):
                    hh, tg = items[i]
                    P = []
                    for mc in range(2):
                        ps = kb.psum(); p_ = pt[(2 * i + mc) % 6]

                        def mm(p):
                            for fc in range(2):
                                ins = p.matmul(ps.t[:, :], lhsT=kmT[2 * hh + fc].t[:, mc * 128:(mc + 1) * 128], rhs=qT[2 * hh + fc].t[:, tg * 512:(tg + 1) * 512], start=(fc == 0), stop=(fc == 1))
                            return ins
                        kb.op("pe", [kmT[2 * hh].b, kmT[2 * hh + 1].b, qT[2 * hh].b, qT[2 * hh + 1].b], [ps.b], mm)
                        kb.op("act", [ps.b], [p_.b], lambda a: a.activation(out=p_.t[:, :], in_=ps.t[:, :], func=AF.Exp))
                        P.append(p_)
                    pd = kb.acc()

                    def mmd(p):
                        for mc in range(2):
                            ins = p.matmul(pd.t[:, :], lhsT=self.onesb.t[:, :], rhs=P[mc].t[:, :], start=(mc == 0), stop=(mc == 1))
                        return ins
                    kb.op("pe", [P[0].b, P[1].b, self.onesb.b], [pd.b], mmd)
                    r_ = rdn[i % 3]
                    kb.op("act", [pd.b], [r_.b], lambda a: a.activation(out=r_.t[:, :], in_=pd.t[:, :], func=AF.Ln))
                    kb.op("act", [r_.b], [r_.b], lambda a: a.activation(out=r_.t[:, :], in_=r_.t[:, :], func=AF.Exp, scale=-1.0))
                    ST[i] = (P, r_)

                def stage2(i):
                    hh, tg = items[i]
                    P, r_ = ST.pop(i)
                    for dc in range(2):
                        po = kb.psum()

                        def mmo(p):
                            for mc in range(2):
                                ins = p.matmul(po.t[:, :], lhsT=vm[mc].t[:, hh * 256 + dc * 128:hh * 256 + (dc + 1) * 128], rhs=P[mc].t[:, :], start=(mc == 0), stop=(mc == 1))
                            return ins
                        kb.op("pe", [P[0].b, P[1].b, vm[0].b, vm[1].b], [po.b], mmo)
                        ot = oT[2 * hh + dc]
                        kb.op("dve", [po.b, r_.b], [ot.b], lambda v: v.tensor_tensor(out=ot.t[:, tg * 512:(tg + 1) * 512], in0=po.t[:, :], in1=r_.t[:, :], op=ALU.mult))
                stage1(0)
                for i in range(len(items)):
                    if i + 1 < len(items): stage1(i + 1)
                    stage2(i)
            self.proj_norm_resid_tg(oT, I["xa_wo"][l], pvt, cG(3), (pvt, cG(4)), wpre)

    def phase_ffn(self, l, final):
        kb = self.kb; I = self.I; pvt = self.pv[l]; w = I["ffn_up"][l]
        with kb.phase(flip=True) as ph:
            tT = self.tiles(ph, 22, [128, L], BF16)
            with kb.phase() as pa:
                hs = self.load_h(pa)
                ug = self.padded(pa, 2); uv = self.padded(pa, 2)
                cg = self.tiles(pa, 2, [128, L], F32); cv = self.tiles(pa, 2, [128, L], F32)
                wb = self.tiles(pa, 4, [128, 8, 128], BF16)
                for i in range(22):
                    for part, (ub, ob, col, ci) in enumerate(((ug[i % 2], cg[i % 2], i * 128, i), (uv[i % 2], cv[i % 2], DFF + i * 128, 22 + i))):
                        wt = wb[(2 * i + part) % 4]
                        kb.dma("pool", wt.t[:], w[:, col:col + 128].rearrange("(kc p) n -> p kc n", p=128), [], [wt.b])
                        for tg in range(4):
                            ps = kb.psum()

                            def mm(p):
                                for kc in range(8):
                                    ins = p.matmul(ps.t[:, :], lhsT=wt.t[:, kc, :], rhs=hs[kc].t[:, tg * 512:(tg + 1) * 512], start=(kc == 0), stop=(kc == 7))
                                return ins
                            kb.op("pe", [wt.b] + [h.b for h in hs], [ps.b], mm)
                            kb.op("act", [ps.b], [ub.b], lambda a: a.activation(out=ub.t[:, 1 + tg * 512:1 + (tg + 1) * 512], in_=ps.t[:, :], func=AF.Copy))
                            kb.op("act", [ps.b, pvt.b], [ob.b], lambda a: a.activation(out=ob.t[:, tg * 512:(tg + 1) * 512], in_=ps.t[:, :], func=AF.Identity, bias=0.0, scale=pvt.t[:, cFCW(1, ci):cFCW(1, ci) + 1]))
                    g_ = cg[i % 2]; v_ = cv[i % 2]
                    self.conv3(ug[i % 2], g_, pvt, lambda k: cFCW(k, i), center_done=True)
                    kb.op("act", [g_.b], [g_.b], lambda a: a.activation(out=g_.t[:, :], in_=g_.t[:, :], func=AF.Gelu_apprx_tanh))
                    self.conv3(uv[i % 2], v_, pvt, lambda k: cFCW(k, 22 + i), center_done=True)
                    kb.op("dve", [g_.b, v_.b], [tT[i].b], lambda v: v.tensor_tensor(out=tT[i].t[:, :], in0=g_.t[:, :], in1=v_.t[:, :], op=ALU.mult))
            self.proj_norm_resid(tT, I["ffn_down"][l], DFF, pvt, cG(5), cgw=256, final=final, nxt=(None if final else (self.pv[l + 1], cG(0))), nxb=2)

    def run(self, stop_after=None):
        kb = self.kb
        self.xs = self.nc.dram_tensor("xscr", [D, L], F32, kind=("ExternalOutput" if self.dbg else "Internal")).ap()
        self.xcur = self.I["xT"]
        n = 0
        for l in range(NL):
            for f in (self.phase_proj, self.phase_filters, self.phase_hyena, self.phase_na, self.phase_sc, self.phase_merge, self.phase_xattn):
                if stop_after is not None and n >= stop_after: break
                f(l); n += 1
            if stop_after is not None and n >= stop_after: break
            self.phase_ffn(l, final=(l == NL - 1)); n += 1
        kb.barrier()


_CONST = {}


def _constants():
    if _CONST: return _CONST
    H = L // 2
    th = np.pi * (np.arange(H, dtype=np.float64) + 0.5) / L
    ang = np.outer(2.0 * th, np.arange(H, dtype=np.float64))
    C = np.cos(ang); Sn = np.sin(ang)
    def fwd(M):
        return np.ascontiguousarray(M.reshape(8, 128, 8, 128).transpose(0, 3, 2, 1)).astype(BF)
    def inv(M):
        return np.ascontiguousarray(M.reshape(8, 128, 2, 512).transpose(2, 1, 0, 3)).astype(BF)
    _CONST["CTb"] = fwd(C); _CONST["STb"] = fwd(Sn); _CONST["Cm"] = inv(C); _CONST["Sm"] = inv(Sn)
    tw = np.zeros((128, 16), np.float32)
    tw[:, 0:8] = np.cos(th).reshape(8, 128).T; tw[:, 8:16] = np.sin(th).reshape(8, 128).T
    _CONST["tw"] = tw
    f32 = np.float32
    t = np.linspace(0.0, 1.0, L, dtype=f32)[:, None]
    bands = 16
    w = (2.0 * np.pi * np.arange(L, dtype=f32)[:, None] / L).astype(f32)
    f = np.linspace(1e-4, bands - 1, bands, dtype=f32)[None, :]
    z = np.concatenate([t, np.cos(f * w), -np.sin(f * w)], axis=-1).astype(f32)
    _CONST["zT"] = np.ascontiguousarray(z.T)
    deltas = np.abs(np.linspace(np.log(1e-2) / 1.5, np.log(1e-2) / 0.3, 512, dtype=f32))
    dec = np.exp(-t * deltas[None, :]).astype(f32)
    decb = dec.copy(); decb[0, :] = 0.0
    eo = lambda a: np.ascontiguousarray(np.concatenate([a[0::2], a[1::2]], axis=0) * f32(1.0 / L))
    _CONST["decay"] = eo(dec)
    _CONST["decayb"] = eo(decb)
    _CONST["identf"] = np.eye(128, dtype=f32); _CONST["identb"] = np.eye(128).astype(BF)
    p = np.arange(128)[:, None, None]; sl = np.arange(NS)[None, :, None]; q = np.arange(64)[None, None, :]
    krl = p // 64; c = p % 64
    dr = 14 - (sl - krl - 4)
    cs = np.clip(q - 8, 0, 48)
    val = (dr >= 0) & (dr <= 14) & (c >= cs) & (c < cs + 16)
    _CONST["na_idx"] = (np.broadcast_to(np.clip(dr, 0, 14), (128, NS, 64)).copy(), np.broadcast_to(np.clip(c - q + 15, 0, 30), (128, NS, 64)).copy(), np.broadcast_to(val, (128, NS, 64)).copy())
    rmk = np.zeros((2, 20, 512), f32)
    for g in (0, 1, 3):
        for ji, j in enumerate(NA_GROUP_CHUNKS[g]):
            ti = NA_TILE_BASE[g] + ji
            for k_ in range(2):
                kr = 2 * j + k_
                r = 8 * g + np.arange(512) // 64
                rs = np.clip(r - 4, 0, 24)
                rmk[k_, ti] = np.where((kr >= rs) & (kr < rs + 8), 0.0, MASKV)
    rmp = np.zeros((128, 20 * 512), f32); rmp[0:2] = rmk.reshape(2, 20 * 512)
    _CONST["narm"] = rmp.astype(BF)
    sel = np.zeros((128, 128), f32); sel[0, :64] = 1; sel[1, 64:] = 1
    _CONST["nasel"] = sel.astype(BF)
    return _CONST


def _fm(v):
    return np.ascontiguousarray(np.asarray(v, np.float32).reshape(-1, 128).T)


def _prep(inputs):
    C = _constants()
    I = {k: np.asarray(v) for k, v in inputs.items()}
    pv = np.zeros((NL, 128, NV), np.float32)
    for l in range(NL):
        for i in range(6): pv[l, :, cG(i):cG(i) + 8] = _fm(I["norm_gains"][l, i])
        pv[l, :, cMN:cMN + 8] = _fm(I["mem_norm"][l])
        for b in range(3): pv[l, :, cGB + 8 * b:cGB + 8 * b + 8] = _fm(I["gate_bias"][l, b])
        for k in range(3):
            pv[l, :, cHSW(k, 0):cHSW(k, 0) + 12] = _fm(I["hy_short_w"][l, k])
            pv[l, :, cSCW(k, 0):cSCW(k, 0) + 4] = _fm(I["sc_conv_w"][l, k])
            pv[l, :, cFCW(k, 0):cFCW(k, 0) + 44] = _fm(I["ffn_conv"][l, k])
        for o in range(2): pv[l, :, cHB(o, 0):cHB(o, 0) + 4] = _fm(I["hy_bias"][l, o])
        pv[l, 0:64, cB1] = I["hy_b1"][l]; pv[l, 0:64, cB2] = I["hy_b2"][l]
        pv[l, 0:64, cF0] = I["hy_freq"][l, 0]; pv[l, 0:64, cF1] = I["hy_freq"][l, 1]
    idr, idc, val = C["na_idx"]
    rpb = I["na_rpb"].astype(np.float32)
    tab = np.where(val[None, None], rpb[:, :, idr, idc], np.float32(MASKV)).astype(np.float32)
    tab = np.ascontiguousarray(tab.transpose(0, 2, 1, 3, 4)).reshape(NL, 128, 8 * NS * 64)
    shared = {"pv": pv, "natab": tab}
    for k in ("w_in", "hy_w1", "hy_w2", "hy_w3", "w_branch", "w_out", "xa_wq", "xa_wkv", "xa_wo", "ffn_up", "ffn_down"):
        shared[k] = np.ascontiguousarray(I[k], dtype=np.float32)
    for k in ("CTb", "STb", "Cm", "Sm", "tw", "zT", "decay", "decayb", "identf", "identb", "narm", "nasel"):
        shared[k] = C[k]
    maps = []
    for b in range(8):
        m = dict(shared)
        m["xT"] = np.ascontiguousarray(I["x"][b].T.astype(np.float32))
        m["memT"] = np.ascontiguousarray(I["mem"][b].T.astype(np.float32))
        maps.append(m)
    return maps


def build(dbg=False, stop_after=None):
    nc = bass.Bass("TRN2", target_bir_lowering=False)
    st = ExitStack()
    kb = KB(nc, st)
    m = Model(nc, kb, st, dbg=dbg)
    m.run(stop_after)
    st.close()
    return nc


def kernel(**inputs):
    maps = _prep(inputs)
    nc = build()
    res = run_bass_kernel_spmd(nc, maps, core_ids=list(range(8)))
    out = np.stack([np.asarray(r["outT"], np.float32).T for r in res.results], axis=0)
    return np.ascontiguousarray(out)
```

```python
import numpy as np, ml_dtypes
import concourse.bass as bass
import concourse.mybir as mybir
from concourse.bass_utils import run_bass_kernel_spmd
from contextlib import ExitStack, contextmanager

F32 = mybir.dt.float32; BF16 = mybir.dt.bfloat16
ALU = mybir.AluOpType; AF = mybir.ActivationFunctionType
BF = ml_dtypes.bfloat16

L = 2048; D = 1024; NM = 256; DFF = 2816; PW = 7680; NL = 2
EPS = 1e-6
NV = 272
cG = lambda i: 8 * i
cMN = 48
cGB = 56
cHSW = lambda k, cc: 80 + 12 * k + cc
cHB = lambda o, cc: 116 + 4 * o + cc
cSCW = lambda k, cc: 124 + 4 * k + cc
cFCW = lambda k, cc: 136 + 44 * k + cc
cB1, cB2, cF0, cF1 = 268, 269, 270, 271
NA_GROUP_CHUNKS = [list(range(0, 6)), list(range(2, 10)), list(range(6, 14)), list(range(10, 16))]
NA_TILE_BASE = [0, 6, 6, 14]
MASKV = -30000.0
NS = 23
SB_LO = 16640; SB_HI = 221184


class Buf:
    __slots__ = ("w", "r")

    def __init__(self):
        self.w = None; self.r = {}


class KB:
    def __init__(self, nc, stack, n_dma_sems=40, same_engine_sync=True):
        self.nc = nc; self.ses = same_engine_sync
        self.eng = {"pe": nc.tensor, "dve": nc.vector, "act": nc.scalar, "pool": nc.gpsimd, "sp": nc.sync}
        self.semh = {}; self.cnt = {}
        for e in ("pe", "dve", "act", "pool"):
            self.semh[e] = stack.enter_context(nc.semaphore("s_" + e)); self.cnt[e] = 0
        self.nd = n_dma_sems
        for i in range(n_dma_sems):
            self.semh[("d", i)] = stack.enter_context(nc.semaphore(f"d{i}")); self.cnt[("d", i)] = 0
        self.dnext = 0
        self.known = {e: {} for e in self.eng}
        self.uid = 0
        self.dbufs = {}
        self.psums = []; self.psn = 0; self.accn = 0; self.rot = 6
        self.lo = SB_LO; self.hi = SB_HI; self.side = 0; self.ghosts = []; self.peak = 0

    def tile(self, stack, shape, dt):
        self.uid += 1
        nb = 1
        for d in shape[1:]: nb *= d
        nb *= (4 if dt == F32 else 2)
        nb = (nb + 63) // 64 * 64
        side = self.side
        if side == 0:
            off = self.lo; self.lo += nb
        else:
            self.hi -= nb; off = self.hi
        assert self.lo <= self.hi, f"SBUF overflow lo={self.lo} hi={self.hi}"
        self.peak = max(self.peak, (self.lo - SB_LO) + (SB_HI - self.hi))
        tl = Tl(self.nc.alloc_sbuf_tensor_at(f"t{self.uid}", list(shape), dt, offset=off))
        s0, e0 = off, off + nb
        newg = []
        for (gs, ge, ev) in self.ghosts:
            if ge <= s0 or gs >= e0:
                newg.append((gs, ge, ev)); continue
            for k, v in ev.items():
                if tl.b.r.get(k, 0) < v: tl.b.r[k] = v
            if gs < s0: newg.append((gs, s0, ev))
            if ge > e0: newg.append((e0, ge, ev))
        self.ghosts = newg
        stack.callback(self._free, tl, off, nb, side)
        return tl

    def _free(self, tl, off, nb, side):
        ev = dict(tl.b.r)
        if tl.b.w is not None and ev.get(tl.b.w[0], 0) < tl.b.w[1]: ev[tl.b.w[0]] = tl.b.w[1]
        self.ghosts.append((off, off + nb, ev))
        if side == 0:
            assert self.lo == off + nb, "non-LIFO free (lo)"; self.lo = off
        else:
            assert self.hi == off, "non-LIFO free (hi)"; self.hi = off + nb

    def dbuf(self, *key):
        b = self.dbufs.get(key)
        if b is None:
            b = self.dbufs[key] = Buf()
        return b

    def psum(self):
        t = self.psums[self.psn % self.rot]; self.psn += 1
        return t

    def acc(self):
        t = self.psums[6 + self.accn % 2]; self.accn += 1
        return t

    def _wait(self, e, k, v):
        kn = self.known[e]
        if kn.get(k, 0) >= v: return
        self.eng[e].wait_ge(self.semh[k], v); kn[k] = v

    def _waits(self, e, reads, writes):
        need = {}
        for b in reads:
            if b.w is not None and need.get(b.w[0], 0) < b.w[1]: need[b.w[0]] = b.w[1]
        for b in writes:
            if b.w is not None and need.get(b.w[0], 0) < b.w[1]: need[b.w[0]] = b.w[1]
            for k, v in b.r.items():
                if need.get(k, 0) < v: need[k] = v
        for k, v in need.items():
            if k == e and (not self.ses or e == "pe"): continue
            self._wait(e, k, v)

    def _record(self, ev, reads, writes):
        k, v = ev
        for b in reads:
            if b.r.get(k, 0) < v: b.r[k] = v
        for b in writes:
            b.w = ev; b.r = {}

    def op(self, e, reads, writes, fn):
        self._waits(e, reads, writes)
        ins = fn(self.eng[e])
        self.cnt[e] += 1
        ins.then_inc(self.semh[e], 1)
        self._record((e, self.cnt[e]), reads, writes)

    def dma(self, q, out, in_, reads, writes):
        i = self.dnext; self.dnext = (i + 1) % self.nd
        key = ("d", i)
        if self.cnt[key] > 0: self._wait(q, key, self.cnt[key])
        self._waits(q, reads, writes)
        self.cnt[key] += 16
        self.eng[q].dma_start(out=out, in_=in_).then_inc(self.semh[key], 16)
        self._record((key, self.cnt[key]), reads, writes)

    def barrier(self):
        for e in self.eng:
            for k, c in self.cnt.items():
                if c > 0 and not (k == e and e == "pe"): self._wait(e, k, c)

    @contextmanager
    def phase(self, flip=False):
        if flip: self.side = 1 - self.side
        with ExitStack() as ph:
            yield ph


class Tl:
    __slots__ = ("t", "b", "__weakref__")

    def __init__(self, t):
        self.t = t; self.b = Buf()


class Model:
    def __init__(self, nc, kb, st, dbg=False):
        self.nc = nc; self.kb = kb; self.dbg = dbg
        dt = lambda n, s, d, k=("ExternalOutput" if dbg else "Internal"): nc.dram_tensor(n, list(s), d, kind=k).ap()
        ein = lambda n, s, d=F32: dt(n, s, d, "ExternalInput")
        self.I = {
            "xT": ein("xT", [D, L]), "memT": ein("memT", [D, NM]), "pv": ein("pv", [NL, 128, NV]),
            "w_in": ein("w_in", [NL, D, PW]), "hy_w1": ein("hy_w1", [NL, 33, 64]), "hy_w2": ein("hy_w2", [NL, 64, 64]),
            "hy_w3": ein("hy_w3", [NL, 64, 2048]), "natab": ein("natab", [NL, 128, 8 * NS * 64]),
            "narm": ein("narm", [128, 20 * 512], BF16), "nasel": ein("nasel", [128, 128], BF16),
            "w_branch": ein("w_branch", [NL, 3, 512, D]), "w_out": ein("w_out", [NL, D, D]),
            "xa_wq": ein("xa_wq", [NL, D, D]), "xa_wkv": ein("xa_wkv", [NL, D, 2 * D]), "xa_wo": ein("xa_wo", [NL, D, D]),
            "ffn_up": ein("ffn_up", [NL, D, 2 * DFF]), "ffn_down": ein("ffn_down", [NL, DFF, D]),
            "CTb": ein("CTb", [8, 128, 8, 128], BF16), "STb": ein("STb", [8, 128, 8, 128], BF16),
            "CoTb": ein("CoTb", [8, 128, 8, 128], BF16), "SoTb": ein("SoTb", [8, 128, 8, 128], BF16),
            "Cm": ein("Cm", [2, 128, 8, 512], BF16), "Sm": ein("Sm", [2, 128, 8, 512], BF16),
            "tw": ein("tw", [128, 16]),
            "zT": ein("zT", [33, L]), "decay": ein("decay", [L, 512]), "decayb": ein("decayb", [L, 512]),
            "identf": ein("identf", [128, 128]), "identb": ein("identb", [128, 128], BF16),
        }
        self.out = dt("outT", [D, L], F32, "ExternalOutput")
        self.S = {
            "hyuT": dt("hyuT", [1536, L], F32), "qT": dt("qT", [512, L], BF16), "kT": dt("kT", [512, L], BF16),
            "vaug": dt("vaug", [16, 128, 1024], BF16), "scuT": dt("scuT", [1536, L], F32),
            "gatesT": dt("gatesT", [3072, L], BF16), "kfr": dt("kfr", [L, 1024], F32), "kfi": dt("kfi", [L, 1024], F32),
            "yaT": dt("yaT", [512, L], BF16), "ybT": dt("ybT", [512, L], BF16), "ycT": dt("ycT", [512, L], BF16),
            "mergedT": dt("mergedT", [D, L], BF16), "hsd": dt("hsd", [2, L, 1024], BF16), "hT": dt("hT", [D, L], BF16),
        }
        mk = lambda shape, d: kb.tile(st, shape, d)
        self.onesb = mk([128, 128], BF16); self.identb = mk([128, 128], BF16); self.identf = mk([128, 128], F32)
        self.onesf = mk([1, 64], F32); self.epst = mk([128, 1], F32)
        self.pv = [mk([128, NV], F32) for _ in range(NL)]
        self.tw = mk([128, 16], F32)
        kb.dma("sp", self.tw.t[:], self.I["tw"][:, :], [], [self.tw.b])
        self.nidentb = mk([128, 128], BF16)
        kb.op("dve", [], [self.onesb.b], lambda v: v.memset(self.onesb.t[:], 1.0))
        kb.op("dve", [], [self.onesf.b], lambda v: v.memset(self.onesf.t[:], 1.0))
        kb.op("dve", [], [self.epst.b], lambda v: v.memset(self.epst.t[:], EPS))
        kb.dma("sp", self.identb.t[:], self.I["identb"][:, :], [], [self.identb.b])
        kb.dma("sp", self.identf.t[:], self.I["identf"][:, :], [], [self.identf.b])
        kb.op("dve", [self.identb.b], [self.nidentb.b], lambda v: v.tensor_scalar(out=self.nidentb.t[:], in0=self.identb.t[:], scalar1=-1.0, scalar2=None, op0=ALU.mult))
        for l in range(NL):
            kb.dma("sp", self.pv[l].t[:], self.I["pv"][l], [], [self.pv[l].b])
        for i in range(8):
            kb.psums.append(Tl(st.enter_context(nc.psum_tensor(f"psum{i}", [128, 512], F32))))

    def tiles(self, ph, n, shape, d):
        return [self.kb.tile(ph, shape, d) for _ in range(n)]

    def load_fm(self, ph, src, nchunks, T, d, q="sp", deps=None):
        ts = self.tiles(ph, nchunks, [128, T], d)
        for c, t in enumerate(ts):
            self.kb.dma(q, t.t[:], src[c * 128:(c + 1) * 128, :], deps(c) if deps else [], [t.b])
        return ts

    def xdeps(self, c):
        return [self.kb.dbuf("x", c)]

    def rms_stats(self, ph, xs, T, sq_tiles):
        kb = self.kb
        rstd = kb.tile(ph, [128, T], F32)
        n = len(xs)
        for t0 in range(0, T, 512):
            w = min(512, T - t0)
            ps = kb.psum()
            for c, x in enumerate(xs):
                sq = sq_tiles[c % len(sq_tiles)]
                kb.op("act", [x.b], [sq.b], lambda a, x=x, sq=sq: a.activation(out=sq.t[:, 0:w], in_=x.t[:, t0:t0 + w], func=AF.Square))
                kb.op("pe", [sq.b, self.onesb.b], [ps.b], lambda p, sq=sq, c=c: p.matmul(ps.t[:, 0:w], lhsT=self.onesb.t[:, :], rhs=sq.t[:, 0:w], start=(c == 0), stop=(c == n - 1)))
            kb.op("act", [ps.b, self.epst.b], [rstd.b], lambda a: a.activation(out=rstd.t[:, t0:t0 + w], in_=ps.t[:, 0:w], func=AF.Ln, bias=self.epst.t[:, 0:1], scale=1.0 / (128 * n)))
        kb.op("act", [rstd.b], [rstd.b], lambda a: a.activation(out=rstd.t[:, :], in_=rstd.t[:, :], func=AF.Exp, scale=-0.5))
        return rstd

    def norm_apply(self, tmp, xs, hs, T, pvt, gcol):
        kb = self.kb
        sq = self.tiles(tmp, 3, [128, 512], BF16)
        rstd = self.rms_stats(tmp, xs, T, sq)
        for c, (x, h) in enumerate(zip(xs, hs)):
            kb.op("dve", [x.b, rstd.b, pvt.b], [h.b], lambda v, x=x, h=h, c=c: v.scalar_tensor_tensor(
                out=h.t[:, :], in0=x.t[:, :], scalar=pvt.t[:, gcol + c:gcol + c + 1], in1=rstd.t[:, :], op0=ALU.mult, op1=ALU.mult))

    def prefetch_w(self, ph, w, K, c0, ncols, cgw, nbuf):
        kb = self.kb; KC = K // 128
        wb = self.tiles(ph, nbuf, [128, KC, cgw], BF16)
        n = 0
        for g0 in list(range(0, ncols, cgw))[:nbuf]:
            gw = min(cgw, ncols - g0)
            kb.dma("pool", wb[n].t[:, :, 0:gw], w[0:K, c0 + g0:c0 + g0 + gw].rearrange("(kc p) n -> p kc n", p=128), [], [wb[n].b])
            n += 1
        return (wb, n)

    def linear_fm(self, ph, w, K, c0, ncols, srcs, T, epi, cgw=512, nbuf=3, wpre=None):
        kb = self.kb; KC = K // 128
        if wpre is not None:
            wb, npre = wpre; nbuf = len(wb)
        else:
            wb = self.tiles(ph, nbuf, [128, KC, cgw], BF16); npre = 0
        gi = 0
        for g0 in range(0, ncols, cgw):
            gw = min(cgw, ncols - g0)
            wt = wb[gi % nbuf]; gi += 1
            if gi > npre:
                kb.dma("pool", wt.t[:, :, 0:gw], w[0:K, c0 + g0:c0 + g0 + gw].rearrange("(kc p) n -> p kc n", p=128), [], [wt.b])
            for o0 in range(0, gw, 128):
                oc = (g0 + o0) // 128
                for t0 in range(0, T, 512):
                    tw = min(512, T - t0)
                    ps = kb.psum()

                    def mm(p, wt=wt, o0=o0, t0=t0, tw=tw, ps=ps):
                        for kc in range(KC):
                            ins = p.matmul(ps.t[:, 0:tw], lhsT=wt.t[:, kc, o0:o0 + 128], rhs=srcs[kc].t[:, t0:t0 + tw], start=(kc == 0), stop=(kc == KC - 1))
                        return ins
                    kb.op("pe", [wt.b] + [s.b for s in srcs], [ps.b], mm)
                    epi(oc, t0, tw, ps)

    def linear_tm(self, ph, w, K, c0, ncols, srcs, T, epi, nbuf=2):
        kb = self.kb; KC = K // 128
        wb = self.tiles(ph, nbuf, [128, KC, 512], BF16)
        for gi, g0 in enumerate(range(0, ncols, 512)):
            gw = min(512, ncols - g0)
            wt = wb[gi % nbuf]
            kb.dma("pool", wt.t[:, :, 0:gw], w[0:K, c0 + g0:c0 + g0 + gw].rearrange("(kc p) n -> p kc n", p=128), [], [wt.b])
            for tc in range(T // 128):
                ps = kb.psum()

                def mm(p, wt=wt, tc=tc, gw=gw, ps=ps):
                    for kc in range(KC):
                        ins = p.matmul(ps.t[:, 0:gw], lhsT=srcs[kc].t[:, tc * 128:(tc + 1) * 128], rhs=wt.t[:, kc, 0:gw], start=(kc == 0), stop=(kc == KC - 1))
                    return ins
                kb.op("pe", [wt.b] + [s.b for s in srcs], [ps.b], mm)
                epi(tc, gi, gw, ps)

    def conv3(self, ub, ob, pvt, col_fn, T=L, center_done=False):
        kb = self.kb
        if not center_done:
            kb.op("dve", [ub.b, pvt.b], [ob.b], lambda v: v.tensor_scalar(out=ob.t[:, 0:T], in0=ub.t[:, 1:1 + T], scalar1=pvt.t[:, col_fn(1):col_fn(1) + 1], scalar2=None, op0=ALU.mult))
        for k in (0, 2):
            kb.op("dve", [ub.b, pvt.b, ob.b], [ob.b], lambda v, k=k: v.scalar_tensor_tensor(
                out=ob.t[:, 0:T], in0=ub.t[:, k:k + T], scalar=pvt.t[:, col_fn(k):col_fn(k) + 1], in1=ob.t[:, 0:T], op0=ALU.mult, op1=ALU.add))

    def padded(self, ph, n, T=L):
        kb = self.kb
        ts = self.tiles(ph, n, [128, T + 2], F32)
        for t in ts:
            kb.op("dve", [], [t.b], lambda v, t=t: v.memset(t.t[:, :], 0.0))
        return ts

    def proj_norm_resid(self, srcs, w, K, pvt, gcol, cgw=512, final=False, nxt=None, nxb=4, wpre=None):
        kb = self.kb
        with kb.phase() as ph:
            o = self.tiles(ph, 8, [128, L], F32)
            sq = self.tiles(ph, 4, [128, 512], BF16)
            stat = kb.psums[4:8]
            xb = self.tiles(ph, nxb, [128, L], F32)
            if nxb == 8:
                for c in range(8):
                    kb.dma("sp", xb[c].t[:], self.xcur[c * 128:(c + 1) * 128, :], [kb.dbuf("x", c)], [xb[c].b])
            old_rot = kb.rot; kb.rot = 4
            pend = []; nsq = [0]

            def flush(keep):
                while len(pend) > keep:
                    s_, tg, first, last = pend.pop(0)
                    kb.op("pe", [s_.b, self.onesb.b], [stat[tg].b], lambda p: p.matmul(stat[tg].t[:, :], lhsT=self.onesb.t[:, :], rhs=s_.t[:, :], start=first, stop=last))

            def epi(oc, t0, tw, ps):
                tg = t0 // 512
                kb.op("act", [ps.b], [o[oc].b], lambda a: a.activation(out=o[oc].t[:, t0:t0 + tw], in_=ps.t[:, 0:tw], func=AF.Copy))
                s_ = sq[nsq[0] % 4]; nsq[0] += 1
                kb.op("act", [ps.b], [s_.b], lambda a: a.activation(out=s_.t[:, :], in_=ps.t[:, 0:tw], func=AF.Square))
                pend.append((s_, tg, oc == 0, oc == 7))
                flush(2)
            with kb.phase() as ph2:
                self.linear_fm(ph2, w, K, 0, D, srcs, L, epi, cgw=cgw, nbuf=2, wpre=wpre)
            flush(0)
            rstd = kb.tile(ph, [128, L], F32)
            for tg in range(4):
                kb.op("act", [stat[tg].b, self.epst.b], [rstd.b], lambda a: a.activation(out=rstd.t[:, tg * 512:(tg + 1) * 512], in_=stat[tg].t[:, :], func=AF.Ln, bias=self.epst.t[:, 0:1], scale=1.0 / D))
            kb.op("act", [rstd.b], [rstd.b], lambda a: a.activation(out=rstd.t[:, :], in_=rstd.t[:, :], func=AF.Exp, scale=-0.5))
            dst = self.out if final else self.xs
            for c in range(8):
                x = xb[c % nxb]
                if nxb != 8:
                    kb.dma("sp", x.t[:], self.xcur[c * 128:(c + 1) * 128, :], [kb.dbuf("x", c)], [x.b])
                kb.op("dve", [o[c].b, rstd.b], [o[c].b], lambda v: v.tensor_tensor(out=o[c].t[:, :], in0=o[c].t[:, :], in1=rstd.t[:, :], op=ALU.mult))
                kb.op("dve", [o[c].b, x.b, pvt.b], [o[c].b], lambda v: v.scalar_tensor_tensor(
                    out=o[c].t[:, :], in0=o[c].t[:, :], scalar=pvt.t[:, gcol + c:gcol + c + 1], in1=x.t[:, :], op0=ALU.mult, op1=ALU.add))
                kb.dma("sp" if nxb == 8 else "pool", dst[c * 128:(c + 1) * 128, :], o[c].t[:], [o[c].b], [kb.dbuf("xo" if final else "x", c)])
                if nxt is not None:
                    for tg in range(4):
                        s_ = sq[nsq[0] % 4]; nsq[0] += 1
                        kb.op("act", [o[c].b], [s_.b], lambda a: a.activation(out=s_.t[:, :], in_=o[c].t[:, tg * 512:(tg + 1) * 512], func=AF.Square))
                        pend.append((s_, tg, c == 0, c == 7))
                    flush(0)
            self.xcur = self.xs
            if nxt is not None:
                flush(0)
                pvn, gn = nxt
                for tg in range(4):
                    kb.op("act", [stat[tg].b, self.epst.b], [rstd.b], lambda a: a.activation(out=rstd.t[:, tg * 512:(tg + 1) * 512], in_=stat[tg].t[:, :], func=AF.Ln, bias=self.epst.t[:, 0:1], scale=1.0 / D))
                kb.op("act", [rstd.b], [rstd.b], lambda a: a.activation(out=rstd.t[:, :], in_=rstd.t[:, :], func=AF.Exp, scale=-0.5))
                hst = self.tiles(ph, 2, [128, L], BF16)
                for c in range(8):
                    h = hst[c % 2]
                    kb.op("dve", [o[c].b, rstd.b, pvn.b], [h.b], lambda v: v.scalar_tensor_tensor(
                        out=h.t[:, :], in0=o[c].t[:, :], scalar=pvn.t[:, gn + c:gn + c + 1], in1=rstd.t[:, :], op0=ALU.mult, op1=ALU.mult))
                    kb.dma("sp", self.S["hT"][c * 128:(c + 1) * 128, :], h.t[:], [h.b], [kb.dbuf("h", c)])
            kb.rot = old_rot

    def proj_norm_resid_tg(self, srcs, w, pvt, gcol, nxt, wpre):
        kb = self.kb
        wb, npre = wpre
        pvn, gn = nxt
        with kb.phase() as ph:
            o = [self.tiles(ph, 4, [128, 512], F32) for _ in range(8)]
            sq = self.tiles(ph, 8, [128, 512], BF16)
            xb = self.tiles(ph, 8, [128, L], F32)
            for c in range(8):
                kb.dma("sp", xb[c].t[:], self.xcur[c * 128:(c + 1) * 128, :], [kb.dbuf("x", c)], [xb[c].b])
            pe_pend = []; tick = [0]

            def pe_flush(min_age):
                while pe_pend and tick[0] - pe_pend[0][0] >= min_age:
                    pe_pend.pop(0)[1]()
            rs1 = self.tiles(ph, 4, [128, 512], F32); rs2 = rs1
            hst = self.tiles(ph, 4, [128, 512], BF16)
            stat = kb.psums[4:8]
            old_rot = kb.rot; kb.rot = 4
            nsq = [0]; nh = [0]

            def rstd_from(st_, dstt):
                kb.op("act", [st_.b, self.epst.b], [dstt.b], lambda a: a.activation(out=dstt.t[:, :], in_=st_.t[:, :], func=AF.Ln, bias=self.epst.t[:, 0:1], scale=1.0 / D))
                kb.op("act", [dstt.b], [dstt.b], lambda a: a.activation(out=dstt.t[:, :], in_=dstt.t[:, :], func=AF.Exp, scale=-0.5))

            def post(tg):
                tsl = slice(tg * 512, (tg + 1) * 512)
                rstd_from(stat[tg], rs1[tg])
                yield
                for c in range(8):
                    oc_ = o[c][tg]
                    kb.op("dve", [oc_.b, rs1[tg].b], [oc_.b], lambda v: v.tensor_tensor(out=oc_.t[:, :], in0=oc_.t[:, :], in1=rs1[tg].t[:, :], op=ALU.mult))
                    kb.op("dve", [oc_.b, xb[c].b, pvt.b], [oc_.b], lambda v: v.scalar_tensor_tensor(
                        out=oc_.t[:, :], in0=oc_.t[:, :], scalar=pvt.t[:, gcol + c:gcol + c + 1], in1=xb[c].t[:, tsl], op0=ALU.mult, op1=ALU.add))
                    kb.dma("sp", self.xs[c * 128:(c + 1) * 128, tsl], oc_.t[:, :], [oc_.b], [kb.dbuf("x", c)])
                    s_ = sq[nsq[0] % 8]; nsq[0] += 1
                    kb.op("act", [oc_.b], [s_.b], lambda a: a.activation(out=s_.t[:, :], in_=oc_.t[:, :], func=AF.Square))
                    pe_pend.append((tick[0], lambda s_=s_, c=c: kb.op("pe", [s_.b, self.onesb.b], [stat[tg].b], lambda p: p.matmul(stat[tg].t[:, :], lhsT=self.onesb.t[:, :], rhs=s_.t[:, :], start=(c == 0), stop=(c == 7)))))
                    while len(pe_pend) > 4: pe_pend.pop(0)[1]()
                    yield
                pe_flush(0)
                rstd_from(stat[tg], rs2[tg])
                yield
                for c in range(8):
                    oc_ = o[c][tg]; h = hst[nh[0] % 4]; nh[0] += 1
                    kb.op("dve", [oc_.b, rs2[tg].b, pvn.b], [h.b], lambda v: v.scalar_tensor_tensor(
                        out=h.t[:, :], in0=oc_.t[:, :], scalar=pvn.t[:, gn + c:gn + c + 1], in1=rs2[tg].t[:, :], op0=ALU.mult, op1=ALU.mult))
                    kb.dma("sp", self.S["hT"][c * 128:(c + 1) * 128, tsl], h.t[:, :], [h.b], [kb.dbuf("h", c)])
                    yield

            gens = []

            def step_all(k=1):
                for _ in range(k):
                    for g_ in list(gens):
                        try:
                            next(g_)
                        except StopIteration:
                            gens.remove(g_)
            for tg in range(4):
                pend = []
                for oc in range(8):
                    wt = wb[oc // 4]; o0 = (oc % 4) * 128
                    ps = kb.psum()

                    def mm(p):
                        for kc in range(8):
                            ins = p.matmul(ps.t[:, :], lhsT=wt.t[:, kc, o0:o0 + 128], rhs=srcs[kc].t[:, tg * 512:(tg + 1) * 512], start=(kc == 0), stop=(kc == 7))
                        return ins
                    kb.op("pe", [wt.b] + [s_.b for s_ in srcs], [ps.b], mm)
                    ot = o[oc][tg]
                    kb.op("act", [ps.b], [ot.b], lambda a: a.activation(out=ot.t[:, :], in_=ps.t[:, :], func=AF.Copy))
                    s_ = sq[nsq[0] % 8]; nsq[0] += 1
                    kb.op("act", [ps.b], [s_.b], lambda a: a.activation(out=s_.t[:, :], in_=ps.t[:, :], func=AF.Square))
                    pend.append((s_, oc))
                    tick[0] += 1
                    pe_flush(2)
                    if len(pend) > 1:
                        s2, oc2 = pend.pop(0)
                        kb.op("pe", [s2.b, self.onesb.b], [stat[tg].b], lambda p: p.matmul(stat[tg].t[:, :], lhsT=self.onesb.t[:, :], rhs=s2.t[:, :], start=(oc2 == 0), stop=(oc2 == 7)))
                    step_all(1 + (oc % 2))
                s2, oc2 = pend.pop(0)
                kb.op("pe", [s2.b, self.onesb.b], [stat[tg].b], lambda p: p.matmul(stat[tg].t[:, :], lhsT=self.onesb.t[:, :], rhs=s2.t[:, :], start=(oc2 == 0), stop=(oc2 == 7)))
                gens.append(post(tg))
            while gens:
                step_all(); tick[0] += 1; pe_flush(1)
            pe_flush(0)
            kb.rot = old_rot
        self.xcur = self.xs

    def load_h(self, ph):
        return self.load_fm(ph, self.S["hT"], 8, L, BF16, deps=lambda c: [self.kb.dbuf("h", c)])

    def phase_proj(self, l):
        kb = self.kb; S = self.S; pvt = self.pv[l]; w = self.I["w_in"][l]
        with kb.phase(flip=True) as ph:
            if l == 0:
                hs = self.tiles(ph, 8, [128, L], BF16)
                with kb.phase() as phx:
                    xs = self.load_fm(phx, self.xcur, 8, L, F32, deps=self.xdeps)
                    self.norm_apply(phx, xs, hs, L, pvt, cG(0))
            else:
                hs = self.load_h(ph)
            ub = self.padded(ph, 2); ob = self.tiles(ph, 2, [128, L], F32)
            stf = self.tiles(ph, 4, [128, 512], F32); stb = self.tiles(ph, 4, [128, 512], BF16)
            cnt = {"f": 0, "b": 0, "e": 0}

            gst = ExitStack()
            self.bg_gen = self.filters_gen(l, gst)

            def epi(oc, t0, tw, ps):
                tg = t0 // 512
                cnt["e"] += 1
                if cnt["e"] > 48 and cnt["e"] % 4 == 0: self.bg()
                if oc < 12:
                    u = ub[oc % 2]; o = ob[oc % 2]
                    kb.op("act", [ps.b], [u.b], lambda a: a.activation(out=u.t[:, 1 + t0:1 + t0 + tw], in_=ps.t[:, 0:tw], func=AF.Copy))
                    kb.op("act", [ps.b, pvt.b], [o.b], lambda a: a.activation(out=o.t[:, t0:t0 + tw], in_=ps.t[:, 0:tw], func=AF.Identity, bias=0.0, scale=pvt.t[:, cHSW(1, oc):cHSW(1, oc) + 1]))
                    if tg == 3:
                        self.conv3(u, o, pvt, lambda k: cHSW(k, oc), center_done=True)
                        kb.dma("sp", S["hyuT"][oc * 128:(oc + 1) * 128, :], o.t[:], [o.b], [kb.dbuf("hyu", oc)])
                elif oc < 20:
                    s = stb[cnt["b"] % 4]; cnt["b"] += 1
                    sc = 0.125 if oc < 16 else 1.0
                    kb.op("act", [ps.b], [s.b], lambda a: a.activation(out=s.t[:, 0:tw], in_=ps.t[:, 0:tw], func=AF.Identity, bias=0.0, scale=sc))
                    dst = S["qT"] if oc < 16 else S["kT"]; r = (oc - 12) % 4
                    kb.dma("act", dst[r * 128:(r + 1) * 128, t0:t0 + tw], s.t[:, 0:tw], [s.b], [kb.dbuf("qk", oc, tg)])
                elif oc < 24:
                    raise AssertionError
                elif oc < 36:
                    s = stf[cnt["f"] % 4]; cnt["f"] += 1
                    kb.op("act", [ps.b], [s.b], lambda a: a.activation(out=s.t[:, 0:tw], in_=ps.t[:, 0:tw], func=AF.Copy))
                    r = oc - 24
                    kb.dma("act", S["scuT"][r * 128:(r + 1) * 128, t0:t0 + tw], s.t[:, 0:tw], [s.b], [kb.dbuf("scu", r, tg)])
                else:
                    s = stb[cnt["b"] % 4]; cnt["b"] += 1
                    r = oc - 36
                    kb.op("act", [ps.b, pvt.b], [s.b], lambda a: a.activation(out=s.t[:, 0:tw], in_=ps.t[:, 0:tw], func=AF.Sigmoid, bias=pvt.t[:, cGB + r:cGB + r + 1], scale=1.0))
                    kb.dma("act", S["gatesT"][r * 128:(r + 1) * 128, t0:t0 + tw], s.t[:, 0:tw], [s.b], [kb.dbuf("gates", r, tg)])
            with kb.phase() as ph2:
                self.linear_fm(ph2, w, D, 0, 2560, hs, L, epi)
            with kb.phase() as ph2:
                self.linear_fm(ph2, w, D, 3072, PW - 3072, hs, L, lambda oc, t0, tw, ps: epi(oc + 24, t0, tw, ps))
            with kb.phase() as ph2:
                va = self.tiles(ph2, 3, [128, 8, 128], BF16)
                for t in va:
                    kb.op("dve", [], [t.b], lambda v, t=t: v.memset(t.t[:, :, :], 1.0))

                def epiv(tc, gi, gw, ps):
                    t = va[tc % 3]
                    kb.op("act", [ps.b], [t.b], lambda a: a.activation(out=t.t[:, :, 0:64], in_=ps.t[:, :].rearrange("p (h d) -> p h d", h=8), func=AF.Copy))
                    kb.dma("act", S["vaug"][tc].rearrange("p (h d) -> p h d", h=8), t.t[:, :, :], [t.b], [kb.dbuf("vaug", tc)])
                self.linear_tm(ph2, w, D, 2560, 512, hs, L, epiv)
            while self.bg_gen is not None: self.bg()
            kb.side ^= 1
            gst.close()
            kb.side ^= 1

    def sin_act(self, a, m, ps, w, fcol, bcol, pvt, dst, t0):
        kb = self.kb
        kb.op("dve", [ps.b, pvt.b], [a.b], lambda v: v.tensor_scalar(out=a.t[:, 0:w], in0=ps.t[0:64, 0:w], scalar1=pvt.t[0:64, bcol:bcol + 1], scalar2=pvt.t[0:64, fcol:fcol + 1], op0=ALU.add, op1=ALU.mult))
        for it in range(2):
            kb.op("dve", [a.b], [m.b], lambda v: v.tensor_scalar(out=m.t[:, 0:w], in0=a.t[:, 0:w], scalar1=float(np.pi), scalar2=float(-2 * np.pi), op0=ALU.is_gt, op1=ALU.mult))
            kb.op("dve", [a.b, m.b], [a.b], lambda v: v.tensor_tensor(out=a.t[:, 0:w], in0=a.t[:, 0:w], in1=m.t[:, 0:w], op=ALU.add))
            kb.op("dve", [a.b], [m.b], lambda v: v.tensor_scalar(out=m.t[:, 0:w], in0=a.t[:, 0:w], scalar1=float(-np.pi), scalar2=float(2 * np.pi), op0=ALU.is_lt, op1=ALU.mult))
            kb.op("dve", [a.b, m.b], [a.b], lambda v: v.tensor_tensor(out=a.t[:, 0:w], in0=a.t[:, 0:w], in1=m.t[:, 0:w], op=ALU.add))
        kb.op("act", [a.b], [dst.b], lambda c: c.activation(out=dst.t[0:64, t0:t0 + w], in_=a.t[:, 0:w], func=AF.Sin))

    def filters_gen(self, l, p1):
        kb = self.kb; S = self.S; I = self.I; pvt = self.pv[l]
        kb.side ^= 1
        zT = kb.tile(p1, [33, L], F32); w1 = kb.tile(p1, [33, 64], F32); w2 = kb.tile(p1, [64, 64], F32)
        w3 = kb.tile(p1, [64, 2048], BF16)
        h1 = kb.tile(p1, [64, L], F32); h2 = kb.tile(p1, [64, L], F32); h2b = kb.tile(p1, [64, L], BF16)
        sa = kb.tile(p1, [64, 512], F32); sm = kb.tile(p1, [64, 512], F32)
        dec = self.tiles(p1, 2, [128, 512], F32); decb = self.tiles(p1, 2, [128, 512], F32)
        tf = self.tiles(p1, 2, [128, 512], F32); tb = self.tiles(p1, 2, [128, 512], F32)
        hso = self.tiles(p1, 2, [128, 1024], BF16); hdo = self.tiles(p1, 2, [128, 1024], BF16)
        kb.side ^= 1
        kb.dma("sp", zT.t[:], I["zT"][:, :], [], [zT.b]); kb.dma("sp", w1.t[:], I["hy_w1"][l], [], [w1.b])
        kb.dma("sp", w2.t[:], I["hy_w2"][l], [], [w2.b]); kb.dma("pool", w3.t[:], I["hy_w3"][l], [], [w3.b])
        yield
        for src, wt, dst, fc, bc in ((zT, w1, h1, cF0, cB1), (h1, w2, h2, cF1, cB2)):
            kk = 33 if src is zT else 64
            for t0 in range(0, L, 512):
                ps = kb.psum()
                kb.op("pe", [src.b, wt.b], [ps.b], lambda p: p.matmul(ps.t[0:64, :], lhsT=wt.t[0:kk, :], rhs=src.t[0:kk, t0:t0 + 512], start=True, stop=True))
                self.sin_act(sa, sm, ps, 512, fc, bc, pvt, dst, t0)
                yield
        kb.op("act", [h2.b], [h2b.b], lambda a: a.activation(out=h2b.t[:, :], in_=h2.t[:, :], func=AF.Copy))
        yield
        for jc in range(16):
            d = dec[jc % 2]; db = decb[jc % 2]; hs_ = hso[jc % 2]; hd_ = hdo[jc % 2]
            kb.dma("sp", d.t[:], I["decay"][jc * 128:(jc + 1) * 128, :], [], [d.b])
            kb.dma("sp", db.t[:], I["decayb"][jc * 128:(jc + 1) * 128, :], [], [db.b])
            for o in range(2):
                psf = kb.psum(); psb = kb.psum()
                par = jc // 8; mc = jc % 8
                taps = h2b.t[0:64, mc * 256 + par:mc * 256 + 256:2]
                kb.op("pe", [h2b.b, w3.b], [psf.b], lambda p: p.matmul(psf.t[:, :], lhsT=taps, rhs=w3.t[0:64, o * 512:(o + 1) * 512], start=True, stop=True))
                kb.op("pe", [h2b.b, w3.b], [psb.b], lambda p: p.matmul(psb.t[:, :], lhsT=taps, rhs=w3.t[0:64, 1024 + o * 512:1024 + (o + 1) * 512], start=True, stop=True))
                f = tf[o]; b = tb[o]
                kb.op("dve", [psf.b, d.b], [f.b], lambda v: v.tensor_tensor(out=f.t[:, :], in0=psf.t[:, :], in1=d.t[:, :], op=ALU.mult))
                kb.op("dve", [psb.b, db.b], [b.b], lambda v: v.tensor_tensor(out=b.t[:, :], in0=psb.t[:, :], in1=db.t[:, :], op=ALU.mult))
                kb.op("dve", [f.b, b.b], [hs_.b], lambda v: v.tensor_tensor(out=hs_.t[:, o * 512:(o + 1) * 512], in0=f.t[:, :], in1=b.t[:, :], op=ALU.add))
                kb.op("dve", [f.b, b.b], [hd_.b], lambda v: v.tensor_tensor(out=hd_.t[:, o * 512:(o + 1) * 512], in0=b.t[:, :], in1=f.t[:, :], op=ALU.subtract))
                yield
            kb.dma("sp", S["hsd"][0, jc * 128:(jc + 1) * 128, :], hs_.t[:], [hs_.b], [kb.dbuf("hsd", 0, jc)])
            kb.dma("sp", S["hsd"][1, jc * 128:(jc + 1) * 128, :], hd_.t[:], [hd_.b], [kb.dbuf("hsd", 1, jc)])

    def bg(self):
        if self.bg_gen is not None:
            try:
                next(self.bg_gen)
            except StopIteration:
                self.bg_gen = None

    def phase_filters(self, l):
        kb = self.kb; S = self.S; I = self.I; pvt = self.pv[l]
        with kb.phase(flip=True) as ph:
            hsum = self.tiles(ph, 16, [128, 1024], BF16); hdif = self.tiles(ph, 16, [128, 1024], BF16)
            for jc in range(16):
                kb.dma("sp", hsum[jc].t[:], S["hsd"][0, jc * 128:(jc + 1) * 128, :], [kb.dbuf("hsd", 0, jc)], [hsum[jc].b])
                kb.dma("sp", hdif[jc].t[:], S["hsd"][1, jc * 128:(jc + 1) * 128, :], [kb.dbuf("hsd", 1, jc)], [hdif[jc].b])
            ct = self.tiles(ph, 2, [128, 8, 128], BF16); stt = self.tiles(ph, 2, [128, 8, 128], BF16)
            cot = self.tiles(ph, 2, [128, 8, 128], BF16); sot = self.tiles(ph, 2, [128, 8, 128], BF16)
            t1 = self.tiles(ph, 2, [128, 512], F32); t3 = self.tiles(ph, 2, [128, 512], F32)
            n1 = self.tiles(ph, 2, [128, 512], F32); t2 = self.tiles(ph, 2, [128, 512], F32)
            og_ = self.tiles(ph, 8, [128, 512], F32); n = 0; it = 0
            tw = self.tw
            for kc in range(8):
                c = ct[kc % 2]; s_ = stt[kc % 2]
                kb.dma("sp", c.t[:], I["CTb"][kc], [], [c.b]); kb.dma("sp", s_.t[:], I["STb"][kc], [], [s_.b])
                co = cot[kc % 2]; so = sot[kc % 2]
                kb.dma("sp", co.t[:], I["CoTb"][kc], [], [co.b]); kb.dma("sp", so.t[:], I["SoTb"][kc], [], [so.b])
                for og in range(2):
                    def tr(mat, src, par):
                        ps = kb.psum()

                        def mm(p):
                            for mc in range(8):
                                ins = p.matmul(ps.t[:, :], lhsT=mat.t[:, mc, :], rhs=src[par * 8 + mc].t[:, og * 512:(og + 1) * 512], start=(mc == 0), stop=(mc == 7))
                            return ins
                        kb.op("pe", [mat.b] + [x.b for x in src[par * 8:par * 8 + 8]], [ps.b], mm)
                        return ps
                    a1 = t1[it % 2]; a3 = t3[it % 2]; N1 = n1[it % 2]; T2 = t2[it % 2]; it += 1
                    Ec = tr(c, hsum, 0); No = tr(co, hsum, 1)
                    kb.op("act", [No.b], [N1.b], lambda a: a.activation(out=N1.t[:, :], in_=No.t[:, :], func=AF.Copy))
                    for half, op_ in ((0, ALU.add), (1, ALU.subtract)):
                        g = og_[n % 8]; n += 1
                        kb.op("dve", [Ec.b, N1.b], [g.b], lambda v: v.tensor_tensor(out=g.t[:, :], in0=Ec.t[:, :], in1=N1.t[:, :], op=op_))
                        kb.dma("act", S["kfr"][(2 * kc + half) * 128:(2 * kc + half + 1) * 128, og * 512:(og + 1) * 512], g.t[:], [g.b], [kb.dbuf("kfr", 2 * kc + half, og)])
                    Es = tr(s_, hdif, 0); T2p = tr(so, hdif, 1)
                    kb.op("act", [T2p.b], [T2.b], lambda a: a.activation(out=T2.t[:, :], in_=T2p.t[:, :], func=AF.Copy))
                    for half, op_ in ((0, ALU.add), (1, ALU.subtract)):
                        g = og_[n % 8]; n += 1
                        kb.op("dve", [Es.b, T2.b], [g.b], lambda v: v.tensor_tensor(out=g.t[:, :], in0=T2.t[:, :], in1=Es.t[:, :], op=op_))
                        kb.dma("act", S["kfi"][(2 * kc + half) * 128:(2 * kc + half + 1) * 128, og * 512:(og + 1) * 512], g.t[:], [g.b], [kb.dbuf("kfi", 2 * kc + half, og)])

    def phase_hyena(self, l):
        kb = self.kb; S = self.S; I = self.I; pvt = self.pv[l]; tw = self.tw
        with kb.phase(flip=True) as ph:
            zv = self.load_fm(ph, S["hyuT"][0:512, :], 4, L, F32, deps=lambda c: [kb.dbuf("hyu", c)])
            cm1 = kb.tile(ph, [128, 8, 512], BF16); sm1 = kb.tile(ph, [128, 8, 512], BF16)
            UV = [self.tiles(ph, 8, [128, 512], BF16) for _ in range(4)]
            xm = self.tiles(ph, 2, [128, 1024], F32); e1 = self.tiles(ph, 2, [128, 512], F32); ost = self.tiles(ph, 2, [128, 1024], BF16)
            ztok = self.tiles(ph, 16, [128, 512], BF16)
            ct = self.tiles(ph, 2, [128, 8, 128], BF16); stt = self.tiles(ph, 2, [128, 8, 128], BF16)
            cot = self.tiles(ph, 2, [128, 8, 128], BF16); sot = self.tiles(ph, 2, [128, 8, 128], BF16)
            kf = [self.tiles(ph, 2, [128, 512], F32) for _ in range(4)]
            F = lambda k: self.tiles(ph, k, [128, 512], F32)
            N1, T2 = F(2)
            ABd = [F(4), F(4)]
            Md = [F(4), F(4)]; TTd = [F(2), F(2)]
            YWb = [self.tiles(ph, 4, [128, 512], BF16), self.tiles(ph, 4, [128, 512], BF16)]
            tt = lambda o, a, b, op: (lambda v: v.tensor_tensor(out=o, in0=a, in1=b, op=op))
            for order in range(2):
                for par in range(2):
                    for mc in range(8):
                        ps = kb.psum()

                        def tr(p):
                            for cc in range(4):
                                ins = p.transpose(ps.t[:, cc * 128:(cc + 1) * 128], zv[cc].t[:, mc * 256 + par:mc * 256 + 256:2], self.identf.t[:, :])
                            return ins
                        kb.op("pe", [z.b for z in zv] + [self.identf.b], [ps.b], tr)
                        zt = ztok[par * 8 + mc]
                        kb.op("act", [ps.b], [zt.b], lambda a: a.activation(out=zt.t[:, :], in_=ps.t[:, :], func=AF.Copy))
                def stageA(kc):
                    c = ct[kc % 2]; s_ = stt[kc % 2]
                    kb.dma("sp", c.t[:], I["CTb"][kc], [], [c.b]); kb.dma("sp", s_.t[:], I["STb"][kc], [], [s_.b])
                    co = cot[kc % 2]; so = sot[kc % 2]
                    kb.dma("sp", co.t[:], I["CoTb"][kc], [], [co.b]); kb.dma("sp", so.t[:], I["SoTb"][kc], [], [so.b])
                    K = [kf[i][kc % 2] for i in range(4)]
                    for i, (nm, half) in enumerate((("kfr", 0), ("kfr", 1), ("kfi", 0), ("kfi", 1))):
                        kb.dma("sp", K[i].t[:], S[nm][(2 * kc + half) * 128:(2 * kc + half + 1) * 128, order * 512:(order + 1) * 512], [kb.dbuf(nm, 2 * kc + half, order)], [K[i].b])
                    cth = tw.t[:, kc:kc + 1]; sth = tw.t[:, 8 + kc:9 + kc]

                    def trf(mat, par):
                        ps = kb.psum()

                        def mm(p):
                            for mc in range(8):
                                ins = p.matmul(ps.t[:, :], lhsT=mat.t[:, mc, :], rhs=ztok[par * 8 + mc].t[:, :], start=(mc == 0), stop=(mc == 7))
                            return ins
                        kb.op("pe", [mat.b] + [x.b for x in ztok[par * 8:par * 8 + 8]], [ps.b], mm)
                        return ps
                    Ec = trf(c, 0); Es = trf(s_, 0); No = trf(co, 1); T2p = trf(so, 1)
                    kb.op("act", [No.b], [N1.b], lambda a: a.activation(out=N1.t[:, :], in_=No.t[:, :], func=AF.Copy))
                    kb.op("act", [T2p.b], [T2.b], lambda a: a.activation(out=T2.t[:, :], in_=T2p.t[:, :], func=AF.Copy))
                    Alo, Ahi, Blo, Bhi = ABd[kc % 2]
                    kb.op("dve", [Ec.b, N1.b], [Alo.b], tt(Alo.t[:, :], Ec.t[:, :], N1.t[:, :], ALU.add))
                    kb.op("dve", [Ec.b, N1.b], [Ahi.b], tt(Ahi.t[:, :], Ec.t[:, :], N1.t[:, :], ALU.subtract))
                    kb.op("dve", [Es.b, T2.b], [Blo.b], tt(Blo.t[:, :], T2.t[:, :], Es.t[:, :], ALU.add))
                    kb.op("dve", [Es.b, T2.b], [Bhi.b], tt(Bhi.t[:, :], T2.t[:, :], Es.t[:, :], ALU.subtract))
                    for half, (A_, B_, Kr, Ki) in enumerate(((Alo, Blo, K[0], K[2]), (Ahi, Bhi, K[1], K[3]))):
                        mA = Md[kc % 2][2 * half]; mB = Md[kc % 2][2 * half + 1]
                        kb.op("dve", [A_.b, Ki.b], [mA.b], tt(mA.t[:, :], A_.t[:, :], Ki.t[:, :], ALU.mult))
                        kb.op("dve", [B_.b, Ki.b], [mB.b], tt(mB.t[:, :], B_.t[:, :], Ki.t[:, :], ALU.mult))
                        kb.op("dve", [A_.b, Kr.b], [A_.b], tt(A_.t[:, :], A_.t[:, :], Kr.t[:, :], ALU.mult))
                        kb.op("dve", [B_.b, Kr.b], [B_.b], tt(B_.t[:, :], B_.t[:, :], Kr.t[:, :], ALU.mult))
                        Yb = YWb[kc % 2][half]; Wb = YWb[kc % 2][2 + half]
                        kb.op("pool", [A_.b, mB.b], [Yb.b], tt(Yb.t[:, :], A_.t[:, :], mB.t[:, :], ALU.add))
                        kb.op("pool", [B_.b, mA.b], [Wb.b], tt(Wb.t[:, :], B_.t[:, :], mA.t[:, :], ALU.subtract))
                def stageB(kc):
                    cth = tw.t[:, kc:kc + 1]; sth = tw.t[:, 8 + kc:9 + kc]
                    Ue, Ve, Uo, Vo = (UV[i][kc] for i in range(4))
                    x1_, x2_ = TTd[kc % 2]
                    Y0, Y1, W0, W1 = YWb[kc % 2]
                    I_ = self.identb; N_ = self.nidentb

                    def comb(x0, x1, s1):
                        ps = kb.psum()

                        def mm(p):
                            p.matmul(ps.t[:, :], lhsT=I_.t[:, :], rhs=x0.t[:, :], start=True, stop=False)
                            return p.matmul(ps.t[:, :], lhsT=s1.t[:, :], rhs=x1.t[:, :], start=False, stop=True)
                        kb.op("pe", [x0.b, x1.b, I_.b, s1.b], [ps.b], mm)
                        return ps
                    pUe = comb(Y0, Y1, I_); pVe = comb(W0, W1, N_); pP = comb(Y0, Y1, N_); pQ = comb(W0, W1, I_)
                    kb.op("act", [pUe.b], [Ue.b], lambda a: a.activation(out=Ue.t[:, :], in_=pUe.t[:, :], func=AF.Copy))
                    kb.op("act", [pVe.b], [Ve.b], lambda a: a.activation(out=Ve.t[:, :], in_=pVe.t[:, :], func=AF.Copy))
                    kb.op("act", [pP.b, tw.b], [x1_.b], lambda a: a.activation(out=x1_.t[:, :], in_=pP.t[:, :], func=AF.Identity, bias=0.0, scale=cth))
                    kb.op("act", [pP.b, tw.b], [x2_.b], lambda a: a.activation(out=x2_.t[:, :], in_=pP.t[:, :], func=AF.Identity, bias=0.0, scale=sth))
                    kb.op("dve", [pQ.b, x1_.b, tw.b], [Uo.b], lambda v: v.scalar_tensor_tensor(out=Uo.t[:, :], in0=pQ.t[:, :], scalar=sth, in1=x1_.t[:, :], op0=ALU.mult, op1=ALU.add))
                    kb.op("dve", [pQ.b, x2_.b, tw.b], [Vo.b], lambda v: v.scalar_tensor_tensor(out=Vo.t[:, :], in0=pQ.t[:, :], scalar=cth, in1=x2_.t[:, :], op0=ALU.mult, op1=ALU.subtract))
                kb.rot = 8
                stageA(0)
                for kc in range(8):
                    if kc + 1 < 8: stageA(kc + 1)
                    stageB(kc)
                kb.rot = 6
                xoff = 512 * (order + 1); it = 0
                for mg in range(2):
                    kb.dma("sp", cm1.t[:], I["Cm"][mg], [], [cm1.b]); kb.dma("sp", sm1.t[:], I["Sm"][mg], [], [sm1.b])
                    for cc in range(4):
                        x_ = xm[it % 2]; o_ = ost[it % 2]; it += 1
                        kb.dma("sp", x_.t[:], S["hyuT"][xoff + cc * 128:xoff + (cc + 1) * 128, mg * 1024:(mg + 1) * 1024], [kb.dbuf("hyu", xoff // 128 + cc)], [x_.b])
                        z = zv[cc]
                        for par in range(2):
                            U = UV[2 * par]; V = UV[2 * par + 1]
                            ps = kb.psum()

                            def mm(p):
                                for kc in range(8):
                                    p.matmul(ps.t[:, :], lhsT=U[kc].t[:, cc * 128:(cc + 1) * 128], rhs=cm1.t[:, kc, :], start=(kc == 0), stop=False)
                                for kc in range(8):
                                    ins = p.matmul(ps.t[:, :], lhsT=V[kc].t[:, cc * 128:(cc + 1) * 128], rhs=sm1.t[:, kc, :], start=False, stop=(kc == 7))
                                return ins
                            kb.op("pe", [cm1.b, sm1.b] + [x.b for x in U] + [x.b for x in V], [ps.b], mm)
                            t_ = e1[par]
                            zsl = z.t[:, mg * 1024 + par:(mg + 1) * 1024:2]; xsl = x_.t[:, par:1024:2]
                            kb.op("dve", [z.b, ps.b, pvt.b], [t_.b], lambda v: v.scalar_tensor_tensor(
                                out=t_.t[:, :], in0=zsl, scalar=pvt.t[:, cHB(order, cc):cHB(order, cc) + 1], in1=ps.t[:, :], op0=ALU.mult, op1=ALU.add))
                            if order == 0:
                                kb.op("dve", [t_.b, x_.b], [z.b], lambda v: v.tensor_tensor(out=zsl, in0=t_.t[:, :], in1=xsl, op=ALU.mult))
                            else:
                                kb.op("dve", [t_.b, x_.b], [o_.b], lambda v: v.tensor_tensor(out=o_.t[:, par:1024:2], in0=t_.t[:, :], in1=xsl, op=ALU.mult))
                        if order == 1:
                            kb.dma("pool", S["yaT"][cc * 128:(cc + 1) * 128, mg * 1024:(mg + 1) * 1024], o_.t[:], [o_.b], [kb.dbuf("ya", cc, 2 * mg), kb.dbuf("ya", cc, 2 * mg + 1)])

    def phase_na(self, l):
        kb = self.kb; S = self.S; I = self.I
        with kb.phase(flip=True) as ph:
            qT = self.tiles(ph, 4, [128, L], BF16)
            kTm = self.tiles(ph, 8, [128, L], BF16)
            tabs = self.tiles(ph, 8, [128, NS * 64], BF16); rm = kb.tile(ph, [128, 20 * 512], BF16); sel = kb.tile(ph, [128, 128], BF16)
            va = self.tiles(ph, 16, [128, 1024], BF16)
            for h in range(8):
                kb.op("dve", [], [kTm[h].b], lambda v: v.memset(kTm[h].t[:, :], 0.0))
                kb.dma("pool", tabs[h].t[:], I["natab"][l][:, h * NS * 64:(h + 1) * NS * 64], [], [tabs[h].b])
            kb.dma("sp", sel.t[:], I["nasel"][:, :], [], [sel.b]); kb.dma("sp", rm.t[:], I["narm"][:, :], [], [rm.b])
            for h in range(8):
                hc = h // 2; bp = (h % 2) * 64
                if h % 2 == 0:
                    kb.dma("sp", qT[hc].t[:], S["qT"][hc * 128:(hc + 1) * 128, :], [kb.dbuf("qk", 12 + hc, tg) for tg in range(4)], [qT[hc].b])
                kb.dma("sp", kTm[h].t[bp:bp + 64, :], S["kT"][hc * 128 + bp:hc * 128 + bp + 64, :], [kb.dbuf("qk", 16 + hc, tg) for tg in range(4)], [kTm[h].b])
                if h == 0:
                    for tc in range(16):
                        kb.dma("sp", va[tc].t[:], S["vaug"][tc], [kb.dbuf("vaug", tc)], [va[tc].b])
            yb = self.tiles(ph, 4, [128, L], BF16)
            pt = self.tiles(ph, 4, [128, 512], BF16)
            rd = self.tiles(ph, 2, [64, 512], F32)
            items = [(h, g, ji, j) for h in range(8) for g in range(4) for ji, j in enumerate(NA_GROUP_CHUNKS[g])]
            P = {}
            gst = ExitStack(); self.bg_gen = self.sc_gen(l, gst)

            def emit_s(i):
                h, g, ji, j = items[i]
                hc = h // 2
                p_ = pt[i % 4]
                ps = kb.psum()
                s0 = 11 - 2 * j + 8 * g; ti = NA_TILE_BASE[g] + ji
                assert 0 <= s0 and s0 + 8 <= NS

                def mm(p):
                    p.matmul(ps.t[:, :], lhsT=kTm[h].t[:, j * 128:(j + 1) * 128], rhs=qT[hc].t[:, g * 512:(g + 1) * 512], start=True, stop=False)
                    p.matmul(ps.t[:, :], lhsT=self.identb.t[:, :], rhs=tabs[h].t[:, s0 * 64:(s0 + 8) * 64], start=False, stop=False)
                    return p.matmul(ps.t[:, :], lhsT=sel.t[:, :], rhs=rm.t[:, ti * 512:(ti + 1) * 512], start=False, stop=True)
                kb.op("pe", [kTm[h].b, qT[hc].b, tabs[h].b, rm.b, sel.b, self.identb.b], [ps.b], mm)
                kb.op("act", [ps.b], [p_.b], lambda a: a.activation(out=p_.t[:, :], in_=ps.t[:, :], func=AF.Exp))
                P[i] = p_
            emit_s(0); emit_s(1)
            m = 0; po = None
            for i, (h, g, ji, j) in enumerate(items):
                hc = h // 2; bp = (h % 2) * 64
                if i + 2 < len(items): emit_s(i + 2)
                if i % 7 == 3: self.bg()
                nch = len(NA_GROUP_CHUNKS[g])
                if ji == 0: po = kb.acc()
                p_ = P.pop(i)
                kb.op("pe", [va[j].b, p_.b], [po.b], lambda p: p.matmul(po.t[:, :], lhsT=va[j].t[:, h * 128:(h + 1) * 128], rhs=p_.t[:, :], start=(ji == 0), stop=(ji == nch - 1)))
                if ji == nch - 1:
                    r_ = rd[m % 2]; m += 1
                    kb.op("act", [po.b], [r_.b], lambda a: a.activation(out=r_.t[:, :], in_=po.t[64:128, :], func=AF.Ln))
                    kb.op("act", [r_.b], [r_.b], lambda a: a.activation(out=r_.t[:, :], in_=r_.t[:, :], func=AF.Exp, scale=-1.0))
                    kb.op("dve", [po.b, r_.b], [yb[hc].b], lambda v: v.tensor_tensor(out=yb[hc].t[bp:bp + 64, g * 512:(g + 1) * 512], in0=po.t[0:64, :], in1=r_.t[:, :], op=ALU.mult))
            while self.bg_gen is not None: self.bg()
            gst.close()
            for c in range(4):
                kb.dma("sp", S["ybT"][c * 128:(c + 1) * 128, :], yb[c].t[:], [yb[c].b], [kb.dbuf("yb", c)])

    def sc_gen(self, l, st):
        kb = self.kb; S = self.S; pvt = self.pv[l]
        kb.side ^= 1
        ub = self.tiles(st, 1, [128, L + 2], F32)[0]; cv = kb.tile(st, [128, L], F32); ot = self.tiles(st, 2, [128, L], BF16)
        bb = kb.tile(st, [128, L], F32); cb = kb.tile(st, [128, L], F32); xb = kb.tile(st, [128, L], F32)
        kb.side ^= 1
        kb.op("dve", [], [ub.b], lambda v: v.memset(ub.t[:, :], 0.0))
        yield
        for cc in range(4):
            for t, off in ((bb, 0), (cb, 512), (xb, 1024)):
                kb.dma("sp", t.t[:], S["scuT"][off + cc * 128:off + (cc + 1) * 128, :], [kb.dbuf("scu", off // 128 + cc, tg) for tg in range(4)], [t.b])
            yield
            kb.op("dve", [cb.b, xb.b], [ub.b], lambda v: v.tensor_tensor(out=ub.t[:, 1:L + 1], in0=cb.t[:, :], in1=xb.t[:, :], op=ALU.mult))
            yield
            kb.op("dve", [ub.b, pvt.b], [cv.b], lambda v: v.tensor_scalar(out=cv.t[:, :], in0=ub.t[:, 1:1 + L], scalar1=pvt.t[:, cSCW(1, cc):cSCW(1, cc) + 1], scalar2=None, op0=ALU.mult))
            yield
            for k in (0, 2):
                kb.op("dve", [ub.b, pvt.b, cv.b], [cv.b], lambda v: v.scalar_tensor_tensor(
                    out=cv.t[:, :], in0=ub.t[:, k:k + L], scalar=pvt.t[:, cSCW(k, cc):cSCW(k, cc) + 1], in1=cv.t[:, :], op0=ALU.mult, op1=ALU.add))
                yield
            o_ = ot[cc % 2]
            kb.op("dve", [cv.b, bb.b], [o_.b], lambda v: v.tensor_tensor(out=o_.t[:, :], in0=cv.t[:, :], in1=bb.t[:, :], op=ALU.mult))
            kb.dma("sp", S["ycT"][cc * 128:(cc + 1) * 128, :], o_.t[:], [o_.b], [kb.dbuf("yc", cc)])
            yield

    def phase_sc(self, l):
        pass

    def phase_merge(self, l):
        kb = self.kb; S = self.S; I = self.I; pvt = self.pv[l]
        with kb.phase(flip=True) as ph0:
          ms = self.tiles(ph0, 8, [128, L], BF16)
          wpre = self.prefetch_w(ph0, I["w_out"][l], D, 0, D, 512, 2)
          with kb.phase() as ph:
            ys = [self.load_fm(ph, S["yaT"], 4, L, BF16, deps=lambda c: [kb.dbuf("ya", c, tg) for tg in range(4)]),
                  self.load_fm(ph, S["ybT"], 4, L, BF16, deps=lambda c: [kb.dbuf("yb", c)]),
                  self.load_fm(ph, S["ycT"], 4, L, BF16, deps=lambda c: [kb.dbuf("yc", c)])]
            wb = self.tiles(ph, 3, [128, 4, D], BF16)
            for b in range(3):
                kb.dma("pool", wb[b].t[:], I["w_branch"][l, b].rearrange("(kc p) n -> p kc n", p=128), [], [wb[b].b])
            gt = self.tiles(ph, 6, [128, 512], BF16); acc = self.tiles(ph, 2, [128, 512], F32); tmp = self.tiles(ph, 4, [128, 512], F32)
            ost = self.tiles(ph, 2, [128, 512], BF16)
            n = 0; it = 0
            for oc in range(8):
                for tg in range(4):
                    a_ = acc[it % 2]; o_ = ost[it % 2]; it += 1
                    for b in range(3):
                        t_ = tmp[(2 * it + b) % 4]
                        g_ = gt[n % 6]; n += 1
                        kb.dma("sp", g_.t[:], S["gatesT"][(b * 8 + oc) * 128:(b * 8 + oc + 1) * 128, tg * 512:(tg + 1) * 512], [kb.dbuf("gates", b * 8 + oc, tg)], [g_.b])
                        ps = kb.psum()

                        def mm(p, b=b, ps=ps):
                            for cc in range(4):
                                ins = p.matmul(ps.t[:, :], lhsT=wb[b].t[:, cc, oc * 128:(oc + 1) * 128], rhs=ys[b][cc].t[:, tg * 512:(tg + 1) * 512], start=(cc == 0), stop=(cc == 3))
                            return ins
                        kb.op("pe", [wb[b].b] + [y.b for y in ys[b]], [ps.b], mm)
                        if b == 0:
                            kb.op("dve", [ps.b, g_.b], [a_.b], lambda v, a_=a_, g_=g_, ps=ps: v.tensor_tensor(out=a_.t[:, :], in0=ps.t[:, :], in1=g_.t[:, :], op=ALU.mult))
                        else:
                            kb.op("dve", [ps.b, g_.b], [t_.b], lambda v, t_=t_, g_=g_, ps=ps: v.tensor_tensor(out=t_.t[:, :], in0=ps.t[:, :], in1=g_.t[:, :], op=ALU.mult))
                            if b == 1:
                                kb.op("pool", [a_.b, t_.b], [a_.b], lambda v, a_=a_, t_=t_: v.tensor_tensor(out=a_.t[:, :], in0=a_.t[:, :], in1=t_.t[:, :], op=ALU.add))
                            else:
                                kb.op("dve", [a_.b, t_.b], [ms[oc].b], lambda v, a_=a_, t_=t_: v.tensor_tensor(out=ms[oc].t[:, tg * 512:(tg + 1) * 512], in0=a_.t[:, :], in1=t_.t[:, :], op=ALU.add))
          self.proj_norm_resid_tg(ms, I["w_out"][l], pvt, cG(1), (pvt, cG(2)), wpre)

    def phase_xattn(self, l):
        kb = self.kb; I = self.I; pvt = self.pv[l]
        with kb.phase(flip=True) as ph:
            oT = self.tiles(ph, 8, [128, L], BF16)
            wpre = (self.tiles(ph, 2, [128, 8, 512], BF16), 2)
            with kb.phase() as pa:
                hs = self.load_h(pa); mn = self.tiles(pa, 8, [128, NM], BF16)
                qT = self.tiles(pa, 8, [128, L], BF16); kmT = self.tiles(pa, 8, [128, NM], BF16); vm = self.tiles(pa, 2, [128, D], BF16)
                pt = self.tiles(pa, 6, [128, 512], BF16); rdn = self.tiles(pa, 3, [128, 512], F32)
                with kb.phase() as phx:
                    ms = self.load_fm(phx, I["memT"], 8, NM, F32)
                    self.norm_apply(phx, ms, mn, NM, pvt, cMN)
                with kb.phase() as p2:
                    self.linear_fm(p2, I["xa_wq"][l], D, 0, D, hs, L, lambda oc, t0, tw, ps: kb.op("act", [ps.b], [qT[oc].b], lambda a: a.activation(out=qT[oc].t[:, t0:t0 + tw], in_=ps.t[:, 0:tw], func=AF.Identity, bias=0.0, scale=1.0 / 16)))
                with kb.phase() as p2:
                    self.linear_fm(p2, I["xa_wkv"][l], D, 0, D, mn, NM, lambda oc, t0, tw, ps: kb.op("act", [ps.b], [kmT[oc].b], lambda a: a.activation(out=kmT[oc].t[:, t0:t0 + tw], in_=ps.t[:, 0:tw], func=AF.Copy)))
                with kb.phase() as p2:
                    self.linear_tm(p2, I["xa_wkv"][l], D, D, D, mn, NM, lambda tc, gi, gw, ps: kb.op("act", [ps.b], [vm[tc].b], lambda a: a.activation(out=vm[tc].t[:, gi * 512:gi * 512 + gw], in_=ps.t[:, 0:gw], func=AF.Copy)))
                items = [(hh, tg) for hh in range(4) for tg in range(4)]
                ST = {}
                for n_ in range(2):
                    kb.dma("pool", wpre[0][n_].t[:], I["xa_wo"][l][:, n_ * 512:(n_ + 1) * 512].rearrange("(kc p) n -> p kc n", p=128), [], [wpre[0][n_].b])

                def stage1(i):
                    hh, tg = items[i]
                    P = []
                    for mc in range(2):
                        ps = kb.psum(); p_ = pt[(2 * i + mc) % 6]

                        def mm(p):
                            for fc in range(2):
                                ins = p.matmul(ps.t[:, :], lhsT=kmT[2 * hh + fc].t[:, mc * 128:(mc + 1) * 128], rhs=qT[2 * hh + fc].t[:, tg * 512:(tg + 1) * 512], start=(fc == 0), stop=(fc == 1))
                            return ins
                        kb.op("pe", [kmT[2 * hh].b, kmT[2 * hh + 1].b, qT[2 * hh].b, qT[2 * hh + 1].b], [ps.b], mm)
                        kb.op("act", [ps.b], [p_.b], lambda a: a.activation(out=p_.t[:, :], in_=ps.t[:, :], func=AF.Exp))
                        P.append(p_)
                    pd = kb.acc()

                    def mmd(p):
                        for mc in range(2):
                            ins = p.matmul(pd.t[:, :], lhsT=self.onesb.t[:, :], rhs=P[mc].t[:, :], start=(mc == 0), stop=(mc == 1))
                        return ins
                    kb.op("pe", [P[0].b, P[1].b, self.onesb.b], [pd.b], mmd)
                    r_ = rdn[i % 3]
                    kb.op("act", [pd.b], [r_.b], lambda a: a.activation(out=r_.t[:, :], in_=pd.t[:, :], func=AF.Ln))
                    kb.op("act", [r_.b], [r_.b], lambda a: a.activation(out=r_.t[:, :], in_=r_.t[:, :], func=AF.Exp, scale=-1.0))
                    ST[i] = (P, r_)

                def stage2(i):
                    hh, tg = items[i]
                    P, r_ = ST.pop(i)
                    for dc in range(2):
                        po = kb.psum()

                        def mmo(p):
                            for mc in range(2):
                                ins = p.matmul(po.t[:, :], lhsT=vm[mc].t[:, hh * 256 + dc * 128:hh * 256 + (dc + 1) * 128], rhs=P[mc].t[:, :], start=(mc == 0), stop=(mc == 1))
                            return ins
                        kb.op("pe", [P[0].b, P[1].b, vm[0].b, vm[1].b], [po.b], mmo)
                        ot = oT[2 * hh + dc]
                        kb.op("dve", [po.b, r_.b], [ot.b], lambda v: v.tensor_tensor(out=ot.t[:, tg * 512:(tg + 1) * 512], in0=po.t[:, :], in1=r_.t[:, :], op=ALU.mult))
                stage1(0)
                for i in range(len(items)):
                    if i + 1 < len(items): stage1(i + 1)
                    stage2(i)
            self.proj_norm_resid_tg(oT, I["xa_wo"][l], pvt, cG(3), (pvt, cG(4)), wpre)

    def phase_ffn(self, l, final):
        kb = self.kb; I = self.I; pvt = self.pv[l]; w = I["ffn_up"][l]
        with kb.phase(flip=True) as ph:
            tT = self.tiles(ph, 22, [128, L], BF16)
            with kb.phase() as pa:
                hs = self.load_h(pa)
                ug = self.padded(pa, 2); uv = self.padded(pa, 2)
                cg = self.tiles(pa, 2, [128, L], F32); cv = self.tiles(pa, 2, [128, L], F32)
                wb = self.tiles(pa, 4, [128, 8, 128], BF16)
                for i in range(22):
                    for part, (ub, ob, col, ci) in enumerate(((ug[i % 2], cg[i % 2], i * 128, i), (uv[i % 2], cv[i % 2], DFF + i * 128, 22 + i))):
                        wt = wb[(2 * i + part) % 4]
                        kb.dma("pool", wt.t[:], w[:, col:col + 128].rearrange("(kc p) n -> p kc n", p=128), [], [wt.b])
                        for tg in range(4):
                            ps = kb.psum()

                            def mm(p):
                                for kc in range(8):
                                    ins = p.matmul(ps.t[:, :], lhsT=wt.t[:, kc, :], rhs=hs[kc].t[:, tg * 512:(tg + 1) * 512], start=(kc == 0), stop=(kc == 7))
                                return ins
                            kb.op("pe", [wt.b] + [h.b for h in hs], [ps.b], mm)
                            kb.op("act", [ps.b], [ub.b], lambda a: a.activation(out=ub.t[:, 1 + tg * 512:1 + (tg + 1) * 512], in_=ps.t[:, :], func=AF.Copy))
                            kb.op("act", [ps.b, pvt.b], [ob.b], lambda a: a.activation(out=ob.t[:, tg * 512:(tg + 1) * 512], in_=ps.t[:, :], func=AF.Identity, bias=0.0, scale=pvt.t[:, cFCW(1, ci):cFCW(1, ci) + 1]))
                    g_ = cg[i % 2]; v_ = cv[i % 2]
                    self.conv3(ug[i % 2], g_, pvt, lambda k: cFCW(k, i), center_done=True)
                    kb.op("act", [g_.b], [g_.b], lambda a: a.activation(out=g_.t[:, :], in_=g_.t[:, :], func=AF.Gelu_apprx_tanh))
                    self.conv3(uv[i % 2], v_, pvt, lambda k: cFCW(k, 22 + i), center_done=True)
                    kb.op("dve", [g_.b, v_.b], [tT[i].b], lambda v: v.tensor_tensor(out=tT[i].t[:, :], in0=g_.t[:, :], in1=v_.t[:, :], op=ALU.mult))
            self.proj_norm_resid(tT, I["ffn_down"][l], DFF, pvt, cG(5), cgw=256, final=final, nxt=(None if final else (self.pv[l + 1], cG(0))), nxb=2)

    def run(self, stop_after=None):
        kb = self.kb
        self.xs = self.nc.dram_tensor("xscr", [D, L], F32, kind=("ExternalOutput" if self.dbg else "Internal")).ap()
        self.xcur = self.I["xT"]
        n = 0
        for l in range(NL):
            for f in (self.phase_proj, self.phase_filters, self.phase_hyena, self.phase_na, self.phase_sc, self.phase_merge, self.phase_xattn):
                if stop_after is not None and n >= stop_after: break
                f(l); n += 1
            if stop_after is not None and n >= stop_after: break
            self.phase_ffn(l, final=(l == NL - 1)); n += 1
        kb.barrier()


_CONST = {}


def _constants():
    if _CONST: return _CONST
    H = L // 2
    th = np.pi * (np.arange(H, dtype=np.float64) + 0.5) / L
    ang = np.outer(2.0 * th, np.arange(H, dtype=np.float64))
    C = np.cos(ang); Sn = np.sin(ang)
    def fwd(M):
        return np.ascontiguousarray(M.reshape(8, 128, 8, 128).transpose(0, 3, 2, 1)).astype(BF)
    def inv(M):
        return np.ascontiguousarray(M.reshape(8, 128, 2, 512).transpose(2, 1, 0, 3)).astype(BF)
    _CONST["CTb"] = fwd(C); _CONST["STb"] = fwd(Sn); _CONST["Cm"] = inv(C); _CONST["Sm"] = inv(Sn)
    ango = np.outer(th, 2.0 * np.arange(H, dtype=np.float64) + 1.0)
    _CONST["CoTb"] = fwd(np.cos(ango)); _CONST["SoTb"] = fwd(np.sin(ango))
    tw = np.zeros((128, 16), np.float32)
    tw[:, 0:8] = np.cos(th).reshape(8, 128).T; tw[:, 8:16] = np.sin(th).reshape(8, 128).T
    _CONST["tw"] = tw
    f32 = np.float32
    t = np.linspace(0.0, 1.0, L, dtype=f32)[:, None]
    bands = 16
    w = (2.0 * np.pi * np.arange(L, dtype=f32)[:, None] / L).astype(f32)
    f = np.linspace(1e-4, bands - 1, bands, dtype=f32)[None, :]
    z = np.concatenate([t, np.cos(f * w), -np.sin(f * w)], axis=-1).astype(f32)
    _CONST["zT"] = np.ascontiguousarray(z.T)
    deltas = np.abs(np.linspace(np.log(1e-2) / 1.5, np.log(1e-2) / 0.3, 512, dtype=f32))
    dec = np.exp(-t * deltas[None, :]).astype(f32)
    decb = dec.copy(); decb[0, :] = 0.0
    eo = lambda a: np.ascontiguousarray(np.concatenate([a[0::2], a[1::2]], axis=0) * f32(1.0 / L))
    _CONST["decay"] = eo(dec)
    _CONST["decayb"] = eo(decb)
    _CONST["identf"] = np.eye(128, dtype=f32); _CONST["identb"] = np.eye(128).astype(BF)
    p = np.arange(128)[:, None, None]; sl = np.arange(NS)[None, :, None]; q = np.arange(64)[None, None, :]
    krl = p // 64; c = p % 64
    dr = 14 - (sl - krl - 4)
    cs = np.clip(q - 8, 0, 48)
    val = (dr >= 0) & (dr <= 14) & (c >= cs) & (c < cs + 16)
    _CONST["na_idx"] = (np.broadcast_to(np.clip(dr, 0, 14), (128, NS, 64)).copy(), np.broadcast_to(np.clip(c - q + 15, 0, 30), (128, NS, 64)).copy(), np.broadcast_to(val, (128, NS, 64)).copy())
    rmk = np.zeros((2, 20, 512), f32)
    for g in (0, 1, 3):
        for ji, j in enumerate(NA_GROUP_CHUNKS[g]):
            ti = NA_TILE_BASE[g] + ji
            for k_ in range(2):
                kr = 2 * j + k_
                r = 8 * g + np.arange(512) // 64
                rs = np.clip(r - 4, 0, 24)
                rmk[k_, ti] = np.where((kr >= rs) & (kr < rs + 8), 0.0, MASKV)
    rmp = np.zeros((128, 20 * 512), f32); rmp[0:2] = rmk.reshape(2, 20 * 512)
    _CONST["narm"] = rmp.astype(BF)
    sel = np.zeros((128, 128), f32); sel[0, :64] = 1; sel[1, 64:] = 1
    _CONST["nasel"] = sel.astype(BF)
    return _CONST


def _fm(v):
    return np.ascontiguousarray(np.asarray(v, np.float32).reshape(-1, 128).T)


def _prep(inputs):
    C = _constants()
    I = {k: np.asarray(v) for k, v in inputs.items()}
    pv = np.zeros((NL, 128, NV), np.float32)
    for l in range(NL):
        for i in range(6): pv[l, :, cG(i):cG(i) + 8] = _fm(I["norm_gains"][l, i])
        pv[l, :, cMN:cMN + 8] = _fm(I["mem_norm"][l])
        for b in range(3): pv[l, :, cGB + 8 * b:cGB + 8 * b + 8] = _fm(I["gate_bias"][l, b])
        for k in range(3):
            pv[l, :, cHSW(k, 0):cHSW(k, 0) + 12] = _fm(I["hy_short_w"][l, k])
            pv[l, :, cSCW(k, 0):cSCW(k, 0) + 4] = _fm(I["sc_conv_w"][l, k])
            pv[l, :, cFCW(k, 0):cFCW(k, 0) + 44] = _fm(I["ffn_conv"][l, k])
        for o in range(2): pv[l, :, cHB(o, 0):cHB(o, 0) + 4] = _fm(I["hy_bias"][l, o])
        pv[l, 0:64, cB1] = I["hy_b1"][l]; pv[l, 0:64, cB2] = I["hy_b2"][l]
        pv[l, 0:64, cF0] = I["hy_freq"][l, 0]; pv[l, 0:64, cF1] = I["hy_freq"][l, 1]
    idr, idc, val = C["na_idx"]
    rpb = I["na_rpb"].astype(np.float32)
    tab = np.where(val[None, None], rpb[:, :, idr, idc], np.float32(MASKV)).astype(np.float32)
    tab = np.ascontiguousarray(tab.transpose(0, 2, 1, 3, 4)).reshape(NL, 128, 8 * NS * 64)
    shared = {"pv": pv, "natab": tab}
    for k in ("w_in", "hy_w1", "hy_w2", "hy_w3", "w_branch", "w_out", "xa_wq", "xa_wkv", "xa_wo", "ffn_up", "ffn_down"):
        shared[k] = np.ascontiguousarray(I[k], dtype=np.float32)
    for k in ("CTb", "STb", "CoTb", "SoTb", "Cm", "Sm", "tw", "zT", "decay", "decayb", "identf", "identb", "narm", "nasel"):
        shared[k] = C[k]
    maps = []
    for b in range(8):
        m = dict(shared)
        m["xT"] = np.ascontiguousarray(I["x"][b].T.astype(np.float32))
        m["memT"] = np.ascontiguousarray(I["mem"][b].T.astype(np.float32))
        maps.append(m)
    return maps


def build(dbg=False, stop_after=None):
    nc = bass.Bass("TRN2", target_bir_lowering=False)
    st = ExitStack()
    kb = KB(nc, st)
    m = Model(nc, kb, st, dbg=dbg)
    m.run(stop_after)
    st.close()
    return nc


def kernel(**inputs):
    maps = _prep(inputs)
    nc = build()
    res = run_bass_kernel_spmd(nc, maps, core_ids=list(range(8)))
    out = np.stack([np.asarray(r["outT"], np.float32).T for r in res.results], axis=0)
    return np.ascontiguousarray(out)
```
